# Optimizing a Trainium2 kernel written in Bass

```python
import jax, jax.numpy as jnp
from jax import lax
import numpy as np

D_MODEL = 1024
BATCH = 8
SEQ = 4096
DEPTH = 1

CTX_LEN = 256
GRID_W = 64
D_SSM = D_MODEL
SSM_HEADDIM = 64
SSM_HEADS = D_SSM // SSM_HEADDIM
SSM_GROUPS = 2
D_STATE = 128
D_CONV = 5
CHUNK = 128
D_POOL = D_MODEL
POOL_WINDOWS = (2, 4, 8, 16)
POOL_GROUP = D_POOL // len(POOL_WINDOWS)
D_MIX = D_SSM + D_POOL
D_XBC = D_SSM + 2 * SSM_GROUPS * D_STATE
D_IN_PROJ = D_SSM + D_XBC + 2 * SSM_HEADS + D_POOL
D_FF = 4 * D_MODEL
EPS = 1e-6
IN_SPLITS = (D_SSM, D_SSM + D_XBC, D_SSM + D_XBC + 2 * SSM_HEADS)

kernel_name = "hybrid_ssd_pool_dit_block"


def rmsnorm(x, g):
    xf = x.astype(jnp.float32)
    return xf * lax.rsqrt(jnp.mean(xf * xf, axis=-1, keepdims=True) + EPS) * g.astype(jnp.float32)


def modulate(h, shift, scale):
    return h * (1.0 + scale) + shift


def dwconv_centred(u, w, b):
    c = u.shape[-1]
    k = w.shape[0]
    out = lax.conv_general_dilated(
        u, w.astype(jnp.float32)[:, None, :], window_strides=(1,),
        padding=((k // 2, k // 2),), dimension_numbers=('NWC', 'WIO', 'NWC'),
        feature_group_count=c)
    return out + b.astype(jnp.float32)


def box_mean(u, w, axis):
    L = u.shape[axis]
    cs = jnp.cumsum(u, axis=axis)
    cs = jnp.concatenate([jnp.zeros_like(lax.slice_in_dim(cs, 0, 1, axis=axis)), cs], axis=axis)
    t = jnp.arange(L)
    lo = jnp.clip(t - w // 2, 0, L)
    hi = jnp.clip(t + w // 2, 0, L)
    s = jnp.take(cs, hi, axis=axis) - jnp.take(cs, lo, axis=axis)
    cnt = (hi - lo).astype(u.dtype).reshape((L,) + (1,) * (u.ndim - axis - 1))
    return s / cnt


def pool_mixer(u, pool_w, pool_scale, grid):
    b, l, _ = u.shape
    ug = u.reshape(b, l, len(POOL_WINDOWS), POOL_GROUP)
    outs = []
    for gi, w in enumerate(POOL_WINDOWS):
        ui = ug[:, :, gi]
        if grid:
            rows = l // GRID_W
            v = ui.reshape(b, rows, GRID_W, POOL_GROUP)
            m = box_mean(box_mean(v, w, 1), w, 2).reshape(b, l, POOL_GROUP)
        else:
            m = box_mean(ui, w, 1)
        outs.append(m - ui)
    d = jnp.stack(outs, axis=2)
    y = jnp.einsum('blgc,gcd->blgd', d, pool_w.astype(jnp.float32)).reshape(b, l, D_POOL)
    return y * pool_scale.astype(jnp.float32)


def ssd_chunked(xs, dt, a, bm, cm, h0):
    b, l, h, p = xs.shape
    g, n = bm.shape[2], bm.shape[3]
    r = h // g
    nc = l // CHUNK
    xdt = (xs * dt[..., None]).reshape(b, nc, CHUNK, g, r, p)
    acs = jnp.cumsum((dt * a).reshape(b, nc, CHUNK, g, r), axis=2)
    bc = bm.reshape(b, nc, CHUNK, g, n)
    cc = cm.reshape(b, nc, CHUNK, g, n)
    lower = jnp.tril(jnp.ones((CHUNK, CHUNK), dtype=bool))[:, :, None, None]
    seg = acs[:, :, :, None] - acs[:, :, None, :]
    decay = jnp.exp(jnp.where(lower, seg, -jnp.inf))
    cb = jnp.einsum('bctgn,bcsgn->bctsg', cc, bc)
    y_diag = jnp.einsum('bctsgr,bcsgrp->bctgrp', cb[..., None] * decay, xdt)
    st = jnp.einsum('bcsgn,bcsgr,bcsgrp->bcgrpn', bc, jnp.exp(acs[:, :, -1:] - acs), xdt)

    def step(s, inp):
        dec, add = inp
        return s * dec[..., None, None] + add, s

    s_final, s_in = lax.scan(step, h0, (jnp.moveaxis(jnp.exp(acs[:, :, -1]), 1, 0),
                                        jnp.moveaxis(st, 1, 0)))
    s_in = jnp.moveaxis(s_in, 0, 1)
    y_off = jnp.einsum('bctgn,bcgrpn,bctgr->bctgrp', cc, s_in, jnp.exp(acs))
    return (y_diag + y_off).reshape(b, l, h, p), s_final


def ssm_project(h, w_in, conv_w, conv_b, dt_bias):
    b, l, _ = h.shape
    proj = h @ w_in.astype(jnp.float32)
    z, xbc, dt_raw, u = jnp.split(proj, IN_SPLITS, axis=-1)
    xbc = jax.nn.silu(dwconv_centred(xbc, conv_w, conv_b))
    xs, bm, cm = jnp.split(xbc, (D_SSM, D_SSM + SSM_GROUPS * D_STATE), axis=-1)
    xs = xs.reshape(b, l, SSM_HEADS, SSM_HEADDIM)
    bm = bm.reshape(b, l, SSM_GROUPS, D_STATE)
    cm = cm.reshape(b, l, SSM_GROUPS, D_STATE)
    dt = jax.nn.softplus(dt_raw.reshape(b, l, 2, SSM_HEADS) + dt_bias.astype(jnp.float32))
    return z, xs, bm, cm, dt, u


def bidir_ssd(xs, dt, a_log, bm, cm, h0_f, h0_b):
    a = -jnp.exp(a_log.astype(jnp.float32))
    rev = lambda t: jnp.flip(t, axis=1)
    y_f, s_f = ssd_chunked(xs, dt[:, :, 0], a[0], bm, cm, h0_f)
    y_b, s_b = ssd_chunked(rev(xs), rev(dt[:, :, 1]), a[1], rev(bm), rev(cm), h0_b)
    return y_f + rev(y_b), s_f, s_b


def mixer_out(y_scan, xs, z, u, d_skip, ssm_norm_w, pool_w, pool_scale, w_out, grid):
    b, l = xs.shape[:2]
    y = y_scan + d_skip.astype(jnp.float32)[:, None] * xs
    y = y.reshape(b, l, SSM_GROUPS, -1) * jax.nn.silu(z).reshape(b, l, SSM_GROUPS, -1)
    y = rmsnorm(y, ssm_norm_w.reshape(SSM_GROUPS, -1)).reshape(b, l, D_SSM)
    pm = pool_mixer(u, pool_w, pool_scale, grid)
    return jnp.concatenate([y, pm], axis=-1) @ w_out.astype(jnp.float32)


def sq_relu_mlp(h, w1, w2):
    return jnp.square(jax.nn.relu(h @ w1.astype(jnp.float32))) @ w2.astype(jnp.float32)


def setup_inputs(seed: int = 0) -> dict:
    key = jax.random.key(seed)
    ks = jax.random.split(key, 24)
    nrm = lambda k, shape, s: jax.random.normal(k, shape, jnp.float32) * s
    dt0 = jnp.exp(jax.random.uniform(ks[13], (DEPTH, 2, SSM_HEADS), jnp.float32,
                                     np.log(1e-3), np.log(1e-1)))
    return {
        "x": nrm(ks[0], (BATCH, SEQ, D_MODEL), 1.0),
        "c": nrm(ks[1], (BATCH, D_MODEL), 1.0),
        "ctx": nrm(ks[2], (BATCH, CTX_LEN, D_MODEL), 1.0),
        "c_ctx": nrm(ks[3], (D_MODEL,), 1.0),
        "w_ada": nrm(ks[4], (DEPTH, D_MODEL, 6 * D_MODEL), 0.5 * D_MODEL ** -0.5),
        "b_ada": nrm(ks[5], (DEPTH, 6 * D_MODEL), 0.02),
        "pre_mix_g": 1.0 + nrm(ks[6], (DEPTH, D_MODEL), 0.02),
        "post_mix_g": 1.0 + nrm(ks[7], (DEPTH, D_MODEL), 0.02),
        "pre_mlp_g": 1.0 + nrm(ks[8], (DEPTH, D_MODEL), 0.02),
        "post_mlp_g": 1.0 + nrm(ks[9], (DEPTH, D_MODEL), 0.02),
        "w_in": nrm(ks[10], (DEPTH, D_MODEL, D_IN_PROJ), D_MODEL ** -0.5),
        "conv_w": nrm(ks[11], (DEPTH, D_CONV, D_XBC), D_CONV ** -0.5),
        "conv_b": nrm(ks[12], (DEPTH, D_XBC), 0.02),
        "dt_bias": dt0 + jnp.log(-jnp.expm1(-dt0)),
        "a_log": jnp.log(jax.random.uniform(ks[14], (DEPTH, 2, SSM_HEADS), jnp.float32, 1.0, 16.0)),
        "d_skip": 1.0 + nrm(ks[15], (DEPTH, SSM_HEADS), 0.1),
        "ssm_norm_w": 1.0 + nrm(ks[16], (DEPTH, D_SSM), 0.02),
        "pool_w": nrm(ks[17], (DEPTH, len(POOL_WINDOWS), POOL_GROUP, POOL_GROUP), POOL_GROUP ** -0.5),
        "pool_scale": 1.0 + nrm(ks[18], (DEPTH, D_POOL), 0.1),
        "w_out": nrm(ks[19], (DEPTH, D_MIX, D_MODEL), D_MIX ** -0.5),
        "w_mlp1": nrm(ks[20], (DEPTH, D_MODEL, D_FF), D_MODEL ** -0.5),
        "w_mlp2": nrm(ks[21], (DEPTH, D_FF, D_MODEL), D_FF ** -0.5),
    }


def reference(x, c, ctx, c_ctx, w_ada, b_ada, pre_mix_g, post_mix_g, pre_mlp_g, post_mlp_g,
              w_in, conv_w, conv_b, dt_bias, a_log, d_skip, ssm_norm_w, pool_w, pool_scale,
              w_out, w_mlp1, w_mlp2):
    b = x.shape[0]
    xs_ = x.astype(jnp.float32)
    cs_ = ctx.astype(jnp.float32)
    h0 = jnp.zeros((b, SSM_GROUPS, SSM_HEADS // SSM_GROUPS, SSM_HEADDIM, D_STATE), jnp.float32)
    for i in range(DEPTH):
        wa = w_ada[i].astype(jnp.float32)
        ba = b_ada[i].astype(jnp.float32)
        ada = (jax.nn.silu(c.astype(jnp.float32)) @ wa + ba)[:, None, :]
        sh1, sc1, g1, sh2, sc2, g2 = jnp.split(ada, 6, axis=-1)
        ada_c = jax.nn.silu(c_ctx.astype(jnp.float32)) @ wa + ba
        csh1, csc1, cg1, csh2, csc2, cg2 = jnp.split(ada_c, 6, axis=-1)
        hc = modulate(rmsnorm(cs_, pre_mix_g[i]), csh1, csc1)
        zc, xc, bc, cc, dtc, uc = ssm_project(hc, w_in[i], conv_w[i], conv_b[i], dt_bias[i])
        yc, s_f, s_b = bidir_ssd(xc, dtc, a_log[i], bc, cc, h0, h0)
        hx = modulate(rmsnorm(xs_, pre_mix_g[i]), sh1, sc1)
        zx, xx, bx, cx, dtx, ux = ssm_project(hx, w_in[i], conv_w[i], conv_b[i], dt_bias[i])
        yx, _, _ = bidir_ssd(xx, dtx, a_log[i], bx, cx, s_f, s_b)
        mix_x = mixer_out(yx, xx, zx, ux, d_skip[i], ssm_norm_w[i], pool_w[i], pool_scale[i],
                          w_out[i], True)
        xs_ = xs_ + g1 * rmsnorm(mix_x, post_mix_g[i])
        hm = modulate(rmsnorm(xs_, pre_mlp_g[i]), sh2, sc2)
        xs_ = xs_ + g2 * rmsnorm(sq_relu_mlp(hm, w_mlp1[i], w_mlp2[i]), post_mlp_g[i])
        if i < DEPTH - 1:
            mix_c = mixer_out(yc, xc, zc, uc, d_skip[i], ssm_norm_w[i], pool_w[i], pool_scale[i],
                              w_out[i], False)
            cs_ = cs_ + cg1 * rmsnorm(mix_c, post_mix_g[i])
            hcm = modulate(rmsnorm(cs_, pre_mlp_g[i]), csh2, csc2)
            cs_ = cs_ + cg2 * rmsnorm(sq_relu_mlp(hcm, w_mlp1[i], w_mlp2[i]), post_mlp_g[i])
    return xs_.astype(x.dtype)
```

```python
import contextlib
import numpy as np
import concourse.bass as bass
import concourse.mybir as mybir
from concourse.bass_utils import run_bass_kernel_spmd

F32 = mybir.dt.float32
BF16 = mybir.dt.bfloat16
AF = mybir.ActivationFunctionType
ALU = mybir.AluOpType

L = 4096
D = 1024
KD = 8
NCH = 32
CTXL = 256
DFF = 4096
EPS = 1e-6
ENGS = ("pe", "act", "dve", "pool", "sp")
N_DMA_SEMS = 40
N_SW_SEMS = 14


class Buf:
    def __init__(self, ap=None, name=""):
        self.ap = ap
        self.name = name
        self.w = {}
        self.r = {}
        self.strict = "sq" in name

    def __getitem__(self, k):
        return self.ap[k]


def I(name, *args, **kw):
    return lambda e: getattr(e, name)(*args, **kw)


class Sched:
    def __init__(self, nc, stack):
        self.nc = nc
        self.q = {e: [] for e in ENGS}
        self.cnt = {e: 0 for e in ENGS}
        self.known = {e: {} for e in ENGS}
        self.dma_val = [0] * N_DMA_SEMS
        self.dma_rr = 0
        self.csem = {e: stack.enter_context(nc.semaphore("c_" + e)) for e in ENGS}
        self.dsem = [stack.enter_context(nc.semaphore("d_%d" % i)) for i in range(N_DMA_SEMS)]
        self.swsem = [stack.enter_context(nc.semaphore("w_%d" % i)) for i in range(N_SW_SEMS)]
        self.sw_used = 0
        self.n_ins = 0

    def _need(self, eng, key, val, waits):
        if key == eng and eng == "pe":
            return
        if self.known[eng].get(key, 0) >= val:
            return
        self.known[eng][key] = val
        waits[key] = max(waits.get(key, 0), val)

    def _deps(self, eng, reads, writes):
        waits = {}
        for t in reads:
            for k, v in t.w.items():
                self._need(eng, k, v, waits)
        for t in writes:
            for k, v in t.w.items():
                if k != eng or eng == "pool" or t.strict:
                    self._need(eng, k, v, waits)
            for k, v in t.r.items():
                if k != eng or eng == "pool" or t.strict:
                    self._need(eng, k, v, waits)
        return waits

    def _commit(self, key, val, reads, writes):
        for t in reads:
            t.r[key] = max(t.r.get(key, 0), val)
        for t in writes:
            t.w[key] = max(t.w.get(key, 0), val)
            t.r = {}

    def op(self, eng, fn, reads=(), writes=(), inc=True):
        waits = self._deps(eng, reads, writes)
        if inc:
            self.cnt[eng] += 1
            self.q[eng].append((waits, fn, ("c", eng)))
            self._commit(eng, self.cnt[eng], reads, writes)
        else:
            assert eng == "pe"
            self.q[eng].append((waits, fn, ("n", eng)))
            self._commit(eng, self.cnt[eng] + 1, reads, writes)

    def dma(self, eng, fn, reads=(), writes=()):
        waits = self._deps(eng, reads, writes)
        i = self.dma_rr
        self.dma_rr = (self.dma_rr + 1) % N_DMA_SEMS
        if self.dma_val[i] > 0:
            self._need(eng, i, self.dma_val[i], waits)
        self.dma_val[i] += 16
        self.q[eng].append((waits, fn, ("d", i)))
        self._commit(i, self.dma_val[i], reads, writes)

    def dma_sw(self, fn, reads=(), writes=()):
        eng = "pool"
        waits = self._deps(eng, reads, writes)
        key = "sw%d" % self.sw_used
        self.sw_used += 1
        assert self.sw_used <= N_SW_SEMS
        self.q[eng].append((waits, fn, ("w", key)))
        self._commit(key, 16, reads, writes)

    def _sem(self, key):
        if isinstance(key, int):
            return self.dsem[key]
        if key.startswith("sw"):
            return self.swsem[int(key[2:])]
        return self.csem[key]

    def barrier(self):
        for eng in ENGS:
            waits = {}
            for e2 in ENGS:
                if e2 != eng and self.cnt[e2] > 0:
                    self._need(eng, e2, self.cnt[e2], waits)
            for i in range(N_DMA_SEMS):
                if self.dma_val[i] > 0:
                    self._need(eng, i, self.dma_val[i], waits)
            for i in range(self.sw_used):
                self._need(eng, "sw%d" % i, 16, waits)
            self.q[eng].append((waits, None, None))

    def emit(self):
        nc = self.nc
        with nc.Block() as block:
            def run(name, e):
                for waits, fn, inc in self.q[name]:
                    for key, val in waits.items():
                        e.wait_ge(self._sem(key), val)
                    if fn is None:
                        continue
                    ins = fn(e)
                    self.n_ins += 1
                    if inc[0] == "n":
                        pass
                    elif inc[0] == "c":
                        ins.then_inc(self.csem[inc[1]], 1)
                    elif inc[0] == "w":
                        ins.then_inc(self._sem(inc[1]), 16)
                    else:
                        ins.then_inc(self.dsem[inc[1]], 16)

            @block.tensor
            def _(e):
                run("pe", e)

            @block.scalar
            def _(e):
                run("act", e)

            @block.vector
            def _(e):
                run("dve", e)

            @block.gpsimd
            def _(e):
                run("pool", e)

            @block.sync
            def _(e):
                run("sp", e)
        self.q = {e: [] for e in ENGS}


def build_program(debug=None):
    nc = bass.Bass("TRN2", target_bir_lowering=False)
    ES = contextlib.ExitStack()
    with ES:
        _build(nc, ES, debug)
    return nc


def _build(nc, ES, debug):
    def din(name, shape):
        return nc.dram_tensor(name, list(shape), F32, kind="ExternalInput").ap()

    x_d = din("x", [L, D]); ctx_d = din("ctx", [CTXL, D])
    c_d = din("c", [1, D]); cctx_d = din("c_ctx", [1, D])
    wada_d = din("w_ada", [D, 6 * D]); bada_d = din("b_ada", [1, 6 * D])
    gpm_d = din("pre_mix_g", [1, D]); gqm_d = din("post_mix_g", [1, D])
    gpl_d = din("pre_mlp_g", [1, D]); gql_d = din("post_mlp_g", [1, D])
    win_d = din("w_in", [D, 3616]); convw_d = din("conv_w", [5, 1536]); convb_d = din("conv_b", [1, 1536])
    dtb_d = din("dt_bias", [1, 32]); alog_d = din("a_log", [1, 32]); dsk_d = din("d_skip", [1, 16])
    snw_d = din("ssm_norm_w", [1, D]); poolw_d = din("pool_w", [4, 256, 256]); pscale_d = din("pool_scale", [1, D])
    wout_d = din("w_out", [2 * D, D]); w1_d = din("w_mlp1", [D, DFF]); w2_d = din("w_mlp2", [DFF, D])
    kconst_d = din("kconst", [128, 5, 128])
    kinv_d = din("kinv", [1, 4 * 64])
    kpool_d = din("kpool", [128, NPM, 128])
    out_d = nc.dram_tensor("out", [L, D], F32, kind="ExternalOutput").ap()
    ysc_d = nc.dram_tensor("y_scratch", [L, D], F32, kind="Internal").ap()
    dbg_d = {}
    if debug:
        for name, shape in debug.get("shapes", {}).items():
            dbg_d[name] = nc.dram_tensor("dbg_" + name, list(shape), F32, kind="ExternalOutput").ap()

    s = Sched(nc, ES)

    def sb(stack, name, shape, dt=F32):
        return Buf(stack.enter_context(nc.sbuf_tensor(name, list(shape), dt)), name)

    def ps(stack, name, shape, dt=F32):
        return Buf(stack.enter_context(nc.psum_tensor(name, list(shape), dt)), name)

    def ring(stack, name, n, shape, dt=F32, psum=False):
        mk = ps if psum else sb
        return [mk(stack, "%s%d" % (name, i), shape, dt) for i in range(n)]

    class RR:
        def __init__(self, bufs):
            self.bufs = bufs; self.i = 0

        def next(self):
            b = self.bufs[self.i % len(self.bufs)]; self.i += 1
            return b

    def dump(name, src_ap, dst_slice, reads):
        if debug and name in dbg_d:
            s.dma("sp", I("dma_start", out=dst_slice(dbg_d[name]), in_=src_ap), reads=reads, writes=[Buf(None, "dbg")])

    G = ES
    kc = sb(G, "kc", [128, 5, 128])
    kcb = sb(G, "kcb", [128, 5, 128], BF16)
    ident_b = kcb[:, 0, :]
    trif, trib, ones = kc[:, 1, :], kc[:, 2, :], kc[:, 3, :]
    colA1 = sb(G, "colA1", [128, 8]); colS1 = sb(G, "colS1", [128, 8])
    colcA1 = sb(G, "colcA1", [128, 8]); colcS1 = sb(G, "colcS1", [128, 8])
    colA2 = sb(G, "colA2", [128, 8]); colS2 = sb(G, "colS2", [128, 8])
    G1 = sb(G, "G1", [128, D]); G2 = sb(G, "G2", [128, D])
    convw = sb(G, "convw", [128, 12, 5]); convb = sb(G, "convb", [128, 12])
    dtb_row = sb(G, "dtb_row", [128, 32]); a_row = sb(G, "a_row", [128, 32]); dsk_row = sb(G, "dsk_row", [128, 16])
    halfneg = sb(G, "halfneg", [128, 1])
    dram_x = Buf(x_d, "x"); dram_out = Buf(out_d, "out")
    ysc = [Buf(ysc_d, "ysc%d" % i) for i in range(NCH)]
    outb = [Buf(out_d, "out%d" % i) for i in range(NCH)]

    with contextlib.ExitStack() as P0:
        s.dma("sp", I("dma_start", out=kc[:], in_=kconst_d[:, :, :]), writes=[kc])
        s.op("dve", I("tensor_copy", out=kcb[:], in_=kc[:]), reads=[kc], writes=[kcb])
        s.op("dve", I("memset", halfneg[:], -0.5), writes=[halfneg])

        def col_load(dst, src_row, n):
            s.dma("sp", I("dma_start", out=dst[:], in_=src_row.rearrange("o (k p) -> p (o k)", p=128),
                                              allow_slow_non_contiguous=True), writes=[dst])

        gpm_c = sb(P0, "gpm_c", [128, 8]); gpl_c = sb(P0, "gpl_c", [128, 8])
        col_load(gpm_c, gpm_d, 8); col_load(gpl_c, gpl_d, 8)
        col_load(convb, convb_d, 12)
        for k in range(5):
            s.dma("sp", I("dma_start", out=convw[:, :, k], in_=convw_d[k:k + 1, :].rearrange("o (j p) -> p (o j)", p=128),
                                                   allow_slow_non_contiguous=True), writes=[convw])
        s.dma("sp", I("dma_start", out=dtb_row[:], in_=dtb_d.partition_broadcast(128) if len(dtb_d.shape) == 1 else dtb_d[0, :].partition_broadcast(128)), writes=[dtb_row])
        alog_row = sb(P0, "alog_row", [128, 32])
        s.dma("sp", I("dma_start", out=alog_row[:], in_=alog_d[0, :].partition_broadcast(128)), writes=[alog_row])
        s.dma("sp", I("dma_start", out=dsk_row[:], in_=dsk_d[0, :].partition_broadcast(128)), writes=[dsk_row])
        s.op("act", I("activation", out=a_row[:], in_=alog_row[:], func=AF.Exp), reads=[alog_row], writes=[a_row])
        s.op("dve", I("tensor_scalar", out=a_row[:], in0=a_row[:], scalar1=-1.0, scalar2=None, op0=ALU.mult), reads=[a_row], writes=[a_row])
        gq_row = sb(P0, "gq_row", [128, D]); gl_row = sb(P0, "gl_row", [128, D])
        s.dma("sp", I("dma_start", out=gq_row[:], in_=gqm_d[0, :].partition_broadcast(128)), writes=[gq_row])
        s.dma("sp", I("dma_start", out=gl_row[:], in_=gql_d[0, :].partition_broadcast(128)), writes=[gl_row])

        craw = sb(P0, "craw", [128, 8, 2]); sc = sb(P0, "sc", [128, 8, 2]); scb = sb(P0, "scb", [128, 8, 128])
        s.dma("sp", I("dma_start", out=craw[:, :, 0], in_=c_d.rearrange("o (k p) -> p (o k)", p=128), allow_slow_non_contiguous=True), writes=[craw])
        s.dma("sp", I("dma_start", out=craw[:, :, 1], in_=cctx_d.rearrange("o (k p) -> p (o k)", p=128), allow_slow_non_contiguous=True), writes=[craw])
        s.op("act", I("activation", out=sc[:], in_=craw[:], func=AF.Silu), reads=[craw], writes=[sc])
        s.op("dve", I("tensor_copy", out=scb[:], in_=sc[:, :, 0:1].to_broadcast([128, 8, 128])), reads=[sc], writes=[scb])
        bada = sb(P0, "bada", [1, 6 * D])
        s.dma("sp", I("dma_start", out=bada[:], in_=bada_d[:, :]), writes=[bada])
        adacol = sb(P0, "adacol", [128, 48, 2])
        growraw = sb(P0, "growraw", [128, 2, D])
        wring = RR(ring(P0, "wada", 4, [128, 8, 512]))
        pcol = RR(ring(P0, "pcol", 2, [128, 2], psum=True))
        prow = RR(ring(P0, "prow", 2, [128, 512], psum=True))
        col_tiles = {0: 0, 1: 4, 2: 8, 3: 12, 6: 24, 7: 28, 8: 32, 9: 36}
        row_tiles = {4: (0, 0), 5: (0, 512), 10: (1, 0), 11: (1, 512)}
        for ct in range(12):
            W = wring.next()
            s.dma("sp", I("dma_start", out=W[:], in_=wada_d[:, ct * 512:(ct + 1) * 512].rearrange("(k p) n -> p k n", p=128)), writes=[W])
            if ct in col_tiles:
                for jj in range(4):
                    pc = pcol.next()
                    for k in range(8):
                        s.op("pe", I("matmul", pc[:, :], lhsT=W[:, k, jj * 128:(jj + 1) * 128], rhs=sc[:, k, :], start=(k == 0), stop=False), inc=False,
                             reads=[W, sc], writes=[pc])
                    c0 = ct * 512 + jj * 128
                    s.op("pe", I("matmul", pc[:, :], lhsT=bada[0:1, c0:c0 + 128], rhs=kc[0:1, 3, 0:2], start=False, stop=True), inc=True,
                         reads=[bada, kc], writes=[pc])
                    bi = col_tiles[ct] + jj
                    s.op("act", I("activation", out=adacol[:, bi, :], in_=pc[:, :], func=AF.Identity), reads=[pc], writes=[adacol])
            else:
                pr = prow.next()
                for k in range(8):
                    s.op("pe", I("matmul", pr[:, :], lhsT=scb[:, k, :], rhs=W[:, k, :], start=(k == 0), stop=False), inc=False,
                         reads=[W, scb], writes=[pr])
                s.op("pe", I("matmul", pr[:, :], lhsT=kc[0:1, 3, :], rhs=bada[0:1, ct * 512:(ct + 1) * 512], start=False, stop=True), inc=True,
                     reads=[bada, kc], writes=[pr])
                gi, off = row_tiles[ct]
                s.op("act", I("activation", out=growraw[:, gi, off:off + 512], in_=pr[:, :], func=AF.Identity), reads=[pr], writes=[growraw])
        s.op("dve", I("scalar_tensor_tensor", out=colA1[:], in0=adacol[:, 8:16, 0], scalar=1.0, in1=gpm_c[:], op0=ALU.add, op1=ALU.mult), reads=[adacol, gpm_c], writes=[colA1])
        s.op("dve", I("scalar_tensor_tensor", out=colcA1[:], in0=adacol[:, 8:16, 1], scalar=1.0, in1=gpm_c[:], op0=ALU.add, op1=ALU.mult), reads=[adacol, gpm_c], writes=[colcA1])
        s.op("dve", I("scalar_tensor_tensor", out=colA2[:], in0=adacol[:, 32:40, 0], scalar=1.0, in1=gpl_c[:], op0=ALU.add, op1=ALU.mult), reads=[adacol, gpl_c], writes=[colA2])
        s.op("dve", I("tensor_copy", out=colS1[:], in_=adacol[:, 0:8, 0]), reads=[adacol], writes=[colS1])
        s.op("dve", I("tensor_copy", out=colcS1[:], in_=adacol[:, 0:8, 1]), reads=[adacol], writes=[colcS1])
        s.op("dve", I("tensor_copy", out=colS2[:], in_=adacol[:, 24:32, 0]), reads=[adacol], writes=[colS2])
        s.op("dve", I("tensor_tensor", out=G1[:], in0=growraw[:, 0, :], in1=gq_row[:], op=ALU.mult), reads=[growraw, gq_row], writes=[G1])
        s.op("dve", I("tensor_tensor", out=G2[:], in0=growraw[:, 1, :], in1=gl_row[:], op=ALU.mult), reads=[growraw, gl_row], writes=[G2])
        if debug and "ada" in dbg_d:
            dump("ada", colA1[:], lambda d: d[:, 0:8], [colA1]); dump("ada", colS1[:], lambda d: d[:, 8:16], [colS1])
            dump("ada", colcA1[:], lambda d: d[:, 16:24], [colcA1]); dump("ada", colA2[:], lambda d: d[:, 24:32], [colA2])
            dump("ada", colS2[:], lambda d: d[:, 32:40], [colS2])
            dump("ada", G1[:, 0:64], lambda d: d[:, 40:104], [G1]); dump("ada", G2[:, 0:64], lambda d: d[:, 104:168], [G2])
        s.barrier()
        s.emit()
    if debug and debug.get("stop") == "p0":
        return

    def prenorm_load(W, src_rows_ap, src_buf):
        xt = W["xt"].next()
        s.dma("sp", I("dma_start", out=xt[:], in_=src_rows_ap), reads=[src_buf], writes=[xt])
        return xt

    def prenorm_front(W, src_rows_ap, src_buf, xt=None):
        if xt is None:
            xt = prenorm_load(W, src_rows_ap, src_buf)
        sq = W["sq"].next(); ss = W["ss"].next(); xn = W["xn"].next()
        s.op("act", I("activation", out=sq[:], in_=xt[:], func=AF.Square, accum_out=ss[:, 0:1]), reads=[xt], writes=[sq, ss])
        s.op("dve", I("tensor_scalar", out=ss[:, 1:2], in0=ss[:, 0:1], scalar1=1.0 / D, scalar2=EPS, op0=ALU.mult, op1=ALU.add), reads=[ss], writes=[ss])
        s.op("pool", I("tensor_tensor", out=ss[:, 2:3], in0=ss[:, 1:2], in1=halfneg[:], op=ALU.pow), reads=[ss, halfneg], writes=[ss])
        s.op("dve", I("tensor_scalar", out=xn[:], in0=xt[:], scalar1=ss[:, 2:3], scalar2=None, op0=ALU.mult), reads=[xt, ss], writes=[xn])
        return xt, xn

    def prenorm_back(W, xn, colA, colS, dst_fn, dst_buf):
        pt = W["pt"].next()
        for k in range(8):
            s.op("pe", I("transpose", out=pt[:, k * 128:(k + 1) * 128], in_=xn[:, k * 128:(k + 1) * 128], identity=ident_b), inc=(k == 7), reads=[xn, kcb], writes=[pt])
        for k in range(8):
            if k % 2 == 0:
                s.op("act", I("activation", out=dst_fn(k), in_=pt[:, k * 128:(k + 1) * 128], func=AF.Identity, scale=colA[:, k:k + 1], bias=colS[:, k:k + 1]),
                     reads=[pt, colA, colS], writes=[dst_buf])
            else:
                s.op("dve", I("tensor_scalar", out=dst_fn(k), in0=pt[:, k * 128:(k + 1) * 128], scalar1=colA[:, k:k + 1], scalar2=colS[:, k:k + 1], op0=ALU.mult, op1=ALU.add),
                     reads=[pt, colA, colS], writes=[dst_buf])

    def prenorm_T(W, src_rows_ap, src_buf, colA, colS, dst_fn, dst_buf):
        xt, xn = prenorm_front(W, src_rows_ap, src_buf)
        prenorm_back(W, xn, colA, colS, dst_fn, dst_buf)
        return xt

    def prenorm_work(stack, tag):
        return {
            "xt": RR(ring(stack, tag + "xt", 4, [128, D])),
            "sq": RR(ring(stack, tag + "sq", 1, [128, D], BF16)),
            "ss": RR(ring(stack, tag + "ss", 4, [128, 4])),
            "xn": RR(ring(stack, tag + "xn", 4, [128, D], BF16)),
            "pt": RR(ring(stack, tag + "pt", 2, [128, D], BF16, psum=True)),
        }

    def wload_cast(dst, dst_ap, src_ap):
        s.dma_sw(I("dma_start", out=dst_ap, in_=src_ap), writes=[dst])

    with contextlib.ExitStack() as PS:
        xsT = sb(PS, "xsT", [128, 8, L + 2], BF16)
        BT = sb(PS, "BT", [128, 2, L + 2], BF16)
        CT = sb(PS, "CT", [128, 2, L + 2], BF16)
        dts = sb(PS, "dts", [128, NCH, 32])
        cxsT = sb(PS, "cxsT", [128, 8, CTXL + 2], BF16)
        cBT = sb(PS, "cBT", [128, 2, CTXL + 2], BF16)
        cCT = sb(PS, "cCT", [128, 2, CTXL + 2], BF16)
        cdts = sb(PS, "cdts", [128, 2, 32])

        with contextlib.ExitStack() as P2:
            wx = sb(P2, "wx", [128, 8, 1536], BF16)
            wdt = sb(P2, "wdt", [128, 8, 32], BF16)
            wload_cast(wx, wx[:], win_d[:, 1024:2560].rearrange("(k p) n -> p k n", p=128))
            wload_cast(wdt, wdt[:], win_d[:, 2560:2592].rearrange("(k p) n -> p k n", p=128))
            PW = prenorm_work(P2, "p2")
            hx_ring = RR(ring(P2, "hxT", 2, [128, 8, 512], BF16))
            pmm = RR(ring(P2, "pmm", 3, [128, 512], psum=True))
            pdt = RR(ring(P2, "pdt", 1, [128, 32], psum=True))
            Pr = RR(ring(P2, "Ppre", 3, [128, 516], BF16))
            pcv = RR(ring(P2, "pcv", 2, [128, 512], psum=True))
            carry = sb(P2, "carry", [128, 12, 4], BF16)
            dgw = sb(P2, "dgw", [128, 60, 128], BF16)
            s.op("dve", I("tensor_tensor", out=dgw[:], in0=kcb[:, 0:1, :].to_broadcast([128, 60, 128]),
                          in1=convw[:].rearrange("p j k -> p (j k)").unsqueeze(2).to_broadcast([128, 60, 128]), op=ALU.mult), reads=[kcb, convw], writes=[dgw])
            dtt = RR(ring(P2, "dtt", 2, [128, 32]))

            def conv_chunk(j, P, ncol, dst_ap, dst_buf):
                pc = pcv.next()
                for k in range(5):
                    s.op("pe", I("matmul", pc[:, 0:ncol], lhsT=dgw[:, j * 5 + k, :], rhs=P[:, k:k + ncol], start=(k == 0), stop=(k == 4)), inc=(k == 4), reads=[dgw, P], writes=[pc])
                s.op("act", I("activation", out=dst_ap, in_=pc[:, 0:ncol], func=AF.Silu, bias=convb[:, j:j + 1]), reads=[pc, convb], writes=[dst_buf])

            import os
            SUBCUT = int(os.environ.get("P2SUB", "9"))

            def inproj_seq(src_d, src_buf, ntok, colA, colS, xs_dst, B_dst, C_dst, dt_dst, nj):
                s.op("pool", I("memset", carry[:], 0.0), writes=[carry])
                tile_n = min(512, ntok)
                ntile = ntok // tile_n
                nsub = tile_n // 128

                def dest(j, c0, n):
                    if j < 8:
                        return xs_dst[:, j, c0:c0 + n], xs_dst
                    if j < 10:
                        return B_dst[:, j - 8, c0:c0 + n], B_dst
                    return C_dst[:, j - 10, c0:c0 + n], C_dst

                def tile_loads(i):
                    return [prenorm_load(PW, src_d[i * tile_n + sub * 128:i * tile_n + (sub + 1) * 128, :], src_buf) for sub in range(nsub)]

                nxt = tile_loads(0)
                for i in range(ntile):
                    hx = hx_ring.next()
                    xts = nxt
                    fr = [prenorm_front(PW, None, None, xt=xts[sub]) for sub in range(nsub)]
                    if i + 1 < ntile:
                        nxt = tile_loads(i + 1)
                    for sub in range(nsub):
                        prenorm_back(PW, fr[sub][1], colA, colS, lambda k, sub=sub, hx=hx: hx[:, k, sub * 128:(sub + 1) * 128], hx)
                    for sub in range(nsub if SUBCUT >= 2 else 0):
                        cidx = i * nsub + sub
                        pd = pdt.next(); t1 = dtt.next()
                        for k in range(8):
                            s.op("pe", I("matmul", pd[:, :], lhsT=hx[:, k, sub * 128:(sub + 1) * 128], rhs=wdt[:, k, :], start=(k == 0), stop=(k == 7)), inc=(k == 7),
                                 reads=[hx, wdt], writes=[pd])
                        s.op("dve", I("tensor_tensor", out=t1[:], in0=pd[:, :], in1=dtb_row[:], op=ALU.add), reads=[pd, dtb_row], writes=[t1])
                        s.op("act", I("activation", out=t1[:], in_=t1[:], func=AF.Exp), reads=[t1], writes=[t1])
                        s.op("act", I("activation", out=dt_dst[:, cidx, :], in_=t1[:], func=AF.Ln, bias=1.0), reads=[t1], writes=[dt_dst])
                    pend = None
                    for j in range(nj if SUBCUT >= 3 else 0):
                        pm = pmm.next(); P = Pr.next()
                        for k in range(8):
                            s.op("pe", I("matmul", pm[:, 0:tile_n], lhsT=wx[:, k, j * 128:(j + 1) * 128], rhs=hx[:, k, 0:tile_n], start=(k == 0), stop=(k == 7)), inc=(k == 7),
                                 reads=[hx, wx], writes=[pm])
                        s.op("pool", I("tensor_copy", out=P[:, 0:4], in_=carry[:, j, :]), reads=[carry], writes=[P])
                        s.op("act", I("activation", out=P[:, 4:4 + tile_n], in_=pm[:, 0:tile_n], func=AF.Identity), reads=[pm], writes=[P])
                        s.op("pool", I("tensor_copy", out=carry[:, j, :], in_=P[:, tile_n:tile_n + 4]), reads=[P], writes=[carry])
                        if pend is not None:
                            conv_chunk(*pend)
                        dap, dbuf = dest(j, i * tile_n, tile_n)
                        pend = (j, P, tile_n, dap, dbuf)
                    if pend is not None:
                        conv_chunk(*pend)
                for j in range(nj if SUBCUT >= 5 else 0):
                    P = Pr.next()
                    s.op("pool", I("tensor_copy", out=P[:, 0:4], in_=carry[:, j, :]), reads=[carry], writes=[P])
                    s.op("pool", I("memset", P[:, 4:8], 0.0), writes=[P])
                    dap, dbuf = dest(j, ntok, 2)
                    conv_chunk(j, P, 2, dap, dbuf)

            ctx_buf = Buf(ctx_d, "ctx")
            import os
            cut = int(os.environ.get("P2CUT", "9"))
            if cut >= 1:
                inproj_seq(ctx_d, ctx_buf, CTXL, colcA1, colcS1, cxsT, cBT, cCT, cdts, 12)
            if cut >= 2:
                inproj_seq(x_d, dram_x, L, colA1, colS1, xsT, BT, CT, dts, 12)
            if debug and "xsT" in dbg_d:
                tmpf = sb(P2, "dbgtmp", [128, 512])
                for (nm, srcb, n3) in (("xsT", xsT, 8), ("BT", BT, 2), ("CT", CT, 2)):
                    for j in range(n3):
                        for q in range(1):
                            s.op("dve", I("tensor_copy", out=tmpf[:], in_=srcb[:, j, 2:514]), reads=[srcb], writes=[tmpf])
                            dump(nm, tmpf[:], lambda d, j=j: d[:, j, :], [tmpf])
                dump("dts", dts[:], lambda d: d[:, :, :], [dts])
                dump("cdts", cdts[:], lambda d: d[:, :, :], [cdts])
            s.barrier()
            s.emit()
        if debug and debug.get("stop") == "p2":
            return

        with contextlib.ExitStack() as P3:
            Sst = [sb(P3, "S_f", [128, D]), sb(P3, "S_b", [128, D])]
            Sbf = [sb(P3, "Sbf_f", [128, D], BF16), sb(P3, "Sbf_b", [128, D], BF16)]
            tri_b16 = {0: kcb[:, 1, :], 1: kcb[:, 2, :]}
            tri_f32 = {0: trif, 1: trib}
            Umat = sb(P3, "Umat", [128, 2, 128])
            Dg = sb(P3, "Dg", [128, 16, 128], BF16)
            s.op("dve", I("tensor_tensor", out=Dg[:], in0=kcb[:, 0:1, :].to_broadcast([128, 16, 128]), in1=dsk_row[:].unsqueeze(2).to_broadcast([128, 16, 128]), op=ALU.mult),
                 reads=[kcb, dsk_row], writes=[Dg])
            s.op("dve", I("tensor_scalar", out=Umat[:], in0=kc[:, 1:3, :], scalar1=-1.0, scalar2=1.0, op0=ALU.mult, op1=ALU.add), reads=[kc], writes=[Umat])
            R2 = lambda name, shape, dt=F32, n=2: RR(ring(P3, name, n, shape, dt))
            dtA_r = R2("dtA", [128, 16])
            sm_r = R2("smalls", [128, 5, 16])
            rhs_r = R2("rhs32", [128, 16, 128], F32, 1)
            L_r = R2("Lmat", [128, 16, 128], BF16, 1); M_r = R2("Mmat", [128, 16, 128], BF16)
            cbm_r = R2("cbm", [128, 2, 128], BF16)
            xs_r = R2("xs_sb", [128, 16, 64], BF16); xdt_r = R2("xdt", [128, 16, 64], BF16); xdw_r = R2("xdw", [128, 16, 64], BF16)
            Btok = sb(P3, "Btok", [128, NCH, 256], BF16)
            cBtok = sb(P3, "cBtok", [128, 2, 256], BF16)
            yt_r = R2("ytmp", [128, D], F32, 1); yo_r = R2("yout", [128, D]); yb_r = R2("ybld", [128, D], F32, 1)
            p_small = ps(P3, "p_small", [128, 512])
            p_T = ps(P3, "p_T", [128, 1024], BF16)
            p_seg = RR(ring(P3, "p_seg", 2, [128, 512], psum=True))
            p_yo = [ps(P3, "p_yo0", [128, 512]), ps(P3, "p_yo1", [128, 512])]
            p_st = [ps(P3, "p_st0", [128, 512]), ps(P3, "p_st1", [128, 512])]

            def make_btok(Bsrc, dst, nchunk):
                for c in range(nchunk):
                    c0 = 2 + c * 128
                    for g in range(2):
                        s.op("pe", I("transpose", out=p_T[:, g * 128:(g + 1) * 128], in_=Bsrc[:, g, c0:c0 + 128], identity=ident_b), reads=[Bsrc, kcb], writes=[p_T])
                    s.op("act", I("activation", out=dst[:, c, :], in_=p_T[:, 0:256], func=AF.Identity), reads=[p_T], writes=[dst])

            make_btok(cBT, cBtok, 2)
            make_btok(BT, Btok, NCH)

            def stageA(d, xsrc, Bsrc, Csrc, dtsrc, Btsrc, c, need_y):
                c0 = 2 + c * 128
                ctxd = {}
                dtA = dtA_r.next(); sm = sm_r.next()
                dtd = dtsrc[:, c, d * 16:(d + 1) * 16]
                s.op("dve", I("tensor_tensor", out=dtA[:], in0=dtd, in1=a_row[:, d * 16:(d + 1) * 16], op=ALU.mult), reads=[dtsrc, a_row], writes=[dtA])
                s.op("pe", I("matmul", p_small[:, 0:16], lhsT=tri_f32[d], rhs=dtA[:], start=True, stop=True), inc=True, reads=[kc, dtA], writes=[p_small])
                s.op("pe", I("matmul", p_small[:, 16:32], lhsT=ones, rhs=dtA[:], start=True, stop=True), inc=True, reads=[kc, dtA], writes=[p_small])
                s.op("act", I("activation", out=sm[:, 0, :], in_=p_small[:, 0:16], func=AF.Identity), reads=[p_small], writes=[sm])
                s.op("act", I("activation", out=sm[:, 2, :], in_=p_small[:, 0:16], func=AF.Exp), reads=[p_small], writes=[sm])
                s.op("act", I("activation", out=sm[:, 4, :], in_=p_small[:, 16:32], func=AF.Exp), reads=[p_small], writes=[sm])
                s.op("dve", I("tensor_tensor", out=sm[:, 1, :], in0=p_small[:, 16:32], in1=sm[:, 0, :], op=ALU.subtract), reads=[p_small, sm], writes=[sm])
                s.op("act", I("activation", out=sm[:, 3, :], in_=sm[:, 1, :], func=AF.Exp), reads=[sm], writes=[sm])
                s.op("dve", I("tensor_tensor", out=sm[:, 3, :], in0=sm[:, 3, :], in1=dtd, op=ALU.mult), reads=[sm, dtsrc], writes=[sm])
                for k in range(8):
                    s.op("pe", I("transpose", out=p_T[:, k * 128:(k + 1) * 128], in_=xsrc[:, k, c0:c0 + 128], identity=ident_b), inc=(k == 7), reads=[xsrc, kcb], writes=[p_T])
                xs_sb = xs_r.next(); xdw = xdw_r.next()
                s.op("act", I("activation", out=xs_sb[:].rearrange("p h q -> p (h q)"), in_=p_T[:, :], func=AF.Identity), reads=[p_T], writes=[xs_sb])
                s.op("pool", I("tensor_tensor", out=xdw[:], in0=xs_sb[:], in1=sm[:, 3, :].unsqueeze(2).to_broadcast([128, 16, 64]), op=ALU.mult), reads=[xs_sb, sm], writes=[xdw])
                ctxd.update(sm=sm, xs_sb=xs_sb, xdw=xdw, Btsrc=Btsrc, c=c, c0=c0)
                if need_y:
                    xdt = xdt_r.next(); cbm = cbm_r.next(); rhs = rhs_r.next(); Lm = L_r.next(); Mm = M_r.next()
                    if not hasattr(rhs, "halves"):
                        rhs.halves = [Buf(rhs.ap, rhs.name + "_lo"), Buf(rhs.ap, rhs.name + "_hi")]
                    s.op("dve", I("tensor_tensor", out=xdt[:], in0=xs_sb[:], in1=dtd.unsqueeze(2).to_broadcast([128, 16, 64]), op=ALU.mult), reads=[xs_sb, dtsrc], writes=[xdt])
                    s.op("dve", I("tensor_tensor", out=rhs[:, 0:8, :], in0=tri_f32[d].unsqueeze(1).to_broadcast([128, 8, 128]), in1=dtA[:, 0:8].unsqueeze(2).to_broadcast([128, 8, 128]), op=ALU.mult),
                         reads=[kc, dtA], writes=[rhs.halves[0]])
                    for h in range(8, 16):
                        s.op("act", I("activation", out=rhs[:, h, :], in_=tri_f32[d], func=AF.Identity, scale=dtA[:, h:h + 1]), reads=[kc, dtA], writes=[rhs.halves[1]])
                    for g in range(2):
                        s.op("pe", I("matmul", p_small[:, 256 + g * 128:256 + (g + 1) * 128], lhsT=Bsrc[:, g, c0:c0 + 128], rhs=Csrc[:, g, c0:c0 + 128], start=True, stop=True), inc=True,
                             reads=[Bsrc, Csrc], writes=[p_small])
                    s.op("dve", I("tensor_tensor", out=cbm[:], in0=p_small[:, 256:512].rearrange("p (g t) -> p g t", g=2),
                                  in1=tri_f32[d].unsqueeze(1).to_broadcast([128, 2, 128]), op=ALU.mult), reads=[p_small, kc], writes=[cbm])
                    ctxd.update(rhs=rhs, Lm=Lm, cbm=cbm)
                    ctxd.update(xdt=xdt, Mm=Mm)
                return ctxd

            def stageA2(d, A):
                rhs = A["rhs"]; Lm = A["Lm"]; Mm = A["Mm"]; cbm = A["cbm"]
                for q in range(4):
                    pq = p_seg.next()
                    s.op("pe", I("matmul", pq[:, :], lhsT=Umat[:, d, :], rhs=rhs[:, q * 4:(q + 1) * 4, :].rearrange("p h t -> p (h t)"), start=True, stop=True), inc=True,
                         reads=[Umat, rhs.halves[q // 2]], writes=[pq])
                    s.op("act", I("activation", out=Lm[:, q * 4:(q + 1) * 4, :].rearrange("p h t -> p (h t)"), in_=pq[:, :], func=AF.Exp), reads=[pq], writes=[Lm])
                    gq = q // 2
                    s.op("dve", I("tensor_tensor", out=Mm[:, q * 4:(q + 1) * 4, :], in0=Lm[:, q * 4:(q + 1) * 4, :],
                                  in1=cbm[:, gq:gq + 1, :].to_broadcast([128, 4, 128]), op=ALU.mult), reads=[Lm, cbm], writes=[Mm])

            def stageB(d, Csrc, c, A, need_y, final):
                c0 = A["c0"]; sm = A["sm"]; Btsrc = A["Btsrc"]
                S = Sst[d]; Sb = Sbf[d]
                if need_y:
                    for g in range(2):
                        s.op("pe", I("matmul", p_yo[g][:, :], lhsT=Csrc[:, g, c0:c0 + 128], rhs=Sb[:, g * 512:(g + 1) * 512], start=True, stop=True), inc=True, reads=[Csrc, Sb], writes=[p_yo[g]])
                for g in range(2):
                    s.op("pe", I("matmul", p_st[g][:, :], lhsT=Btsrc[:, c, g * 128:(g + 1) * 128], rhs=A["xdw"][:, g * 8:(g + 1) * 8, :].rearrange("p h q -> p (h q)"), start=True, stop=True), inc=True,
                         reads=[Btsrc, A["xdw"]], writes=[p_st[g]])
                for g in range(2):
                    s.op("pool", I("tensor_tensor", out=S[:, g * 512:(g + 1) * 512].rearrange("p (h q) -> p h q", h=8), in0=S[:, g * 512:(g + 1) * 512].rearrange("p (h q) -> p h q", h=8),
                                   in1=sm[:, 4, g * 8:(g + 1) * 8].unsqueeze(2).to_broadcast([128, 8, 64]), op=ALU.mult), reads=[S, sm], writes=[S])
                yt = yt_r.next() if need_y else None
                if need_y:
                    for g in range(2):
                        s.op("dve", I("tensor_tensor", out=yt[:, g * 512:(g + 1) * 512].rearrange("p (h q) -> p h q", h=8), in0=p_yo[g][:, :].rearrange("p (h q) -> p h q", h=8),
                                      in1=sm[:, 2, g * 8:(g + 1) * 8].unsqueeze(2).to_broadcast([128, 8, 64]), op=ALU.mult), reads=[p_yo[g], sm], writes=[yt])
                for g in range(2):
                    s.op("dve", I("tensor_tensor", out=S[:, g * 512:(g + 1) * 512], in0=p_st[g][:, :], in1=S[:, g * 512:(g + 1) * 512], op=ALU.add), reads=[p_st[g], S], writes=[S])
                s.op("act", I("activation", out=Sb[:], in_=S[:], func=AF.Identity), reads=[S], writes=[Sb])
                if need_y:
                    yo = yo_r.next()
                    for g in range(2):
                        for h in range(8):
                            hh = g * 8 + h
                            s.op("pe", I("matmul", p_yo[g][:, h * 64:(h + 1) * 64], lhsT=A["Mm"][:, hh, :], rhs=A["xdt"][:, hh, :], start=True, stop=(d != 0)), inc=(d != 0 and h == 7),
                                 reads=[A["Mm"], A["xdt"]], writes=[p_yo[g]])
                            if d == 0:
                                s.op("pe", I("matmul", p_yo[g][:, h * 64:(h + 1) * 64], lhsT=Dg[:, hh, :], rhs=A["xs_sb"][:, hh, :], start=False, stop=True), inc=(h == 7),
                                     reads=[Dg, A["xs_sb"]], writes=[p_yo[g]])
                        s.op("dve", I("tensor_tensor", out=yo[:, g * 512:(g + 1) * 512], in0=p_yo[g][:, :], in1=yt[:, g * 512:(g + 1) * 512], op=ALU.add), reads=[p_yo[g], yt], writes=[yo])
                    final(c, yo, A)

            def ssd_pass(d, xsrc, Bsrc, Csrc, dtsrc, Btsrc, nchunk, need_y, final):
                order = list(range(nchunk)) if d == 0 else list(range(nchunk - 1, -1, -1))
                prev = None
                for step in range(nchunk + 1):
                    cur = None
                    if step < nchunk:
                        cur = (order[step], stageA(d, xsrc, Bsrc, Csrc, dtsrc, Btsrc, order[step], need_y))
                    if prev is not None:
                        stageB(d, Csrc, prev[0], prev[1], need_y, final)
                    if cur is not None and need_y:
                        stageA2(d, cur[1])
                    prev = cur

            for d in range(2):
                s.op("pool", I("memset", Sst[d][:], 0.0), writes=[Sst[d]])
                s.op("pool", I("memset", Sbf[d][:], 0.0), writes=[Sbf[d]])
                ssd_pass(d, cxsT, cBT, cCT, cdts, cBtok, 2, False, None)
            if debug and "S0" in dbg_d:
                dump("S0", Sst[0][:], lambda dd: dd[:, 0, :], [Sst[0]]); dump("S0", Sst[1][:], lambda dd: dd[:, 1, :], [Sst[1]])

            def fin_b(c, yo, A):
                s.dma("sp", I("dma_start", out=ysc_d[c * 128:(c + 1) * 128, :], in_=yo[:]), reads=[yo], writes=[ysc[c]])

            ssd_pass(1, xsT, BT, CT, dts, Btok, NCH, True, fin_b)

            def fin_f(c, yo, A):
                yb = yb_r.next()
                s.dma("sp", I("dma_start", out=yb[:], in_=ysc_d[c * 128:(c + 1) * 128, :]), reads=[ysc[c]], writes=[yb])
                s.op("pool", I("tensor_tensor", out=yb[:], in0=yb[:], in1=yo[:], op=ALU.add), reads=[yb, yo], writes=[yb])
                s.dma("sp", I("dma_start", out=ysc_d[c * 128:(c + 1) * 128, :], in_=yb[:]), reads=[yb], writes=[ysc[c]])

            ssd_pass(0, xsT, BT, CT, dts, Btok, NCH, True, fin_f)
            s.barrier()
            s.emit()

    if debug and debug.get("stop") == "ssd":
        for c in range(NCH):
            s.dma("sp", I("dma_start", out=out_d[c * 128:(c + 1) * 128, :], in_=ysc_d[c * 128:(c + 1) * 128, :]), reads=[ysc[c]], writes=[outb[c]])
        s.barrier()
        s.emit()
        return

    with contextlib.ExitStack() as P4:
        uT = sb(P4, "pmT_all", [128, 8, L], BF16)
        uTh = [Buf(uT.ap, "pmT%d" % j) for j in range(8)]
        pscol = sb(P4, "pscol", [128, 8])
        s.dma("sp", I("dma_start", out=pscol[:], in_=pscale_d.rearrange("o (k p) -> p (o k)", p=128), allow_slow_non_contiguous=True), writes=[pscol])
        with contextlib.ExitStack() as P4ab:
            utok = sb(P4ab, "utok", [128, NCH, 1024], BF16)
            utokh = [Buf(utok.ap, "utok%d" % c) for c in range(NCH)]
            with contextlib.ExitStack() as P4a:
                wu = sb(P4a, "wu", [128, 8, 1024], BF16)
                wload_cast(wu, wu[:], win_d[:, 2592:3616].rearrange("(k p) n -> p k n", p=128))
                PW = prenorm_work(P4a, "p4a")
                hx_ring = RR(ring(P4a, "hx4", 2, [128, 8, 512], BF16))
                pu = RR(ring(P4a, "pu", 4, [128, 512], psum=True))

                def tile_loads4(i):
                    return [prenorm_load(PW, x_d[i * 512 + sub * 128:i * 512 + (sub + 1) * 128, :], dram_x) for sub in range(4)]

                nxt = tile_loads4(0)
                for i in range(8):
                    hx = hx_ring.next()
                    xts = nxt
                    fr = [prenorm_front(PW, None, None, xt=xts[sub]) for sub in range(4)]
                    if i + 1 < 8:
                        nxt = tile_loads4(i + 1)
                    for sub in range(4):
                        prenorm_back(PW, fr[sub][1], colA1, colS1, lambda k, sub=sub, hx=hx: hx[:, k, sub * 128:(sub + 1) * 128], hx)
                    for sub in range(4):
                        c = i * 4 + sub
                        for hf in range(2):
                            pq = pu.next()
                            for k in range(8):
                                s.op("pe", I("matmul", pq[:, :], lhsT=hx[:, k, sub * 128:(sub + 1) * 128], rhs=wu[:, k, hf * 512:(hf + 1) * 512], start=(k == 0), stop=(k == 7)), inc=(k == 7), reads=[wu, hx], writes=[pq])
                            if hf == 0:
                                s.op("act", I("activation", out=utok[:, c, 0:512], in_=pq[:, :], func=AF.Identity), reads=[pq], writes=[utokh[c]])
                            else:
                                s.op("dve", I("tensor_copy", out=utok[:, c, 512:1024], in_=pq[:, :]), reads=[pq], writes=[utokh[c]])
                s.barrier()
                s.emit()
            with contextlib.ExitStack() as P4b:
                kinv = sb(P4b, "kinv_sb", [128, 4, 64])
                s.dma("sp", I("dma_start", out=kinv[:].rearrange("p w j -> p (w j)"), in_=kinv_d[0, :].partition_broadcast(128)), writes=[kinv])
                kpb = sb(P4b, "kpb", [128, NPM, 128], BF16)
                wload_cast(kpb, kpb[:], kpool_d[:, :, :])
                pw = sb(P4b, "pw", [128, 4, 2, 256], BF16)
                wload_cast(pw, pw[:], poolw_d.rearrange("g (ci p) n -> p g ci n", p=128))
                imap = sb(P4b, "imap", [128, 64, 64])
                dT = [sb(P4b, "dT0", [128, L], BF16), sb(P4b, "dT1", [128, L], BF16)]
                tmp_r = RR(ring(P4b, "ptmp", 2, [128, 512]))
                p_S = RR(ring(P4b, "p_S", 2, [128, 512], psum=True))
                p_U = RR(ring(P4b, "p_U", 2, [128, 512], BF16, psum=True))
                ppm = RR(ring(P4b, "ppm", 4, [128, 512], psum=True))
                for g in range(4):
                    s.op("dve", I("tensor_tensor", out=imap[:], in0=kinv[:, g, :].unsqueeze(2).to_broadcast([128, 64, 64]), in1=kinv[:, g, :].unsqueeze(1).to_broadcast([128, 64, 64]), op=ALU.mult),
                         reads=[kinv], writes=[imap])
                    deltas = sorted(d_ for (gi, d_) in _PIDX if gi == g)
                    for jj in range(2):
                        j = 2 * g + jj
                        for q in range(8):
                            pS = p_S.next(); pU = p_U.next(); tmp = tmp_r.next()
                            for cc in range(4):
                                cd = 4 * q + cc
                                valid = [d_ for d_ in deltas if 0 <= cd + d_ < NCH]
                                for n_, d_ in enumerate(valid):
                                    s.op("pe", I("matmul", pS[:, cc * 128:(cc + 1) * 128], lhsT=utok[:, cd + d_, j * 128:(j + 1) * 128], rhs=kpb[:, _PIDX[(g, d_)], :],
                                                 start=(n_ == 0), stop=(n_ == len(valid) - 1)), inc=(n_ == len(valid) - 1), reads=[utokh[cd + d_], kpb], writes=[pS])
                                s.op("pe", I("transpose", out=pU[:, cc * 128:(cc + 1) * 128], in_=utok[:, cd, j * 128:(j + 1) * 128], identity=ident_b), reads=[utokh[cd], kcb], writes=[pU])
                            s.op("dve", I("tensor_tensor", out=tmp[:], in0=pS[:, :], in1=imap[:, 8 * q:8 * q + 8, :].rearrange("p r q -> p (r q)"), op=ALU.mult), reads=[pS, imap], writes=[tmp])
                            s.op("dve", I("tensor_tensor", out=dT[jj][:, q * 512:(q + 1) * 512], in0=tmp[:], in1=pU[:, :], op=ALU.subtract), reads=[tmp, pU], writes=[dT[jj]])
                    for i in range(8):
                        pcs = [ppm.next(), ppm.next()]
                        for co in range(2):
                            for ci in range(2):
                                s.op("pe", I("matmul", pcs[co][:, :], lhsT=pw[:, g, ci, co * 128:(co + 1) * 128], rhs=dT[ci][:, i * 512:(i + 1) * 512], start=(ci == 0), stop=(ci == 1)), inc=(ci == 1),
                                     reads=[pw, dT[ci]], writes=[pcs[co]])
                        for co in range(2):
                            s.op("act", I("activation", out=uT[:, 2 * g + co, i * 512:(i + 1) * 512], in_=pcs[co][:, :], func=AF.Identity, scale=pscol[:, 2 * g + co:2 * g + co + 1]),
                                 reads=[pcs[co], pscol], writes=[uTh[2 * g + co]])
                s.barrier()
                s.emit()
        with contextlib.ExitStack() as P4c:
            wz = sb(P4c, "wz", [128, 8, 1024], BF16)
            wo = sb(P4c, "wo", [128, 16, 1024], BF16)
            sncol = sb(P4c, "sncol", [128, 8])
            s.dma("sp", I("dma_start", out=sncol[:], in_=snw_d.rearrange("o (k p) -> p (o k)", p=128), allow_slow_non_contiguous=True), writes=[sncol])
            wload_cast(wz, wz[:], win_d[:, 0:1024].rearrange("(k p) n -> p k n", p=128))
            wload_cast(wo, wo[:], wout_d.rearrange("(k p) n -> p k n", p=128))
            for k in range(8):
                s.op("dve" if k % 2 else "pool", I("tensor_scalar", out=wo[:, k, :], in0=wo[:, k, :], scalar1=sncol[:, k:k + 1], scalar2=None, op0=ALU.mult), reads=[wo, sncol], writes=[wo])
            PW = {
                "xt": RR(ring(P4c, "p4cxt", 7, [128, D])),
                "sq": RR(ring(P4c, "p4csq", 1, [128, D], BF16)),
                "ss": RR(ring(P4c, "p4css", 3, [128, 4])),
                "xn": RR(ring(P4c, "p4cxn", 2, [128, D], BF16)),
                "pt": RR(ring(P4c, "p4cpt", 1, [128, D], BF16, psum=True)),
            }
            hxc_r = RR(ring(P4c, "hxc", 3, [128, 8, 128], BF16))
            sz_r = RR(ring(P4c, "sz", 1, [128, D])); yt_r = RR(ring(P4c, "yt4", 4, [128, D]))
            ygn_r = RR(ring(P4c, "ygn", 2, [128, D], BF16)); ygT_r = RR(ring(P4c, "ygT", 2, [128, 8, 128], BF16))
            st_r = RR(ring(P4c, "st4", 3, [128, 16])); x1_r = RR(ring(P4c, "x1", 3, [128, D]))
            sq4 = sb(P4c, "sq4", [128, 512], BF16)
            p_z = RR(ring(P4c, "p_z", 2, [128, 512], psum=True))
            p_T4 = ps(P4c, "p_T4", [128, D], BF16)
            p_mix = RR(ring(P4c, "p_mix", 4, [128, 512], psum=True))
            ST = {}

            def FL(c):
                xt = prenorm_load(PW, x_d[c * 128:(c + 1) * 128, :], dram_x)
                yt = yt_r.next()
                s.dma("sp", I("dma_start", out=yt[:], in_=ysc_d[c * 128:(c + 1) * 128, :]), reads=[ysc[c]], writes=[yt])
                ST[c] = dict(xt=xt, yt=yt)

            def F(c):
                xt, xn = prenorm_front(PW, None, None, xt=ST[c]["xt"])
                ST[c].update(xn=xn, st=st_r.next())

            def PT(c):
                hxc = hxc_r.next(); ST[c]["hxc"] = hxc
                prenorm_back(PW, ST[c]["xn"], colA1, colS1, lambda k, hxc=hxc: hxc[:, k, :], hxc)

            def YT(c):
                ygn = ST[c]["ygn"]; ygT = ygT_r.next(); ST[c]["ygT"] = ygT
                for k in range(8):
                    s.op("pe", I("transpose", out=p_T4[:, k * 128:(k + 1) * 128], in_=ygn[:, k * 128:(k + 1) * 128], identity=ident_b), inc=(k == 7), reads=[ygn, kcb], writes=[p_T4])
                s.op("act", I("activation", out=ygT[:].rearrange("p k t -> p (k t)"), in_=p_T4[:, :], func=AF.Identity), reads=[p_T4], writes=[ygT])

            def Z(c):
                hxc = ST[c]["hxc"]; yt = ST[c]["yt"]; st = ST[c]["st"]
                sz = sz_r.next(); ygn = ygn_r.next(); ST[c]["ygn"] = ygn
                for hf in range(2):
                    pz = p_z.next()
                    for k in range(8):
                        s.op("pe", I("matmul", pz[:, :], lhsT=hxc[:, k, :], rhs=wz[:, k, hf * 512:(hf + 1) * 512], start=(k == 0), stop=(k == 7)), inc=(k == 7), reads=[hxc, wz], writes=[pz])
                    s.op("act", I("activation", out=sz[:, hf * 512:(hf + 1) * 512], in_=pz[:, :], func=AF.Silu), reads=[pz], writes=[sz])
                s.op("dve", I("tensor_tensor", out=yt[:], in0=yt[:], in1=sz[:], op=ALU.mult), reads=[yt, sz], writes=[yt])
                for gg in range(2):
                    s.op("act", I("activation", out=sq4[:], in_=yt[:, gg * 512:(gg + 1) * 512], func=AF.Square, accum_out=st[:, gg:gg + 1]), reads=[yt], writes=[sq4, st])
                s.op("dve", I("tensor_scalar", out=st[:, 2:4], in0=st[:, 0:2], scalar1=1.0 / 512, scalar2=EPS, op0=ALU.mult, op1=ALU.add), reads=[st], writes=[st])
                s.op("pool", I("tensor_tensor", out=st[:, 4:6], in0=st[:, 2:4], in1=halfneg[:, 0:1].to_broadcast([128, 2]), op=ALU.pow), reads=[st, halfneg], writes=[st])
                for gg in range(2):
                    s.op("dve", I("tensor_scalar", out=ygn[:, gg * 512:(gg + 1) * 512], in0=yt[:, gg * 512:(gg + 1) * 512], scalar1=st[:, 4 + gg:5 + gg], scalar2=None, op0=ALU.mult),
                         reads=[yt, st], writes=[ygn])

            def OP(c):
                ygT = ST[c]["ygT"]; st = ST[c]["st"]
                pms = [p_mix.next(), p_mix.next()]; ST[c]["pms"] = pms
                for hf in range(2):
                    for k in range(16):
                        lhs = ygT[:, k, :] if k < 8 else uT[:, k - 8, c * 128:(c + 1) * 128]
                        rd = [ygT, wo] if k < 8 else [uTh[k - 8], wo]
                        s.op("pe", I("matmul", pms[hf][:, :], lhsT=lhs, rhs=wo[:, k, hf * 512:(hf + 1) * 512], start=(k == 0), stop=(k == 15)), inc=(k == 15), reads=rd, writes=[pms[hf]])
                    s.op("act", I("activation", out=sq4[:], in_=pms[hf][:, :], func=AF.Square, accum_out=st[:, 6 + hf:7 + hf]), reads=[pms[hf]], writes=[sq4, st])
                s.op("dve", I("tensor_tensor", out=st[:, 8:9], in0=st[:, 6:7], in1=st[:, 7:8], op=ALU.add), reads=[st], writes=[st])
                s.op("dve", I("tensor_scalar", out=st[:, 9:10], in0=st[:, 8:9], scalar1=1.0 / D, scalar2=EPS, op0=ALU.mult, op1=ALU.add), reads=[st], writes=[st])
                s.op("pool", I("tensor_tensor", out=st[:, 10:11], in0=st[:, 9:10], in1=halfneg[:], op=ALU.pow), reads=[st, halfneg], writes=[st])

            def FIN(c):
                pms = ST[c]["pms"]; st = ST[c]["st"]; xt = ST[c]["xt"]; x1 = x1_r.next()
                for hf in range(2):
                    s.op("dve", I("scalar_tensor_tensor", out=x1[:, hf * 512:(hf + 1) * 512], in0=pms[hf][:, :], scalar=st[:, 10:11], in1=G1[:, hf * 512:(hf + 1) * 512], op0=ALU.mult, op1=ALU.mult),
                         reads=[pms[hf], st, G1], writes=[x1])
                s.op("dve", I("tensor_tensor", out=x1[:], in0=x1[:], in1=xt[:], op=ALU.add), reads=[x1, xt], writes=[x1])
                PEND.append((c, x1))
                del ST[c]

            PEND = []

            def flush_stores():
                while PEND:
                    c, x1 = PEND.pop(0)
                    s.dma("sp", I("dma_start", out=out_d[c * 128:(c + 1) * 128, :], in_=x1[:]), reads=[x1], writes=[outb[c]])

            FL(0); FL(1); FL(2); F(0); F(1); PT(0)
            for t in range(NCH + 2):
                if t + 3 < NCH:
                    FL(t + 3)
                flush_stores()
                if t + 2 < NCH:
                    F(t + 2)
                if t + 1 < NCH:
                    PT(t + 1)
                if 0 <= t - 1 < NCH:
                    YT(t - 1)
                if t < NCH:
                    Z(t)
                if 0 <= t - 1 < NCH:
                    OP(t - 1)
                if 0 <= t - 2 < NCH:
                    FIN(t - 2)
            flush_stores()
            s.barrier()
            s.emit()
    if debug and debug.get("stop") == "p4":
        return

    with contextlib.ExitStack() as P5:
        w1 = sb(P5, "w1", [128, 8, DFF], BF16)
        w2 = sb(P5, "w2", [128, 32, D], BF16)
        for hh in range(2):
            wload_cast(w1, w1[:, :, hh * 2048:(hh + 1) * 2048], w1_d[:, hh * 2048:(hh + 1) * 2048].rearrange("(k p) n -> p k n", p=128))
        wload_cast(w2, w2[:], w2_d.rearrange("(f p) n -> p f n", p=128))
        PW = {
            "xt": RR(ring(P5, "p5xt", 4, [128, D])),
            "sq": RR(ring(P5, "p5sq", 1, [128, D], BF16)),
            "ss": RR(ring(P5, "p5ss", 4, [128, 4])),
            "xn": RR(ring(P5, "p5xn", 2, [128, D], BF16)),
            "pt": RR(ring(P5, "p5pt", 2, [128, D], BF16, psum=True)),
        }
        hm_r = RR(ring(P5, "hm", 2, [128, 8, 256], BF16))
        hT = sb(P5, "hT", [128, 32, 256], BF16)
        r_r = RR(ring(P5, "relu", 3, [128, 256], BF16))
        o_r = RR(ring(P5, "o5", 4, [128, D]))
        st_r = RR(ring(P5, "st5", 2, [128, 8]))
        sq5 = sb(P5, "sq5", [128, 512], BF16)
        p_h = RR(ring(P5, "p_h", 3, [128, 256], psum=True))
        p_o = RR(ring(P5, "p_o", 3, [128, 512], psum=True))
        NT = L // 256
        MS = {}

        def front5(i):
            fr = []
            for sub in range(2):
                c = i * 2 + sub
                fr.append(prenorm_front(PW, out_d[c * 128:(c + 1) * 128, :], outb[c]))
            MS[i] = dict(fr=fr)

        def back5(i):
            hm = hm_r.next(); MS[i]["hm"] = hm
            for sub in range(2):
                prenorm_back(PW, MS[i]["fr"][sub][1], colA2, colS2, lambda k, sub=sub, hm=hm: hm[:, k, sub * 128:(sub + 1) * 128], hm)

        def mlp1(i):
            hm = MS[i]["hm"]
            for f in range(32):
                if f == 8 and i + 1 < NT:
                    front5(i + 1)
                ph = p_h.next(); rr = r_r.next()
                for k in range(8):
                    s.op("pe", I("matmul", ph[:, :], lhsT=w1[:, k, f * 128:(f + 1) * 128], rhs=hm[:, k, :], start=(k == 0), stop=(k == 7)), inc=(k == 7), reads=[w1, hm], writes=[ph])
                s.op("act", I("activation", out=rr[:], in_=ph[:, :], func=AF.Relu), reads=[ph], writes=[rr])
                s.op("pool", I("tensor_tensor", out=hT[:, f, :], in0=rr[:], in1=rr[:], op=ALU.mult), reads=[rr], writes=[hT])

        def mlp2(i):
            for sub in range(2):
                c = i * 2 + sub
                o = o_r.next(); st = st_r.next(); xt = MS[i]["fr"][sub][0]
                pos = [p_o.next(), p_o.next()]
                for hf in range(2):
                    for f in range(32):
                        s.op("pe", I("matmul", pos[hf][:, :], lhsT=hT[:, f, sub * 128:(sub + 1) * 128], rhs=w2[:, f, hf * 512:(hf + 1) * 512], start=(f == 0), stop=(f == 31)), inc=(f == 31), reads=[hT, w2], writes=[pos[hf]])
                    s.op("act", I("activation", out=sq5[:], in_=pos[hf][:, :], func=AF.Square, accum_out=st[:, hf:hf + 1]), reads=[pos[hf]], writes=[sq5, st])
                s.op("dve", I("tensor_tensor", out=st[:, 2:3], in0=st[:, 0:1], in1=st[:, 1:2], op=ALU.add), reads=[st], writes=[st])
                s.op("dve", I("tensor_scalar", out=st[:, 3:4], in0=st[:, 2:3], scalar1=1.0 / D, scalar2=EPS, op0=ALU.mult, op1=ALU.add), reads=[st], writes=[st])
                s.op("pool", I("tensor_tensor", out=st[:, 4:5], in0=st[:, 3:4], in1=halfneg[:], op=ALU.pow), reads=[st, halfneg], writes=[st])
                for hf in range(2):
                    s.op("dve", I("scalar_tensor_tensor", out=o[:, hf * 512:(hf + 1) * 512], in0=pos[hf][:, :], scalar=st[:, 4:5], in1=G2[:, hf * 512:(hf + 1) * 512], op0=ALU.mult, op1=ALU.mult),
                         reads=[pos[hf], st, G2], writes=[o])
                s.op("pool", I("tensor_tensor", out=o[:], in0=o[:], in1=xt[:], op=ALU.add), reads=[o, xt], writes=[o])
                PEND5.append((c, o))
            del MS[i]

        PEND5 = []

        def flush5():
            while PEND5:
                c, o = PEND5.pop(0)
                s.dma("sp", I("dma_start", out=out_d[c * 128:(c + 1) * 128, :], in_=o[:]), reads=[o], writes=[outb[c]])

        front5(0); back5(0)
        for i in range(NT):
            flush5()
            mlp1(i)
            if i + 1 < NT:
                back5(i + 1)
            mlp2(i)
        flush5()
        s.barrier()
        s.emit()


def pool_mats():
    mats = []; index = {}
    a = np.repeat(np.arange(2), 64); j = np.tile(np.arange(64), 2)
    for gi, w in enumerate((2, 4, 8, 16)):
        h = w // 2
        for delta in range(-5, 6):
            rd = 2 * delta + a[:, None] - a[None, :]
            cd = j[:, None] - j[None, :]
            m = ((rd >= -h) & (rd <= h - 1) & (cd >= -h) & (cd <= h - 1)).astype(np.float32)
            if m.any():
                index[(gi, delta)] = len(mats); mats.append(m)
    return np.stack(mats, axis=1), index


_PM, _PIDX = pool_mats()
NPM = _PM.shape[1]


def make_consts():
    k = np.arange(128)
    kc = np.zeros((128, 5, 128), np.float32)
    kc[:, 0, :] = np.eye(128, dtype=np.float32)
    kc[:, 1, :] = (k[:, None] <= k[None, :]).astype(np.float32)
    kc[:, 2, :] = (k[:, None] >= k[None, :]).astype(np.float32)
    kc[:, 3, :] = 1.0
    inv = np.zeros((4, 64), np.float32)
    t = np.arange(64)
    for gi, w in enumerate((2, 4, 8, 16)):
        lo = np.clip(t - w // 2, 0, 64); hi = np.clip(t + w // 2, 0, 64)
        inv[gi] = 1.0 / (hi - lo)
    return kc, inv.reshape(1, 256)


def core_inputs(inputs, b):
    f = lambda a: np.ascontiguousarray(np.asarray(a, dtype=np.float32))
    kc, kinv = make_consts()
    m = {
        "x": f(inputs["x"][b]), "ctx": f(inputs["ctx"][b]), "c": f(inputs["c"][b:b + 1]),
        "c_ctx": f(np.asarray(inputs["c_ctx"]).reshape(1, D)),
        "w_ada": f(inputs["w_ada"][0]), "b_ada": f(inputs["b_ada"][0:1]),
        "pre_mix_g": f(inputs["pre_mix_g"][0:1]), "post_mix_g": f(inputs["post_mix_g"][0:1]),
        "pre_mlp_g": f(inputs["pre_mlp_g"][0:1]), "post_mlp_g": f(inputs["post_mlp_g"][0:1]),
        "w_in": f(inputs["w_in"][0]), "conv_w": f(inputs["conv_w"][0]), "conv_b": f(inputs["conv_b"][0:1]),
        "dt_bias": f(np.asarray(inputs["dt_bias"][0]).reshape(1, 32)), "a_log": f(np.asarray(inputs["a_log"][0]).reshape(1, 32)),
        "d_skip": f(inputs["d_skip"][0:1]), "ssm_norm_w": f(inputs["ssm_norm_w"][0:1]),
        "pool_w": f(inputs["pool_w"][0]), "pool_scale": f(inputs["pool_scale"][0:1]),
        "w_out": f(inputs["w_out"][0]), "w_mlp1": f(inputs["w_mlp1"][0]), "w_mlp2": f(inputs["w_mlp2"][0]),
        "kconst": kc, "kinv": kinv, "kpool": np.ascontiguousarray(_PM),
    }
    return m


def kernel(**inputs):
    nc = build_program()
    nb = inputs["x"].shape[0]
    in_maps = [core_inputs(inputs, b) for b in range(nb)]
    res = run_bass_kernel_spmd(nc, in_maps, core_ids=list(range(nb)))
    out = np.stack([np.asarray(r["out"]) for r in res.results], axis=0)
    return out.astype(np.float32)
```

```python
import contextlib
import numpy as np
import concourse.bass as bass
import concourse.mybir as mybir
from concourse.bass_utils import run_bass_kernel_spmd

F32 = mybir.dt.float32
BF16 = mybir.dt.bfloat16
AF = mybir.ActivationFunctionType
ALU = mybir.AluOpType

L = 4096
D = 1024
KD = 8
NCH = 32
CTXL = 256
DFF = 4096
EPS = 1e-6
ENGS = ("pe", "act", "dve", "pool", "sp")
N_DMA_SEMS = 40
N_SW_SEMS = 14


class Buf:
    def __init__(self, ap=None, name=""):
        self.ap = ap
        self.name = name
        self.w = {}
        self.r = {}
        self.strict = "sq" in name

    def __getitem__(self, k):
        return self.ap[k]


def I(name, *args, **kw):
    return lambda e: getattr(e, name)(*args, **kw)


class Sched:
    def __init__(self, nc, stack):
        self.nc = nc
        self.q = {e: [] for e in ENGS}
        self.cnt = {e: 0 for e in ENGS}
        self.known = {e: {} for e in ENGS}
        self.dma_val = [0] * N_DMA_SEMS
        self.dma_rr = 0
        self.csem = {e: stack.enter_context(nc.semaphore("c_" + e)) for e in ENGS}
        self.dsem = [stack.enter_context(nc.semaphore("d_%d" % i)) for i in range(N_DMA_SEMS)]
        self.swsem = [stack.enter_context(nc.semaphore("w_%d" % i)) for i in range(N_SW_SEMS)]
        self.sw_used = 0
        self.n_ins = 0

    def _need(self, eng, key, val, waits):
        if key == eng and eng == "pe":
            return
        if self.known[eng].get(key, 0) >= val:
            return
        self.known[eng][key] = val
        waits[key] = max(waits.get(key, 0), val)

    def _deps(self, eng, reads, writes):
        waits = {}
        for t in reads:
            for k, v in t.w.items():
                self._need(eng, k, v, waits)
        for t in writes:
            for k, v in t.w.items():
                if k != eng or eng == "pool" or t.strict:
                    self._need(eng, k, v, waits)
            for k, v in t.r.items():
                if k != eng or eng == "pool" or t.strict:
                    self._need(eng, k, v, waits)
        return waits

    def _commit(self, key, val, reads, writes):
        for t in reads:
            t.r[key] = max(t.r.get(key, 0), val)
        for t in writes:
            t.w[key] = max(t.w.get(key, 0), val)
            t.r = {}

    def op(self, eng, fn, reads=(), writes=(), inc=True):
        waits = self._deps(eng, reads, writes)
        if inc:
            self.cnt[eng] += 1
            self.q[eng].append((waits, fn, ("c", eng)))
            self._commit(eng, self.cnt[eng], reads, writes)
        else:
            assert eng == "pe"
            self.q[eng].append((waits, fn, ("n", eng)))
            self._commit(eng, self.cnt[eng] + 1, reads, writes)

    def dma(self, eng, fn, reads=(), writes=()):
        waits = self._deps(eng, reads, writes)
        i = self.dma_rr
        self.dma_rr = (self.dma_rr + 1) % N_DMA_SEMS
        if self.dma_val[i] > 0:
            self._need(eng, i, self.dma_val[i], waits)
        self.dma_val[i] += 16
        self.q[eng].append((waits, fn, ("d", i)))
        self._commit(i, self.dma_val[i], reads, writes)

    def dma_sw(self, fn, reads=(), writes=()):
        eng = "pool"
        waits = self._deps(eng, reads, writes)
        key = "sw%d" % self.sw_used
        self.sw_used += 1
        assert self.sw_used <= N_SW_SEMS
        self.q[eng].append((waits, fn, ("w", key)))
        self._commit(key, 16, reads, writes)

    def _sem(self, key):
        if isinstance(key, int):
            return self.dsem[key]
        if key.startswith("sw"):
            return self.swsem[int(key[2:])]
        return self.csem[key]

    def barrier(self):
        for eng in ENGS:
            waits = {}
            for e2 in ENGS:
                if e2 != eng and self.cnt[e2] > 0:
                    self._need(eng, e2, self.cnt[e2], waits)
            for i in range(N_DMA_SEMS):
                if self.dma_val[i] > 0:
                    self._need(eng, i, self.dma_val[i], waits)
            for i in range(self.sw_used):
                self._need(eng, "sw%d" % i, 16, waits)
            self.q[eng].append((waits, None, None))

    def emit(self):
        nc = self.nc
        with nc.Block() as block:
            def run(name, e):
                for waits, fn, inc in self.q[name]:
                    for key, val in waits.items():
                        e.wait_ge(self._sem(key), val)
                    if fn is None:
                        continue
                    ins = fn(e)
                    self.n_ins += 1
                    if inc[0] == "n":
                        pass
                    elif inc[0] == "c":
                        ins.then_inc(self.csem[inc[1]], 1)
                    elif inc[0] == "w":
                        ins.then_inc(self._sem(inc[1]), 16)
                    else:
                        ins.then_inc(self.dsem[inc[1]], 16)

            @block.tensor
            def _(e):
                run("pe", e)

            @block.scalar
            def _(e):
                run("act", e)

            @block.vector
            def _(e):
                run("dve", e)

            @block.gpsimd
            def _(e):
                run("pool", e)

            @block.sync
            def _(e):
                run("sp", e)
        self.q = {e: [] for e in ENGS}


def build_program(debug=None):
    nc = bass.Bass("TRN2", target_bir_lowering=False)
    ES = contextlib.ExitStack()
    with ES:
        _build(nc, ES, debug)
    return nc


def _build(nc, ES, debug):
    def din(name, shape):
        return nc.dram_tensor(name, list(shape), F32, kind="ExternalInput").ap()

    x_d = din("x", [L, D]); ctx_d = din("ctx", [CTXL, D])
    c_d = din("c", [1, D]); cctx_d = din("c_ctx", [1, D])
    wada_d = din("w_ada", [D, 6 * D]); bada_d = din("b_ada", [1, 6 * D])
    gpm_d = din("pre_mix_g", [1, D]); gqm_d = din("post_mix_g", [1, D])
    gpl_d = din("pre_mlp_g", [1, D]); gql_d = din("post_mlp_g", [1, D])
    win_d = din("w_in", [D, 3616]); convw_d = din("conv_w", [5, 1536]); convb_d = din("conv_b", [1, 1536])
    dtb_d = din("dt_bias", [1, 32]); alog_d = din("a_log", [1, 32]); dsk_d = din("d_skip", [1, 16])
    snw_d = din("ssm_norm_w", [1, D]); poolw_d = din("pool_w", [4, 256, 256]); pscale_d = din("pool_scale", [1, D])
    wout_d = din("w_out", [2 * D, D]); w1_d = din("w_mlp1", [D, DFF]); w2_d = din("w_mlp2", [DFF, D])
    kconst_d = din("kconst", [128, 5, 128])
    kinv_d = din("kinv", [1, 4 * 64])
    kpool_d = din("kpool", [128, NPM, 128])
    out_d = nc.dram_tensor("out", [L, D], F32, kind="ExternalOutput").ap()
    ysc_d = nc.dram_tensor("y_scratch", [L, D], F32, kind="Internal").ap()
    dbg_d = {}
    if debug:
        for name, shape in debug.get("shapes", {}).items():
            dbg_d[name] = nc.dram_tensor("dbg_" + name, list(shape), F32, kind="ExternalOutput").ap()

    s = Sched(nc, ES)

    def sb(stack, name, shape, dt=F32):
        return Buf(stack.enter_context(nc.sbuf_tensor(name, list(shape), dt)), name)

    def ps(stack, name, shape, dt=F32):
        return Buf(stack.enter_context(nc.psum_tensor(name, list(shape), dt)), name)

    def ring(stack, name, n, shape, dt=F32, psum=False):
        mk = ps if psum else sb
        return [mk(stack, "%s%d" % (name, i), shape, dt) for i in range(n)]

    class RR:
        def __init__(self, bufs):
            self.bufs = bufs; self.i = 0

        def next(self):
            b = self.bufs[self.i % len(self.bufs)]; self.i += 1
            return b

    def dump(name, src_ap, dst_slice, reads):
        if debug and name in dbg_d:
            s.dma("sp", I("dma_start", out=dst_slice(dbg_d[name]), in_=src_ap), reads=reads, writes=[Buf(None, "dbg")])

    G = ES
    kc = sb(G, "kc", [128, 5, 128])
    kcb = sb(G, "kcb", [128, 5, 128], BF16)
    ident_b = kcb[:, 0, :]
    trif, trib, ones = kc[:, 1, :], kc[:, 2, :], kc[:, 3, :]
    colA1 = sb(G, "colA1", [128, 8]); colS1 = sb(G, "colS1", [128, 8])
    colcA1 = sb(G, "colcA1", [128, 8]); colcS1 = sb(G, "colcS1", [128, 8])
    colA2 = sb(G, "colA2", [128, 8]); colS2 = sb(G, "colS2", [128, 8])
    G1 = sb(G, "G1", [128, D]); G2 = sb(G, "G2", [128, D])
    convw = sb(G, "convw", [128, 12, 5]); convb = sb(G, "convb", [128, 12])
    dtb_row = sb(G, "dtb_row", [128, 32]); a_row = sb(G, "a_row", [128, 32]); dsk_row = sb(G, "dsk_row", [128, 16])
    halfneg = sb(G, "halfneg", [128, 1])
    dram_x = Buf(x_d, "x"); dram_out = Buf(out_d, "out")
    ysc = [Buf(ysc_d, "ysc%d" % i) for i in range(NCH)]
    outb = [Buf(out_d, "out%d" % i) for i in range(NCH)]

    with contextlib.ExitStack() as P0:
        s.dma("sp", I("dma_start", out=kc[:], in_=kconst_d[:, :, :]), writes=[kc])
        s.op("dve", I("tensor_copy", out=kcb[:], in_=kc[:]), reads=[kc], writes=[kcb])
        s.op("dve", I("memset", halfneg[:], -0.5), writes=[halfneg])

        def col_load(dst, src_row, n):
            s.dma("sp", I("dma_start", out=dst[:], in_=src_row.rearrange("o (k p) -> p (o k)", p=128),
                                              allow_slow_non_contiguous=True), writes=[dst])

        gpm_c = sb(P0, "gpm_c", [128, 8]); gpl_c = sb(P0, "gpl_c", [128, 8])
        col_load(gpm_c, gpm_d, 8); col_load(gpl_c, gpl_d, 8)
        col_load(convb, convb_d, 12)
        for k in range(5):
            s.dma("sp", I("dma_start", out=convw[:, :, k], in_=convw_d[k:k + 1, :].rearrange("o (j p) -> p (o j)", p=128),
                                                   allow_slow_non_contiguous=True), writes=[convw])
        s.dma("sp", I("dma_start", out=dtb_row[:], in_=dtb_d.partition_broadcast(128) if len(dtb_d.shape) == 1 else dtb_d[0, :].partition_broadcast(128)), writes=[dtb_row])
        alog_row = sb(P0, "alog_row", [128, 32])
        s.dma("sp", I("dma_start", out=alog_row[:], in_=alog_d[0, :].partition_broadcast(128)), writes=[alog_row])
        s.dma("sp", I("dma_start", out=dsk_row[:], in_=dsk_d[0, :].partition_broadcast(128)), writes=[dsk_row])
        s.op("act", I("activation", out=a_row[:], in_=alog_row[:], func=AF.Exp), reads=[alog_row], writes=[a_row])
        s.op("dve", I("tensor_scalar", out=a_row[:], in0=a_row[:], scalar1=-1.0, scalar2=None, op0=ALU.mult), reads=[a_row], writes=[a_row])
        gq_row = sb(P0, "gq_row", [128, D]); gl_row = sb(P0, "gl_row", [128, D])
        s.dma("sp", I("dma_start", out=gq_row[:], in_=gqm_d[0, :].partition_broadcast(128)), writes=[gq_row])
        s.dma("sp", I("dma_start", out=gl_row[:], in_=gql_d[0, :].partition_broadcast(128)), writes=[gl_row])

        craw = sb(P0, "craw", [128, 8, 2]); sc = sb(P0, "sc", [128, 8, 2]); scb = sb(P0, "scb", [128, 8, 128])
        s.dma("sp", I("dma_start", out=craw[:, :, 0], in_=c_d.rearrange("o (k p) -> p (o k)", p=128), allow_slow_non_contiguous=True), writes=[craw])
        s.dma("sp", I("dma_start", out=craw[:, :, 1], in_=cctx_d.rearrange("o (k p) -> p (o k)", p=128), allow_slow_non_contiguous=True), writes=[craw])
        s.op("act", I("activation", out=sc[:], in_=craw[:], func=AF.Silu), reads=[craw], writes=[sc])
        s.op("dve", I("tensor_copy", out=scb[:], in_=sc[:, :, 0:1].to_broadcast([128, 8, 128])), reads=[sc], writes=[scb])
        bada = sb(P0, "bada", [1, 6 * D])
        s.dma("sp", I("dma_start", out=bada[:], in_=bada_d[:, :]), writes=[bada])
        adacol = sb(P0, "adacol", [128, 48, 2])
        growraw = sb(P0, "growraw", [128, 2, D])
        wring = RR(ring(P0, "wada", 4, [128, 8, 512]))
        pcol = RR(ring(P0, "pcol", 2, [128, 2], psum=True))
        prow = RR(ring(P0, "prow", 2, [128, 512], psum=True))
        col_tiles = {0: 0, 1: 4, 2: 8, 3: 12, 6: 24, 7: 28, 8: 32, 9: 36}
        row_tiles = {4: (0, 0), 5: (0, 512), 10: (1, 0), 11: (1, 512)}
        for ct in range(12):
            W = wring.next()
            s.dma("sp", I("dma_start", out=W[:], in_=wada_d[:, ct * 512:(ct + 1) * 512].rearrange("(k p) n -> p k n", p=128)), writes=[W])
            if ct in col_tiles:
                for jj in range(4):
                    pc = pcol.next()
                    for k in range(8):
                        s.op("pe", I("matmul", pc[:, :], lhsT=W[:, k, jj * 128:(jj + 1) * 128], rhs=sc[:, k, :], start=(k == 0), stop=False), inc=False,
                             reads=[W, sc], writes=[pc])
                    c0 = ct * 512 + jj * 128
                    s.op("pe", I("matmul", pc[:, :], lhsT=bada[0:1, c0:c0 + 128], rhs=kc[0:1, 3, 0:2], start=False, stop=True), inc=True,
                         reads=[bada, kc], writes=[pc])
                    bi = col_tiles[ct] + jj
                    s.op("act", I("activation", out=adacol[:, bi, :], in_=pc[:, :], func=AF.Identity), reads=[pc], writes=[adacol])
            else:
                pr = prow.next()
                for k in range(8):
                    s.op("pe", I("matmul", pr[:, :], lhsT=scb[:, k, :], rhs=W[:, k, :], start=(k == 0), stop=False), inc=False,
                         reads=[W, scb], writes=[pr])
                s.op("pe", I("matmul", pr[:, :], lhsT=kc[0:1, 3, :], rhs=bada[0:1, ct * 512:(ct + 1) * 512], start=False, stop=True), inc=True,
                     reads=[bada, kc], writes=[pr])
                gi, off = row_tiles[ct]
                s.op("act", I("activation", out=growraw[:, gi, off:off + 512], in_=pr[:, :], func=AF.Identity), reads=[pr], writes=[growraw])
        s.op("dve", I("scalar_tensor_tensor", out=colA1[:], in0=adacol[:, 8:16, 0], scalar=1.0, in1=gpm_c[:], op0=ALU.add, op1=ALU.mult), reads=[adacol, gpm_c], writes=[colA1])
        s.op("dve", I("scalar_tensor_tensor", out=colcA1[:], in0=adacol[:, 8:16, 1], scalar=1.0, in1=gpm_c[:], op0=ALU.add, op1=ALU.mult), reads=[adacol, gpm_c], writes=[colcA1])
        s.op("dve", I("scalar_tensor_tensor", out=colA2[:], in0=adacol[:, 32:40, 0], scalar=1.0, in1=gpl_c[:], op0=ALU.add, op1=ALU.mult), reads=[adacol, gpl_c], writes=[colA2])
        s.op("dve", I("tensor_copy", out=colS1[:], in_=adacol[:, 0:8, 0]), reads=[adacol], writes=[colS1])
        s.op("dve", I("tensor_copy", out=colcS1[:], in_=adacol[:, 0:8, 1]), reads=[adacol], writes=[colcS1])
        s.op("dve", I("tensor_copy", out=colS2[:], in_=adacol[:, 24:32, 0]), reads=[adacol], writes=[colS2])
        s.op("dve", I("tensor_tensor", out=G1[:], in0=growraw[:, 0, :], in1=gq_row[:], op=ALU.mult), reads=[growraw, gq_row], writes=[G1])
        s.op("dve", I("tensor_tensor", out=G2[:], in0=growraw[:, 1, :], in1=gl_row[:], op=ALU.mult), reads=[growraw, gl_row], writes=[G2])
        if debug and "ada" in dbg_d:
            dump("ada", colA1[:], lambda d: d[:, 0:8], [colA1]); dump("ada", colS1[:], lambda d: d[:, 8:16], [colS1])
            dump("ada", colcA1[:], lambda d: d[:, 16:24], [colcA1]); dump("ada", colA2[:], lambda d: d[:, 24:32], [colA2])
            dump("ada", colS2[:], lambda d: d[:, 32:40], [colS2])
            dump("ada", G1[:, 0:64], lambda d: d[:, 40:104], [G1]); dump("ada", G2[:, 0:64], lambda d: d[:, 104:168], [G2])
        s.barrier()
        s.emit()
    if debug and debug.get("stop") == "p0":
        return

    def prenorm_load(W, src_rows_ap, src_buf):
        xt = W["xt"].next()
        s.dma("sp", I("dma_start", out=xt[:], in_=src_rows_ap), reads=[src_buf], writes=[xt])
        return xt

    def prenorm_front(W, src_rows_ap, src_buf, xt=None):
        if xt is None:
            xt = prenorm_load(W, src_rows_ap, src_buf)
        sq = W["sq"].next(); ss = W["ss"].next(); xn = W["xn"].next()
        s.op("act", I("activation", out=sq[:], in_=xt[:], func=AF.Square, accum_out=ss[:, 0:1]), reads=[xt], writes=[sq, ss])
        s.op("dve", I("tensor_scalar", out=ss[:, 1:2], in0=ss[:, 0:1], scalar1=1.0 / D, scalar2=EPS, op0=ALU.mult, op1=ALU.add), reads=[ss], writes=[ss])
        s.op("pool", I("tensor_tensor", out=ss[:, 2:3], in0=ss[:, 1:2], in1=halfneg[:], op=ALU.pow), reads=[ss, halfneg], writes=[ss])
        s.op("dve", I("tensor_scalar", out=xn[:], in0=xt[:], scalar1=ss[:, 2:3], scalar2=None, op0=ALU.mult), reads=[xt, ss], writes=[xn])
        return xt, xn

    def prenorm_back(W, xn, colA, colS, dst_fn, dst_buf, all_act=False):
        pt = W["pt"].next()
        for k in range(8):
            s.op("pe", I("transpose", out=pt[:, k * 128:(k + 1) * 128], in_=xn[:, k * 128:(k + 1) * 128], identity=ident_b), inc=(k == 7), reads=[xn, kcb], writes=[pt])
        for k in range(8):
            if k % 2 == 0 or all_act:
                s.op("act", I("activation", out=dst_fn(k), in_=pt[:, k * 128:(k + 1) * 128], func=AF.Identity, scale=colA[:, k:k + 1], bias=colS[:, k:k + 1]),
                     reads=[pt, colA, colS], writes=[dst_buf])
            else:
                s.op("dve", I("tensor_scalar", out=dst_fn(k), in0=pt[:, k * 128:(k + 1) * 128], scalar1=colA[:, k:k + 1], scalar2=colS[:, k:k + 1], op0=ALU.mult, op1=ALU.add),
                     reads=[pt, colA, colS], writes=[dst_buf])

    def prenorm_T(W, src_rows_ap, src_buf, colA, colS, dst_fn, dst_buf):
        xt, xn = prenorm_front(W, src_rows_ap, src_buf)
        prenorm_back(W, xn, colA, colS, dst_fn, dst_buf)
        return xt

    def prenorm_work(stack, tag):
        return {
            "xt": RR(ring(stack, tag + "xt", 4, [128, D])),
            "sq": RR(ring(stack, tag + "sq", 1, [128, D], BF16)),
            "ss": RR(ring(stack, tag + "ss", 4, [128, 4])),
            "xn": RR(ring(stack, tag + "xn", 4, [128, D], BF16)),
            "pt": RR(ring(stack, tag + "pt", 2, [128, D], BF16, psum=True)),
        }

    def wload_cast(dst, dst_ap, src_ap):
        s.dma_sw(I("dma_start", out=dst_ap, in_=src_ap), writes=[dst])

    with contextlib.ExitStack() as PS:
        xsT = sb(PS, "xsT", [128, 8, L + 2], BF16)
        BT = sb(PS, "BT", [128, 2, L + 2], BF16)
        CT = sb(PS, "CT", [128, 2, L + 2], BF16)
        dts = sb(PS, "dts", [128, NCH, 32])
        cxsT = sb(PS, "cxsT", [128, 8, CTXL + 2], BF16)
        cBT = sb(PS, "cBT", [128, 2, CTXL + 2], BF16)
        cCT = sb(PS, "cCT", [128, 2, CTXL + 2], BF16)
        cdts = sb(PS, "cdts", [128, 2, 32])

        with contextlib.ExitStack() as P2:
            wx = sb(P2, "wx", [128, 8, 1536], BF16)
            wdt = sb(P2, "wdt", [128, 8, 32], BF16)
            wload_cast(wx, wx[:], win_d[:, 1024:2560].rearrange("(k p) n -> p k n", p=128))
            wload_cast(wdt, wdt[:], win_d[:, 2560:2592].rearrange("(k p) n -> p k n", p=128))
            PW = prenorm_work(P2, "p2")
            hx_ring = RR(ring(P2, "hxT", 2, [128, 8, 512], BF16))
            pmm = RR(ring(P2, "pmm", 3, [128, 512], psum=True))
            pdt = RR(ring(P2, "pdt", 1, [128, 32], psum=True))
            Pr = RR(ring(P2, "Ppre", 3, [128, 516], BF16))
            pcv = RR(ring(P2, "pcv", 2, [128, 512], psum=True))
            carry = sb(P2, "carry", [128, 12, 4], BF16)
            dgw = sb(P2, "dgw", [128, 60, 128], BF16)
            s.op("dve", I("tensor_tensor", out=dgw[:], in0=kcb[:, 0:1, :].to_broadcast([128, 60, 128]),
                          in1=convw[:].rearrange("p j k -> p (j k)").unsqueeze(2).to_broadcast([128, 60, 128]), op=ALU.mult), reads=[kcb, convw], writes=[dgw])
            dtt = RR(ring(P2, "dtt", 2, [128, 32]))

            def conv_chunk(j, P, ncol, dst_ap, dst_buf):
                pc = pcv.next()
                for k in range(5):
                    s.op("pe", I("matmul", pc[:, 0:ncol], lhsT=dgw[:, j * 5 + k, :], rhs=P[:, k:k + ncol], start=(k == 0), stop=(k == 4)), inc=(k == 4), reads=[dgw, P], writes=[pc])
                s.op("act", I("activation", out=dst_ap, in_=pc[:, 0:ncol], func=AF.Silu, bias=convb[:, j:j + 1]), reads=[pc, convb], writes=[dst_buf])

            import os
            SUBCUT = int(os.environ.get("P2SUB", "9"))

            def inproj_seq(src_d, src_buf, ntok, colA, colS, xs_dst, B_dst, C_dst, dt_dst, nj):
                s.op("pool", I("memset", carry[:], 0.0), writes=[carry])
                tile_n = min(512, ntok)
                ntile = ntok // tile_n
                nsub = tile_n // 128

                def dest(j, c0, n):
                    if j < 8:
                        return xs_dst[:, j, c0:c0 + n], xs_dst
                    if j < 10:
                        return B_dst[:, j - 8, c0:c0 + n], B_dst
                    return C_dst[:, j - 10, c0:c0 + n], C_dst

                def tile_loads(i):
                    return [prenorm_load(PW, src_d[i * tile_n + sub * 128:i * tile_n + (sub + 1) * 128, :], src_buf) for sub in range(nsub)]

                nxt = tile_loads(0)
                for i in range(ntile):
                    hx = hx_ring.next()
                    xts = nxt
                    fr = [prenorm_front(PW, None, None, xt=xts[sub]) for sub in range(nsub)]
                    if i + 1 < ntile:
                        nxt = tile_loads(i + 1)
                    for sub in range(nsub):
                        prenorm_back(PW, fr[sub][1], colA, colS, lambda k, sub=sub, hx=hx: hx[:, k, sub * 128:(sub + 1) * 128], hx)
                    for sub in range(nsub if SUBCUT >= 2 else 0):
                        cidx = i * nsub + sub
                        pd = pdt.next(); t1 = dtt.next()
                        for k in range(8):
                            s.op("pe", I("matmul", pd[:, :], lhsT=hx[:, k, sub * 128:(sub + 1) * 128], rhs=wdt[:, k, :], start=(k == 0), stop=(k == 7)), inc=(k == 7),
                                 reads=[hx, wdt], writes=[pd])
                        s.op("dve", I("tensor_tensor", out=t1[:], in0=pd[:, :], in1=dtb_row[:], op=ALU.add), reads=[pd, dtb_row], writes=[t1])
                        s.op("act", I("activation", out=t1[:], in_=t1[:], func=AF.Exp), reads=[t1], writes=[t1])
                        s.op("act", I("activation", out=dt_dst[:, cidx, :], in_=t1[:], func=AF.Ln, bias=1.0), reads=[t1], writes=[dt_dst])
                    pend = None
                    for j in range(nj if SUBCUT >= 3 else 0):
                        pm = pmm.next(); P = Pr.next()
                        for k in range(8):
                            s.op("pe", I("matmul", pm[:, 0:tile_n], lhsT=wx[:, k, j * 128:(j + 1) * 128], rhs=hx[:, k, 0:tile_n], start=(k == 0), stop=(k == 7)), inc=(k == 7),
                                 reads=[hx, wx], writes=[pm])
                        s.op("pool", I("tensor_copy", out=P[:, 0:4], in_=carry[:, j, :]), reads=[carry], writes=[P])
                        s.op("act", I("activation", out=P[:, 4:4 + tile_n], in_=pm[:, 0:tile_n], func=AF.Identity), reads=[pm], writes=[P])
                        s.op("pool", I("tensor_copy", out=carry[:, j, :], in_=P[:, tile_n:tile_n + 4]), reads=[P], writes=[carry])
                        if pend is not None:
                            conv_chunk(*pend)
                        dap, dbuf = dest(j, i * tile_n, tile_n)
                        pend = (j, P, tile_n, dap, dbuf)
                    if pend is not None:
                        conv_chunk(*pend)
                for j in range(nj if SUBCUT >= 5 else 0):
                    P = Pr.next()
                    s.op("pool", I("tensor_copy", out=P[:, 0:4], in_=carry[:, j, :]), reads=[carry], writes=[P])
                    s.op("pool", I("memset", P[:, 4:8], 0.0), writes=[P])
                    dap, dbuf = dest(j, ntok, 2)
                    conv_chunk(j, P, 2, dap, dbuf)

            ctx_buf = Buf(ctx_d, "ctx")
            import os
            cut = int(os.environ.get("P2CUT", "9"))
            if cut >= 1:
                inproj_seq(ctx_d, ctx_buf, CTXL, colcA1, colcS1, cxsT, cBT, cCT, cdts, 12)
            if cut >= 2:
                inproj_seq(x_d, dram_x, L, colA1, colS1, xsT, BT, CT, dts, 12)
            if debug and "xsT" in dbg_d:
                tmpf = sb(P2, "dbgtmp", [128, 512])
                for (nm, srcb, n3) in (("xsT", xsT, 8), ("BT", BT, 2), ("CT", CT, 2)):
                    for j in range(n3):
                        for q in range(1):
                            s.op("dve", I("tensor_copy", out=tmpf[:], in_=srcb[:, j, 2:514]), reads=[srcb], writes=[tmpf])
                            dump(nm, tmpf[:], lambda d, j=j: d[:, j, :], [tmpf])
                dump("dts", dts[:], lambda d: d[:, :, :], [dts])
                dump("cdts", cdts[:], lambda d: d[:, :, :], [cdts])
            s.barrier()
            s.emit()
        if debug and debug.get("stop") == "p2":
            return

        with contextlib.ExitStack() as P3:
            Sst = [sb(P3, "S_f", [128, D]), sb(P3, "S_b", [128, D])]
            Sbf = [sb(P3, "Sbf_f", [128, D], BF16), sb(P3, "Sbf_b", [128, D], BF16)]
            tri_b16 = {0: kcb[:, 1, :], 1: kcb[:, 2, :]}
            tri_f32 = {0: trif, 1: trib}
            Umat = sb(P3, "Umat", [128, 2, 128])
            Dg = sb(P3, "Dg", [128, 16, 128], BF16)
            s.op("dve", I("tensor_tensor", out=Dg[:], in0=kcb[:, 0:1, :].to_broadcast([128, 16, 128]), in1=dsk_row[:].unsqueeze(2).to_broadcast([128, 16, 128]), op=ALU.mult),
                 reads=[kcb, dsk_row], writes=[Dg])
            s.op("dve", I("tensor_scalar", out=Umat[:], in0=kc[:, 1:3, :], scalar1=-1.0, scalar2=1.0, op0=ALU.mult, op1=ALU.add), reads=[kc], writes=[Umat])
            R2 = lambda name, shape, dt=F32, n=2: RR(ring(P3, name, n, shape, dt))
            dtA_r = R2("dtA", [128, 16])
            sm_r = R2("smalls", [128, 5, 16])
            rhs_r = R2("rhs32", [128, 16, 128], F32, 1)
            L_r = R2("Lmat", [128, 16, 128], BF16, 1); M_r = R2("Mmat", [128, 16, 128], BF16)
            cbm_r = R2("cbm", [128, 2, 128], BF16)
            xs_r = R2("xs_sb", [128, 16, 64], BF16); xdt_r = R2("xdt", [128, 16, 64], BF16); xdw_r = R2("xdw", [128, 16, 64], BF16)
            Btok = sb(P3, "Btok", [128, NCH, 256], BF16)
            cBtok = sb(P3, "cBtok", [128, 2, 256], BF16)
            yt_r = R2("ytmp", [128, D], F32, 1); yo_r = R2("yout", [128, D]); yb_r = R2("ybld", [128, D], F32, 1)
            p_small = ps(P3, "p_small", [128, 512])
            p_T = ps(P3, "p_T", [128, 1024], BF16)
            p_seg = RR(ring(P3, "p_seg", 2, [128, 512], psum=True))
            p_yo = [ps(P3, "p_yo0", [128, 512]), ps(P3, "p_yo1", [128, 512])]
            p_st = [ps(P3, "p_st0", [128, 512]), ps(P3, "p_st1", [128, 512])]

            def make_btok(Bsrc, dst, nchunk):
                for c in range(nchunk):
                    c0 = 2 + c * 128
                    for g in range(2):
                        s.op("pe", I("transpose", out=p_T[:, g * 128:(g + 1) * 128], in_=Bsrc[:, g, c0:c0 + 128], identity=ident_b), reads=[Bsrc, kcb], writes=[p_T])
                    s.op("act", I("activation", out=dst[:, c, :], in_=p_T[:, 0:256], func=AF.Identity), reads=[p_T], writes=[dst])

            make_btok(cBT, cBtok, 2)
            make_btok(BT, Btok, NCH)

            def stageA(d, xsrc, Bsrc, Csrc, dtsrc, Btsrc, c, need_y):
                c0 = 2 + c * 128
                ctxd = {}
                dtA = dtA_r.next(); sm = sm_r.next()
                dtd = dtsrc[:, c, d * 16:(d + 1) * 16]
                s.op("dve", I("tensor_tensor", out=dtA[:], in0=dtd, in1=a_row[:, d * 16:(d + 1) * 16], op=ALU.mult), reads=[dtsrc, a_row], writes=[dtA])
                s.op("pe", I("matmul", p_small[:, 0:16], lhsT=tri_f32[d], rhs=dtA[:], start=True, stop=True), inc=True, reads=[kc, dtA], writes=[p_small])
                s.op("pe", I("matmul", p_small[:, 16:32], lhsT=ones, rhs=dtA[:], start=True, stop=True), inc=True, reads=[kc, dtA], writes=[p_small])
                s.op("act", I("activation", out=sm[:, 0, :], in_=p_small[:, 0:16], func=AF.Identity), reads=[p_small], writes=[sm])
                s.op("act", I("activation", out=sm[:, 2, :], in_=p_small[:, 0:16], func=AF.Exp), reads=[p_small], writes=[sm])
                s.op("act", I("activation", out=sm[:, 4, :], in_=p_small[:, 16:32], func=AF.Exp), reads=[p_small], writes=[sm])
                s.op("dve", I("tensor_tensor", out=sm[:, 1, :], in0=p_small[:, 16:32], in1=sm[:, 0, :], op=ALU.subtract), reads=[p_small, sm], writes=[sm])
                s.op("act", I("activation", out=sm[:, 3, :], in_=sm[:, 1, :], func=AF.Exp), reads=[sm], writes=[sm])
                s.op("dve", I("tensor_tensor", out=sm[:, 3, :], in0=sm[:, 3, :], in1=dtd, op=ALU.mult), reads=[sm, dtsrc], writes=[sm])
                for k in range(8):
                    s.op("pe", I("transpose", out=p_T[:, k * 128:(k + 1) * 128], in_=xsrc[:, k, c0:c0 + 128], identity=ident_b), inc=(k == 7), reads=[xsrc, kcb], writes=[p_T])
                xs_sb = xs_r.next(); xdw = xdw_r.next()
                s.op("act", I("activation", out=xs_sb[:].rearrange("p h q -> p (h q)"), in_=p_T[:, :], func=AF.Identity), reads=[p_T], writes=[xs_sb])
                s.op("pool", I("tensor_tensor", out=xdw[:], in0=xs_sb[:], in1=sm[:, 3, :].unsqueeze(2).to_broadcast([128, 16, 64]), op=ALU.mult), reads=[xs_sb, sm], writes=[xdw])
                ctxd.update(sm=sm, xs_sb=xs_sb, xdw=xdw, Btsrc=Btsrc, c=c, c0=c0)
                if need_y:
                    xdt = xdt_r.next(); cbm = cbm_r.next(); rhs = rhs_r.next(); Lm = L_r.next(); Mm = M_r.next()
                    if not hasattr(rhs, "halves"):
                        rhs.halves = [Buf(rhs.ap, rhs.name + "_lo"), Buf(rhs.ap, rhs.name + "_hi")]
                    s.op("dve", I("tensor_tensor", out=xdt[:], in0=xs_sb[:], in1=dtd.unsqueeze(2).to_broadcast([128, 16, 64]), op=ALU.mult), reads=[xs_sb, dtsrc], writes=[xdt])
                    s.op("dve", I("tensor_tensor", out=rhs[:, 0:8, :], in0=tri_f32[d].unsqueeze(1).to_broadcast([128, 8, 128]), in1=dtA[:, 0:8].unsqueeze(2).to_broadcast([128, 8, 128]), op=ALU.mult),
                         reads=[kc, dtA], writes=[rhs.halves[0]])
                    for h in range(8, 16):
                        s.op("act", I("activation", out=rhs[:, h, :], in_=tri_f32[d], func=AF.Identity, scale=dtA[:, h:h + 1]), reads=[kc, dtA], writes=[rhs.halves[1]])
                    for g in range(2):
                        s.op("pe", I("matmul", p_small[:, 256 + g * 128:256 + (g + 1) * 128], lhsT=Bsrc[:, g, c0:c0 + 128], rhs=Csrc[:, g, c0:c0 + 128], start=True, stop=True), inc=True,
                             reads=[Bsrc, Csrc], writes=[p_small])
                    s.op("dve", I("tensor_tensor", out=cbm[:], in0=p_small[:, 256:512].rearrange("p (g t) -> p g t", g=2),
                                  in1=tri_f32[d].unsqueeze(1).to_broadcast([128, 2, 128]), op=ALU.mult), reads=[p_small, kc], writes=[cbm])
                    ctxd.update(rhs=rhs, Lm=Lm, cbm=cbm)
                    ctxd.update(xdt=xdt, Mm=Mm)
                return ctxd

            def stageA2(d, A):
                rhs = A["rhs"]; Lm = A["Lm"]; Mm = A["Mm"]; cbm = A["cbm"]
                for q in range(4):
                    pq = p_seg.next()
                    s.op("pe", I("matmul", pq[:, :], lhsT=Umat[:, d, :], rhs=rhs[:, q * 4:(q + 1) * 4, :].rearrange("p h t -> p (h t)"), start=True, stop=True), inc=True,
                         reads=[Umat, rhs.halves[q // 2]], writes=[pq])
                    s.op("act", I("activation", out=Lm[:, q * 4:(q + 1) * 4, :].rearrange("p h t -> p (h t)"), in_=pq[:, :], func=AF.Exp), reads=[pq], writes=[Lm])
                    gq = q // 2
                    s.op("dve", I("tensor_tensor", out=Mm[:, q * 4:(q + 1) * 4, :], in0=Lm[:, q * 4:(q + 1) * 4, :],
                                  in1=cbm[:, gq:gq + 1, :].to_broadcast([128, 4, 128]), op=ALU.mult), reads=[Lm, cbm], writes=[Mm])

            def stageB(d, Csrc, c, A, need_y, final):
                c0 = A["c0"]; sm = A["sm"]; Btsrc = A["Btsrc"]
                S = Sst[d]; Sb = Sbf[d]
                if need_y:
                    for g in range(2):
                        s.op("pe", I("matmul", p_yo[g][:, :], lhsT=Csrc[:, g, c0:c0 + 128], rhs=Sb[:, g * 512:(g + 1) * 512], start=True, stop=True), inc=True, reads=[Csrc, Sb], writes=[p_yo[g]])
                for g in range(2):
                    s.op("pe", I("matmul", p_st[g][:, :], lhsT=Btsrc[:, c, g * 128:(g + 1) * 128], rhs=A["xdw"][:, g * 8:(g + 1) * 8, :].rearrange("p h q -> p (h q)"), start=True, stop=True), inc=True,
                         reads=[Btsrc, A["xdw"]], writes=[p_st[g]])
                for g in range(2):
                    s.op("pool", I("tensor_tensor", out=S[:, g * 512:(g + 1) * 512].rearrange("p (h q) -> p h q", h=8), in0=S[:, g * 512:(g + 1) * 512].rearrange("p (h q) -> p h q", h=8),
                                   in1=sm[:, 4, g * 8:(g + 1) * 8].unsqueeze(2).to_broadcast([128, 8, 64]), op=ALU.mult), reads=[S, sm], writes=[S])
                yt = yt_r.next() if need_y else None
                if need_y:
                    for g in range(2):
                        s.op("dve", I("tensor_tensor", out=yt[:, g * 512:(g + 1) * 512].rearrange("p (h q) -> p h q", h=8), in0=p_yo[g][:, :].rearrange("p (h q) -> p h q", h=8),
                                      in1=sm[:, 2, g * 8:(g + 1) * 8].unsqueeze(2).to_broadcast([128, 8, 64]), op=ALU.mult), reads=[p_yo[g], sm], writes=[yt])
                for g in range(2):
                    s.op("dve", I("tensor_tensor", out=S[:, g * 512:(g + 1) * 512], in0=p_st[g][:, :], in1=S[:, g * 512:(g + 1) * 512], op=ALU.add), reads=[p_st[g], S], writes=[S])
                s.op("act", I("activation", out=Sb[:], in_=S[:], func=AF.Identity), reads=[S], writes=[Sb])
                if need_y:
                    yo = yo_r.next()
                    for g in range(2):
                        for h in range(8):
                            hh = g * 8 + h
                            s.op("pe", I("matmul", p_yo[g][:, h * 64:(h + 1) * 64], lhsT=A["Mm"][:, hh, :], rhs=A["xdt"][:, hh, :], start=True, stop=(d != 0)), inc=(d != 0 and h == 7),
                                 reads=[A["Mm"], A["xdt"]], writes=[p_yo[g]])
                            if d == 0:
                                s.op("pe", I("matmul", p_yo[g][:, h * 64:(h + 1) * 64], lhsT=Dg[:, hh, :], rhs=A["xs_sb"][:, hh, :], start=False, stop=True), inc=(h == 7),
                                     reads=[Dg, A["xs_sb"]], writes=[p_yo[g]])
                        s.op("dve", I("tensor_tensor", out=yo[:, g * 512:(g + 1) * 512], in0=p_yo[g][:, :], in1=yt[:, g * 512:(g + 1) * 512], op=ALU.add), reads=[p_yo[g], yt], writes=[yo])
                    final(c, yo, A)

            def ssd_pass(d, xsrc, Bsrc, Csrc, dtsrc, Btsrc, nchunk, need_y, final):
                order = list(range(nchunk)) if d == 0 else list(range(nchunk - 1, -1, -1))
                prev = None
                for step in range(nchunk + 1):
                    cur = None
                    if step < nchunk:
                        cur = (order[step], stageA(d, xsrc, Bsrc, Csrc, dtsrc, Btsrc, order[step], need_y))
                    if prev is not None:
                        stageB(d, Csrc, prev[0], prev[1], need_y, final)
                    if cur is not None and need_y:
                        stageA2(d, cur[1])
                    prev = cur

            for d in range(2):
                s.op("pool", I("memset", Sst[d][:], 0.0), writes=[Sst[d]])
                s.op("pool", I("memset", Sbf[d][:], 0.0), writes=[Sbf[d]])
                ssd_pass(d, cxsT, cBT, cCT, cdts, cBtok, 2, False, None)
            if debug and "S0" in dbg_d:
                dump("S0", Sst[0][:], lambda dd: dd[:, 0, :], [Sst[0]]); dump("S0", Sst[1][:], lambda dd: dd[:, 1, :], [Sst[1]])

            def fin_b(c, yo, A):
                s.dma("sp", I("dma_start", out=ysc_d[c * 128:(c + 1) * 128, :], in_=yo[:]), reads=[yo], writes=[ysc[c]])

            ssd_pass(1, xsT, BT, CT, dts, Btok, NCH, True, fin_b)

            def fin_f(c, yo, A):
                yb = yb_r.next()
                s.dma("sp", I("dma_start", out=yb[:], in_=ysc_d[c * 128:(c + 1) * 128, :]), reads=[ysc[c]], writes=[yb])
                s.op("pool", I("tensor_tensor", out=yb[:], in0=yb[:], in1=yo[:], op=ALU.add), reads=[yb, yo], writes=[yb])
                s.dma("sp", I("dma_start", out=ysc_d[c * 128:(c + 1) * 128, :], in_=yb[:]), reads=[yb], writes=[ysc[c]])

            ssd_pass(0, xsT, BT, CT, dts, Btok, NCH, True, fin_f)
            s.barrier()
            s.emit()

    if debug and debug.get("stop") == "ssd":
        for c in range(NCH):
            s.dma("sp", I("dma_start", out=out_d[c * 128:(c + 1) * 128, :], in_=ysc_d[c * 128:(c + 1) * 128, :]), reads=[ysc[c]], writes=[outb[c]])
        s.barrier()
        s.emit()
        return

    with contextlib.ExitStack() as P4:
        uT = sb(P4, "pmT_all", [128, 8, L], BF16)
        uTh = [Buf(uT.ap, "pmT%d" % j) for j in range(8)]
        pscol = sb(P4, "pscol", [128, 8])
        s.dma("sp", I("dma_start", out=pscol[:], in_=pscale_d.rearrange("o (k p) -> p (o k)", p=128), allow_slow_non_contiguous=True), writes=[pscol])
        with contextlib.ExitStack() as P4ab:
            utok = sb(P4ab, "utok", [128, NCH, 1024], BF16)
            utokh = [Buf(utok.ap, "utok%d" % c) for c in range(NCH)]
            with contextlib.ExitStack() as P4a:
                wu = sb(P4a, "wu", [128, 8, 1024], BF16)
                wload_cast(wu, wu[:], win_d[:, 2592:3616].rearrange("(k p) n -> p k n", p=128))
                PW = prenorm_work(P4a, "p4a")
                hx_ring = RR(ring(P4a, "hx4", 2, [128, 8, 512], BF16))
                pu = RR(ring(P4a, "pu", 4, [128, 512], psum=True))

                def tile_loads4(i):
                    return [prenorm_load(PW, x_d[i * 512 + sub * 128:i * 512 + (sub + 1) * 128, :], dram_x) for sub in range(4)]

                nxt = tile_loads4(0)
                for i in range(8):
                    hx = hx_ring.next()
                    xts = nxt
                    fr = [prenorm_front(PW, None, None, xt=xts[sub]) for sub in range(4)]
                    if i + 1 < 8:
                        nxt = tile_loads4(i + 1)
                    for sub in range(4):
                        prenorm_back(PW, fr[sub][1], colA1, colS1, lambda k, sub=sub, hx=hx: hx[:, k, sub * 128:(sub + 1) * 128], hx)
                    for sub in range(4):
                        c = i * 4 + sub
                        for hf in range(2):
                            pq = pu.next()
                            for k in range(8):
                                s.op("pe", I("matmul", pq[:, :], lhsT=hx[:, k, sub * 128:(sub + 1) * 128], rhs=wu[:, k, hf * 512:(hf + 1) * 512], start=(k == 0), stop=(k == 7)), inc=(k == 7), reads=[wu, hx], writes=[pq])
                            if hf == 0:
                                s.op("act", I("activation", out=utok[:, c, 0:512], in_=pq[:, :], func=AF.Identity), reads=[pq], writes=[utokh[c]])
                            else:
                                s.op("dve", I("tensor_copy", out=utok[:, c, 512:1024], in_=pq[:, :]), reads=[pq], writes=[utokh[c]])
                s.barrier()
                s.emit()
            with contextlib.ExitStack() as P4b:
                kinv = sb(P4b, "kinv_sb", [128, 4, 64])
                s.dma("sp", I("dma_start", out=kinv[:].rearrange("p w j -> p (w j)"), in_=kinv_d[0, :].partition_broadcast(128)), writes=[kinv])
                kpb = sb(P4b, "kpb", [128, NPM, 128], BF16)
                wload_cast(kpb, kpb[:], kpool_d[:, :, :])
                pw = sb(P4b, "pw", [128, 4, 2, 256], BF16)
                wload_cast(pw, pw[:], poolw_d.rearrange("g (ci p) n -> p g ci n", p=128))
                imap = sb(P4b, "imap", [128, 64, 64])
                dT = [sb(P4b, "dT0", [128, L], BF16), sb(P4b, "dT1", [128, L], BF16)]
                tmp_r = RR(ring(P4b, "ptmp", 2, [128, 512]))
                p_S = RR(ring(P4b, "p_S", 2, [128, 512], psum=True))
                p_U = RR(ring(P4b, "p_U", 2, [128, 512], BF16, psum=True))
                ppm = RR(ring(P4b, "ppm", 4, [128, 512], psum=True))
                for g in range(4):
                    s.op("dve", I("tensor_tensor", out=imap[:], in0=kinv[:, g, :].unsqueeze(2).to_broadcast([128, 64, 64]), in1=kinv[:, g, :].unsqueeze(1).to_broadcast([128, 64, 64]), op=ALU.mult),
                         reads=[kinv], writes=[imap])
                    deltas = sorted(d_ for (gi, d_) in _PIDX if gi == g)
                    for jj in range(2):
                        j = 2 * g + jj
                        for q in range(8):
                            pS = p_S.next(); pU = p_U.next(); tmp = tmp_r.next()
                            for cc in range(4):
                                cd = 4 * q + cc
                                valid = [d_ for d_ in deltas if 0 <= cd + d_ < NCH]
                                for n_, d_ in enumerate(valid):
                                    s.op("pe", I("matmul", pS[:, cc * 128:(cc + 1) * 128], lhsT=utok[:, cd + d_, j * 128:(j + 1) * 128], rhs=kpb[:, _PIDX[(g, d_)], :],
                                                 start=(n_ == 0), stop=(n_ == len(valid) - 1)), inc=(n_ == len(valid) - 1), reads=[utokh[cd + d_], kpb], writes=[pS])
                                s.op("pe", I("transpose", out=pU[:, cc * 128:(cc + 1) * 128], in_=utok[:, cd, j * 128:(j + 1) * 128], identity=ident_b), reads=[utokh[cd], kcb], writes=[pU])
                            s.op("dve", I("tensor_tensor", out=tmp[:], in0=pS[:, :], in1=imap[:, 8 * q:8 * q + 8, :].rearrange("p r q -> p (r q)"), op=ALU.mult), reads=[pS, imap], writes=[tmp])
                            s.op("dve", I("tensor_tensor", out=dT[jj][:, q * 512:(q + 1) * 512], in0=tmp[:], in1=pU[:, :], op=ALU.subtract), reads=[tmp, pU], writes=[dT[jj]])
                    for i in range(8):
                        pcs = [ppm.next(), ppm.next()]
                        for co in range(2):
                            for ci in range(2):
                                s.op("pe", I("matmul", pcs[co][:, :], lhsT=pw[:, g, ci, co * 128:(co + 1) * 128], rhs=dT[ci][:, i * 512:(i + 1) * 512], start=(ci == 0), stop=(ci == 1)), inc=(ci == 1),
                                     reads=[pw, dT[ci]], writes=[pcs[co]])
                        for co in range(2):
                            s.op("act", I("activation", out=uT[:, 2 * g + co, i * 512:(i + 1) * 512], in_=pcs[co][:, :], func=AF.Identity, scale=pscol[:, 2 * g + co:2 * g + co + 1]),
                                 reads=[pcs[co], pscol], writes=[uTh[2 * g + co]])
                s.barrier()
                s.emit()
        with contextlib.ExitStack() as P4c:
            wz = sb(P4c, "wz", [128, 8, 1024], BF16)
            wo = sb(P4c, "wo", [128, 16, 1024], BF16)
            sncol = sb(P4c, "sncol", [128, 8])
            s.dma("sp", I("dma_start", out=sncol[:], in_=snw_d.rearrange("o (k p) -> p (o k)", p=128), allow_slow_non_contiguous=True), writes=[sncol])
            wload_cast(wz, wz[:], win_d[:, 0:1024].rearrange("(k p) n -> p k n", p=128))
            wload_cast(wo, wo[:], wout_d.rearrange("(k p) n -> p k n", p=128))
            for k in range(8):
                s.op("dve" if k % 2 else "pool", I("tensor_scalar", out=wo[:, k, :], in0=wo[:, k, :], scalar1=sncol[:, k:k + 1], scalar2=None, op0=ALU.mult), reads=[wo, sncol], writes=[wo])
            PW = {
                "xt": RR(ring(P4c, "p4cxt", 7, [128, D])),
                "sq": RR(ring(P4c, "p4csq", 1, [128, D], BF16)),
                "ss": RR(ring(P4c, "p4css", 3, [128, 4])),
                "xn": RR(ring(P4c, "p4cxn", 2, [128, D], BF16)),
                "pt": RR(ring(P4c, "p4cpt", 1, [128, D], BF16, psum=True)),
            }
            hxc_r = RR(ring(P4c, "hxc", 3, [128, 8, 128], BF16))
            sz_r = RR(ring(P4c, "sz", 1, [128, D])); yt_r = RR(ring(P4c, "yt4", 4, [128, D]))
            ygn_r = RR(ring(P4c, "ygn", 2, [128, D], BF16)); ygT_r = RR(ring(P4c, "ygT", 2, [128, 8, 128], BF16))
            st_r = RR(ring(P4c, "st4", 3, [128, 16])); x1_r = RR(ring(P4c, "x1", 3, [128, D]))
            sq4 = sb(P4c, "sq4", [128, 512], BF16)
            p_z = RR(ring(P4c, "p_z", 2, [128, 512], psum=True))
            p_T4 = ps(P4c, "p_T4", [128, D], BF16)
            p_mix = RR(ring(P4c, "p_mix", 4, [128, 512], psum=True))
            ST = {}

            def FL(c):
                xt = prenorm_load(PW, x_d[c * 128:(c + 1) * 128, :], dram_x)
                yt = yt_r.next()
                s.dma("sp", I("dma_start", out=yt[:], in_=ysc_d[c * 128:(c + 1) * 128, :]), reads=[ysc[c]], writes=[yt])
                ST[c] = dict(xt=xt, yt=yt)

            def F(c):
                xt, xn = prenorm_front(PW, None, None, xt=ST[c]["xt"])
                ST[c].update(xn=xn, st=st_r.next())

            def PT(c):
                hxc = hxc_r.next(); ST[c]["hxc"] = hxc
                prenorm_back(PW, ST[c]["xn"], colA1, colS1, lambda k, hxc=hxc: hxc[:, k, :], hxc, all_act=True)

            def YT(c):
                ygn = ST[c]["ygn"]; ygT = ygT_r.next(); ST[c]["ygT"] = ygT
                for k in range(8):
                    s.op("pe", I("transpose", out=p_T4[:, k * 128:(k + 1) * 128], in_=ygn[:, k * 128:(k + 1) * 128], identity=ident_b), inc=(k == 7), reads=[ygn, kcb], writes=[p_T4])
                s.op("act", I("activation", out=ygT[:].rearrange("p k t -> p (k t)"), in_=p_T4[:, :], func=AF.Identity), reads=[p_T4], writes=[ygT])

            def Z(c):
                hxc = ST[c]["hxc"]; yt = ST[c]["yt"]; st = ST[c]["st"]
                sz = sz_r.next(); ygn = ygn_r.next(); ST[c]["ygn"] = ygn
                for hf in range(2):
                    pz = p_z.next()
                    for k in range(8):
                        s.op("pe", I("matmul", pz[:, :], lhsT=hxc[:, k, :], rhs=wz[:, k, hf * 512:(hf + 1) * 512], start=(k == 0), stop=(k == 7)), inc=(k == 7), reads=[hxc, wz], writes=[pz])
                    s.op("act", I("activation", out=sz[:, hf * 512:(hf + 1) * 512], in_=pz[:, :], func=AF.Silu), reads=[pz], writes=[sz])
                s.op("dve", I("tensor_tensor", out=yt[:], in0=yt[:], in1=sz[:], op=ALU.mult), reads=[yt, sz], writes=[yt])
                for gg in range(2):
                    s.op("act", I("activation", out=sq4[:], in_=yt[:, gg * 512:(gg + 1) * 512], func=AF.Square, accum_out=st[:, gg:gg + 1]), reads=[yt], writes=[sq4, st])
                s.op("dve", I("tensor_scalar", out=st[:, 2:4], in0=st[:, 0:2], scalar1=1.0 / 512, scalar2=EPS, op0=ALU.mult, op1=ALU.add), reads=[st], writes=[st])
                s.op("pool", I("tensor_tensor", out=st[:, 4:6], in0=st[:, 2:4], in1=halfneg[:, 0:1].to_broadcast([128, 2]), op=ALU.pow), reads=[st, halfneg], writes=[st])
                for gg in range(2):
                    s.op("dve", I("tensor_scalar", out=ygn[:, gg * 512:(gg + 1) * 512], in0=yt[:, gg * 512:(gg + 1) * 512], scalar1=st[:, 4 + gg:5 + gg], scalar2=None, op0=ALU.mult),
                         reads=[yt, st], writes=[ygn])

            def OP(c):
                ygT = ST[c]["ygT"]; st = ST[c]["st"]
                pms = [p_mix.next(), p_mix.next()]; ST[c]["pms"] = pms
                for hf in range(2):
                    for k in range(16):
                        lhs = ygT[:, k, :] if k < 8 else uT[:, k - 8, c * 128:(c + 1) * 128]
                        rd = [ygT, wo] if k < 8 else [uTh[k - 8], wo]
                        s.op("pe", I("matmul", pms[hf][:, :], lhsT=lhs, rhs=wo[:, k, hf * 512:(hf + 1) * 512], start=(k == 0), stop=(k == 15)), inc=(k == 15), reads=rd, writes=[pms[hf]])
                    s.op("act", I("activation", out=sq4[:], in_=pms[hf][:, :], func=AF.Square, accum_out=st[:, 6 + hf:7 + hf]), reads=[pms[hf]], writes=[sq4, st])
                s.op("dve", I("tensor_tensor", out=st[:, 8:9], in0=st[:, 6:7], in1=st[:, 7:8], op=ALU.add), reads=[st], writes=[st])
                s.op("dve", I("tensor_scalar", out=st[:, 9:10], in0=st[:, 8:9], scalar1=1.0 / D, scalar2=EPS, op0=ALU.mult, op1=ALU.add), reads=[st], writes=[st])
                s.op("pool", I("tensor_tensor", out=st[:, 10:11], in0=st[:, 9:10], in1=halfneg[:], op=ALU.pow), reads=[st, halfneg], writes=[st])

            def FIN(c):
                pms = ST[c]["pms"]; st = ST[c]["st"]; xt = ST[c]["xt"]; x1 = x1_r.next()
                for hf in range(2):
                    s.op("dve", I("scalar_tensor_tensor", out=x1[:, hf * 512:(hf + 1) * 512], in0=pms[hf][:, :], scalar=st[:, 10:11], in1=G1[:, hf * 512:(hf + 1) * 512], op0=ALU.mult, op1=ALU.mult),
                         reads=[pms[hf], st, G1], writes=[x1])
                s.op("dve", I("tensor_tensor", out=x1[:], in0=x1[:], in1=xt[:], op=ALU.add), reads=[x1, xt], writes=[x1])
                PEND.append((c, x1))
                del ST[c]

            PEND = []

            def flush_stores():
                while PEND:
                    c, x1 = PEND.pop(0)
                    s.dma("sp", I("dma_start", out=out_d[c * 128:(c + 1) * 128, :], in_=x1[:]), reads=[x1], writes=[outb[c]])

            FL(0); FL(1); FL(2); F(0); F(1); PT(0)
            for t in range(NCH + 2):
                if t + 3 < NCH:
                    FL(t + 3)
                flush_stores()
                if t + 2 < NCH:
                    F(t + 2)
                if t + 1 < NCH:
                    PT(t + 1)
                if 0 <= t - 1 < NCH:
                    YT(t - 1)
                if t < NCH:
                    Z(t)
                if 0 <= t - 1 < NCH:
                    OP(t - 1)
                if 0 <= t - 2 < NCH:
                    FIN(t - 2)
            flush_stores()
            s.barrier()
            s.emit()
    if debug and debug.get("stop") == "p4":
        return

    with contextlib.ExitStack() as P5:
        w1 = sb(P5, "w1", [128, 8, DFF], BF16)
        w2 = sb(P5, "w2", [128, 32, D], BF16)
        for hh in range(2):
            wload_cast(w1, w1[:, :, hh * 2048:(hh + 1) * 2048], w1_d[:, hh * 2048:(hh + 1) * 2048].rearrange("(k p) n -> p k n", p=128))
        wload_cast(w2, w2[:], w2_d.rearrange("(f p) n -> p f n", p=128))
        PW = {
            "xt": RR(ring(P5, "p5xt", 4, [128, D])),
            "sq": RR(ring(P5, "p5sq", 1, [128, D], BF16)),
            "ss": RR(ring(P5, "p5ss", 4, [128, 4])),
            "xn": RR(ring(P5, "p5xn", 2, [128, D], BF16)),
            "pt": RR(ring(P5, "p5pt", 2, [128, D], BF16, psum=True)),
        }
        hm_r = RR(ring(P5, "hm", 2, [128, 8, 256], BF16))
        hT = sb(P5, "hT", [128, 32, 256], BF16)
        r_r = RR(ring(P5, "relu", 3, [128, 256], BF16))
        o_r = RR(ring(P5, "o5", 4, [128, D]))
        st_r = RR(ring(P5, "st5", 2, [128, 8]))
        sq5 = sb(P5, "sq5", [128, 512], BF16)
        p_h = RR(ring(P5, "p_h", 3, [128, 256], psum=True))
        p_o = RR(ring(P5, "p_o", 3, [128, 512], psum=True))
        NT = L // 256
        MS = {}

        def front5(i):
            fr = []
            for sub in range(2):
                c = i * 2 + sub
                fr.append(prenorm_front(PW, out_d[c * 128:(c + 1) * 128, :], outb[c]))
            MS[i] = dict(fr=fr)

        def back5(i):
            hm = hm_r.next(); MS[i]["hm"] = hm
            for sub in range(2):
                prenorm_back(PW, MS[i]["fr"][sub][1], colA2, colS2, lambda k, sub=sub, hm=hm: hm[:, k, sub * 128:(sub + 1) * 128], hm)

        def mlp1(i):
            hm = MS[i]["hm"]
            for f in range(32):
                if f == 8 and i + 1 < NT:
                    front5(i + 1)
                ph = p_h.next(); rr = r_r.next()
                for k in range(8):
                    s.op("pe", I("matmul", ph[:, :], lhsT=w1[:, k, f * 128:(f + 1) * 128], rhs=hm[:, k, :], start=(k == 0), stop=(k == 7)), inc=(k == 7), reads=[w1, hm], writes=[ph])
                s.op("act", I("activation", out=rr[:], in_=ph[:, :], func=AF.Relu), reads=[ph], writes=[rr])
                s.op("pool", I("tensor_tensor", out=hT[:, f, :], in0=rr[:], in1=rr[:], op=ALU.mult), reads=[rr], writes=[hT])

        def mlp2(i):
            for sub in range(2):
                c = i * 2 + sub
                o = o_r.next(); st = st_r.next(); xt = MS[i]["fr"][sub][0]
                pos = [p_o.next(), p_o.next()]
                for hf in range(2):
                    for f in range(32):
                        s.op("pe", I("matmul", pos[hf][:, :], lhsT=hT[:, f, sub * 128:(sub + 1) * 128], rhs=w2[:, f, hf * 512:(hf + 1) * 512], start=(f == 0), stop=(f == 31)), inc=(f == 31), reads=[hT, w2], writes=[pos[hf]])
                    s.op("act", I("activation", out=sq5[:], in_=pos[hf][:, :], func=AF.Square, accum_out=st[:, hf:hf + 1]), reads=[pos[hf]], writes=[sq5, st])
                s.op("dve", I("tensor_tensor", out=st[:, 2:3], in0=st[:, 0:1], in1=st[:, 1:2], op=ALU.add), reads=[st], writes=[st])
                s.op("dve", I("tensor_scalar", out=st[:, 3:4], in0=st[:, 2:3], scalar1=1.0 / D, scalar2=EPS, op0=ALU.mult, op1=ALU.add), reads=[st], writes=[st])
                s.op("pool", I("tensor_tensor", out=st[:, 4:5], in0=st[:, 3:4], in1=halfneg[:], op=ALU.pow), reads=[st, halfneg], writes=[st])
                for hf in range(2):
                    s.op("dve", I("scalar_tensor_tensor", out=o[:, hf * 512:(hf + 1) * 512], in0=pos[hf][:, :], scalar=st[:, 4:5], in1=G2[:, hf * 512:(hf + 1) * 512], op0=ALU.mult, op1=ALU.mult),
                         reads=[pos[hf], st, G2], writes=[o])
                s.op("pool", I("tensor_tensor", out=o[:], in0=o[:], in1=xt[:], op=ALU.add), reads=[o, xt], writes=[o])
                PEND5.append((c, o))
            del MS[i]

        PEND5 = []

        def flush5():
            while PEND5:
                c, o = PEND5.pop(0)
                s.dma("sp", I("dma_start", out=out_d[c * 128:(c + 1) * 128, :], in_=o[:]), reads=[o], writes=[outb[c]])

        front5(0); back5(0)
        for i in range(NT):
            flush5()
            mlp1(i)
            if i + 1 < NT:
                back5(i + 1)
            mlp2(i)
        flush5()
        s.barrier()
        s.emit()


def pool_mats():
    mats = []; index = {}
    a = np.repeat(np.arange(2), 64); j = np.tile(np.arange(64), 2)
    for gi, w in enumerate((2, 4, 8, 16)):
        h = w // 2
        for delta in range(-5, 6):
            rd = 2 * delta + a[:, None] - a[None, :]
            cd = j[:, None] - j[None, :]
            m = ((rd >= -h) & (rd <= h - 1) & (cd >= -h) & (cd <= h - 1)).astype(np.float32)
            if m.any():
                index[(gi, delta)] = len(mats); mats.append(m)
    return np.stack(mats, axis=1), index


_PM, _PIDX = pool_mats()
NPM = _PM.shape[1]


def make_consts():
    k = np.arange(128)
    kc = np.zeros((128, 5, 128), np.float32)
    kc[:, 0, :] = np.eye(128, dtype=np.float32)
    kc[:, 1, :] = (k[:, None] <= k[None, :]).astype(np.float32)
    kc[:, 2, :] = (k[:, None] >= k[None, :]).astype(np.float32)
    kc[:, 3, :] = 1.0
    inv = np.zeros((4, 64), np.float32)
    t = np.arange(64)
    for gi, w in enumerate((2, 4, 8, 16)):
        lo = np.clip(t - w // 2, 0, 64); hi = np.clip(t + w // 2, 0, 64)
        inv[gi] = 1.0 / (hi - lo)
    return kc, inv.reshape(1, 256)


def core_inputs(inputs, b):
    f = lambda a: np.ascontiguousarray(np.asarray(a, dtype=np.float32))
    kc, kinv = make_consts()
    m = {
        "x": f(inputs["x"][b]), "ctx": f(inputs["ctx"][b]), "c": f(inputs["c"][b:b + 1]),
        "c_ctx": f(np.asarray(inputs["c_ctx"]).reshape(1, D)),
        "w_ada": f(inputs["w_ada"][0]), "b_ada": f(inputs["b_ada"][0:1]),
        "pre_mix_g": f(inputs["pre_mix_g"][0:1]), "post_mix_g": f(inputs["post_mix_g"][0:1]),
        "pre_mlp_g": f(inputs["pre_mlp_g"][0:1]), "post_mlp_g": f(inputs["post_mlp_g"][0:1]),
        "w_in": f(inputs["w_in"][0]), "conv_w": f(inputs["conv_w"][0]), "conv_b": f(inputs["conv_b"][0:1]),
        "dt_bias": f(np.asarray(inputs["dt_bias"][0]).reshape(1, 32)), "a_log": f(np.asarray(inputs["a_log"][0]).reshape(1, 32)),
        "d_skip": f(inputs["d_skip"][0:1]), "ssm_norm_w": f(inputs["ssm_norm_w"][0:1]),
        "pool_w": f(inputs["pool_w"][0]), "pool_scale": f(inputs["pool_scale"][0:1]),
        "w_out": f(inputs["w_out"][0]), "w_mlp1": f(inputs["w_mlp1"][0]), "w_mlp2": f(inputs["w_mlp2"][0]),
        "kconst": kc, "kinv": kinv, "kpool": np.ascontiguousarray(_PM),
    }
    return m


def kernel(**inputs):
    nc = build_program()
    nb = inputs["x"].shape[0]
    in_maps = [core_inputs(inputs, b) for b in range(nb)]
    res = run_bass_kernel_spmd(nc, in_maps, core_ids=list(range(nb)))
    out = np.stack([np.asarray(r["out"]) for r in res.results], axis=0)
    return out.astype(np.float32)
```

```python
import contextlib
import numpy as np
import concourse.bass as bass
import concourse.mybir as mybir
from concourse.bass_utils import run_bass_kernel_spmd

F32 = mybir.dt.float32
BF16 = mybir.dt.bfloat16
AF = mybir.ActivationFunctionType
ALU = mybir.AluOpType

L = 4096
D = 1024
KD = 8
NCH = 32
CTXL = 256
DFF = 4096
EPS = 1e-6
ENGS = ("pe", "act", "dve", "pool", "sp")
N_DMA_SEMS = 40
N_SW_SEMS = 14


class Buf:
    def __init__(self, ap=None, name=""):
        self.ap = ap
        self.name = name
        self.w = {}
        self.r = {}
        self.strict = "sq" in name

    def __getitem__(self, k):
        return self.ap[k]


def I(name, *args, **kw):
    return lambda e: getattr(e, name)(*args, **kw)


class Sched:
    def __init__(self, nc, stack):
        self.nc = nc
        self.q = {e: [] for e in ENGS}
        self.cnt = {e: 0 for e in ENGS}
        self.known = {e: {} for e in ENGS}
        self.dma_val = [0] * N_DMA_SEMS
        self.dma_rr = 0
        self.csem = {e: stack.enter_context(nc.semaphore("c_" + e)) for e in ENGS}
        self.dsem = [stack.enter_context(nc.semaphore("d_%d" % i)) for i in range(N_DMA_SEMS)]
        self.swsem = [stack.enter_context(nc.semaphore("w_%d" % i)) for i in range(N_SW_SEMS)]
        self.sw_used = 0
        self.n_ins = 0

    def _need(self, eng, key, val, waits):
        if key == eng and eng == "pe":
            return
        if self.known[eng].get(key, 0) >= val:
            return
        self.known[eng][key] = val
        waits[key] = max(waits.get(key, 0), val)

    def _deps(self, eng, reads, writes):
        waits = {}
        for t in reads:
            for k, v in t.w.items():
                self._need(eng, k, v, waits)
        for t in writes:
            for k, v in t.w.items():
                if k != eng or eng == "pool" or t.strict:
                    self._need(eng, k, v, waits)
            for k, v in t.r.items():
                if k != eng or eng == "pool" or t.strict:
                    self._need(eng, k, v, waits)
        return waits

    def _commit(self, key, val, reads, writes):
        for t in reads:
            t.r[key] = max(t.r.get(key, 0), val)
        for t in writes:
            t.w[key] = max(t.w.get(key, 0), val)
            t.r = {}

    def op(self, eng, fn, reads=(), writes=(), inc=True):
        waits = self._deps(eng, reads, writes)
        if inc:
            self.cnt[eng] += 1
            self.q[eng].append((waits, fn, ("c", eng)))
            self._commit(eng, self.cnt[eng], reads, writes)
        else:
            assert eng == "pe"
            self.q[eng].append((waits, fn, ("n", eng)))
            self._commit(eng, self.cnt[eng] + 1, reads, writes)

    def dma(self, eng, fn, reads=(), writes=()):
        waits = self._deps(eng, reads, writes)
        i = self.dma_rr
        self.dma_rr = (self.dma_rr + 1) % N_DMA_SEMS
        if self.dma_val[i] > 0:
            self._need(eng, i, self.dma_val[i], waits)
        self.dma_val[i] += 16
        self.q[eng].append((waits, fn, ("d", i)))
        self._commit(i, self.dma_val[i], reads, writes)

    def dma_sw(self, fn, reads=(), writes=()):
        eng = "pool"
        waits = self._deps(eng, reads, writes)
        key = "sw%d" % self.sw_used
        self.sw_used += 1
        assert self.sw_used <= N_SW_SEMS
        self.q[eng].append((waits, fn, ("w", key)))
        self._commit(key, 16, reads, writes)

    def _sem(self, key):
        if isinstance(key, int):
            return self.dsem[key]
        if key.startswith("sw"):
            return self.swsem[int(key[2:])]
        return self.csem[key]

    def barrier(self):
        for eng in ENGS:
            waits = {}
            for e2 in ENGS:
                if e2 != eng and self.cnt[e2] > 0:
                    self._need(eng, e2, self.cnt[e2], waits)
            for i in range(N_DMA_SEMS):
                if self.dma_val[i] > 0:
                    self._need(eng, i, self.dma_val[i], waits)
            for i in range(self.sw_used):
                self._need(eng, "sw%d" % i, 16, waits)
            self.q[eng].append((waits, None, None))

    def emit(self):
        nc = self.nc
        with nc.Block() as block:
            def run(name, e):
                for waits, fn, inc in self.q[name]:
                    for key, val in waits.items():
                        e.wait_ge(self._sem(key), val)
                    if fn is None:
                        continue
                    ins = fn(e)
                    self.n_ins += 1
                    if inc[0] == "n":
                        pass
                    elif inc[0] == "c":
                        ins.then_inc(self.csem[inc[1]], 1)
                    elif inc[0] == "w":
                        ins.then_inc(self._sem(inc[1]), 16)
                    else:
                        ins.then_inc(self.dsem[inc[1]], 16)

            @block.tensor
            def _(e):
                run("pe", e)

            @block.scalar
            def _(e):
                run("act", e)

            @block.vector
            def _(e):
                run("dve", e)

            @block.gpsimd
            def _(e):
                run("pool", e)

            @block.sync
            def _(e):
                run("sp", e)
        self.q = {e: [] for e in ENGS}


def build_program(debug=None):
    nc = bass.Bass("TRN2", target_bir_lowering=False)
    ES = contextlib.ExitStack()
    with ES:
        _build(nc, ES, debug)
    return nc


def _build(nc, ES, debug):
    def din(name, shape):
        return nc.dram_tensor(name, list(shape), F32, kind="ExternalInput").ap()

    x_d = din("x", [L, D]); ctx_d = din("ctx", [CTXL, D])
    c_d = din("c", [1, D]); cctx_d = din("c_ctx", [1, D])
    wada_d = din("w_ada", [D, 6 * D]); bada_d = din("b_ada", [1, 6 * D])
    gpm_d = din("pre_mix_g", [1, D]); gqm_d = din("post_mix_g", [1, D])
    gpl_d = din("pre_mlp_g", [1, D]); gql_d = din("post_mlp_g", [1, D])
    win_d = din("w_in", [D, 3616]); convw_d = din("conv_w", [5, 1536]); convb_d = din("conv_b", [1, 1536])
    dtb_d = din("dt_bias", [1, 32]); alog_d = din("a_log", [1, 32]); dsk_d = din("d_skip", [1, 16])
    snw_d = din("ssm_norm_w", [1, D]); poolw_d = din("pool_w", [4, 256, 256]); pscale_d = din("pool_scale", [1, D])
    wout_d = din("w_out", [2 * D, D]); w1_d = din("w_mlp1", [D, DFF]); w2_d = din("w_mlp2", [DFF, D])
    kconst_d = din("kconst", [128, 5, 128])
    kinv_d = din("kinv", [1, 4 * 64])
    kpool_d = din("kpool", [128, NPM, 128])
    out_d = nc.dram_tensor("out", [L, D], F32, kind="ExternalOutput").ap()
    ysc_d = nc.dram_tensor("y_scratch", [L, D], F32, kind="Internal").ap()
    dbg_d = {}
    if debug:
        for name, shape in debug.get("shapes", {}).items():
            dbg_d[name] = nc.dram_tensor("dbg_" + name, list(shape), F32, kind="ExternalOutput").ap()

    s = Sched(nc, ES)

    def sb(stack, name, shape, dt=F32):
        return Buf(stack.enter_context(nc.sbuf_tensor(name, list(shape), dt)), name)

    def ps(stack, name, shape, dt=F32):
        return Buf(stack.enter_context(nc.psum_tensor(name, list(shape), dt)), name)

    def ring(stack, name, n, shape, dt=F32, psum=False):
        mk = ps if psum else sb
        return [mk(stack, "%s%d" % (name, i), shape, dt) for i in range(n)]

    class RR:
        def __init__(self, bufs):
            self.bufs = bufs; self.i = 0

        def next(self):
            b = self.bufs[self.i % len(self.bufs)]; self.i += 1
            return b

    def dump(name, src_ap, dst_slice, reads):
        if debug and name in dbg_d:
            s.dma("sp", I("dma_start", out=dst_slice(dbg_d[name]), in_=src_ap), reads=reads, writes=[Buf(None, "dbg")])

    G = ES
    kc = sb(G, "kc", [128, 5, 128])
    kcb = sb(G, "kcb", [128, 5, 128], BF16)
    ident_b = kcb[:, 0, :]
    trif, trib, ones = kc[:, 1, :], kc[:, 2, :], kc[:, 3, :]
    colA1 = sb(G, "colA1", [128, 8]); colS1 = sb(G, "colS1", [128, 8])
    colcA1 = sb(G, "colcA1", [128, 8]); colcS1 = sb(G, "colcS1", [128, 8])
    colA2 = sb(G, "colA2", [128, 8]); colS2 = sb(G, "colS2", [128, 8])
    G1 = sb(G, "G1", [128, D]); G2 = sb(G, "G2", [128, D])
    convw = sb(G, "convw", [128, 12, 5]); convb = sb(G, "convb", [128, 12])
    dtb_row = sb(G, "dtb_row", [128, 32]); a_row = sb(G, "a_row", [128, 32]); dsk_row = sb(G, "dsk_row", [128, 16])
    halfneg = sb(G, "halfneg", [128, 1])
    dram_x = Buf(x_d, "x"); dram_out = Buf(out_d, "out")
    ysc = [Buf(ysc_d, "ysc%d" % i) for i in range(NCH)]
    outb = [Buf(out_d, "out%d" % i) for i in range(NCH)]

    with contextlib.ExitStack() as P0:
        s.dma("sp", I("dma_start", out=kc[:], in_=kconst_d[:, :, :]), writes=[kc])
        s.op("dve", I("tensor_copy", out=kcb[:], in_=kc[:]), reads=[kc], writes=[kcb])
        s.op("dve", I("memset", halfneg[:], -0.5), writes=[halfneg])

        def col_load(dst, src_row, n):
            s.dma("sp", I("dma_start", out=dst[:], in_=src_row.rearrange("o (k p) -> p (o k)", p=128),
                                              allow_slow_non_contiguous=True), writes=[dst])

        gpm_c = sb(P0, "gpm_c", [128, 8]); gpl_c = sb(P0, "gpl_c", [128, 8])
        col_load(gpm_c, gpm_d, 8); col_load(gpl_c, gpl_d, 8)
        col_load(convb, convb_d, 12)
        for k in range(5):
            s.dma("sp", I("dma_start", out=convw[:, :, k], in_=convw_d[k:k + 1, :].rearrange("o (j p) -> p (o j)", p=128),
                                                   allow_slow_non_contiguous=True), writes=[convw])
        s.dma("sp", I("dma_start", out=dtb_row[:], in_=dtb_d.partition_broadcast(128) if len(dtb_d.shape) == 1 else dtb_d[0, :].partition_broadcast(128)), writes=[dtb_row])
        alog_row = sb(P0, "alog_row", [128, 32])
        s.dma("sp", I("dma_start", out=alog_row[:], in_=alog_d[0, :].partition_broadcast(128)), writes=[alog_row])
        s.dma("sp", I("dma_start", out=dsk_row[:], in_=dsk_d[0, :].partition_broadcast(128)), writes=[dsk_row])
        s.op("act", I("activation", out=a_row[:], in_=alog_row[:], func=AF.Exp), reads=[alog_row], writes=[a_row])
        s.op("dve", I("tensor_scalar", out=a_row[:], in0=a_row[:], scalar1=-1.0, scalar2=None, op0=ALU.mult), reads=[a_row], writes=[a_row])
        gq_row = sb(P0, "gq_row", [128, D]); gl_row = sb(P0, "gl_row", [128, D])
        s.dma("sp", I("dma_start", out=gq_row[:], in_=gqm_d[0, :].partition_broadcast(128)), writes=[gq_row])
        s.dma("sp", I("dma_start", out=gl_row[:], in_=gql_d[0, :].partition_broadcast(128)), writes=[gl_row])

        craw = sb(P0, "craw", [128, 8, 2]); sc = sb(P0, "sc", [128, 8, 2]); scb = sb(P0, "scb", [128, 8, 128])
        s.dma("sp", I("dma_start", out=craw[:, :, 0], in_=c_d.rearrange("o (k p) -> p (o k)", p=128), allow_slow_non_contiguous=True), writes=[craw])
        s.dma("sp", I("dma_start", out=craw[:, :, 1], in_=cctx_d.rearrange("o (k p) -> p (o k)", p=128), allow_slow_non_contiguous=True), writes=[craw])
        s.op("act", I("activation", out=sc[:], in_=craw[:], func=AF.Silu), reads=[craw], writes=[sc])
        s.op("dve", I("tensor_copy", out=scb[:], in_=sc[:, :, 0:1].to_broadcast([128, 8, 128])), reads=[sc], writes=[scb])
        bada = sb(P0, "bada", [1, 6 * D])
        s.dma("sp", I("dma_start", out=bada[:], in_=bada_d[:, :]), writes=[bada])
        adacol = sb(P0, "adacol", [128, 48, 2])
        growraw = sb(P0, "growraw", [128, 2, D])
        wring = RR(ring(P0, "wada", 4, [128, 8, 512]))
        pcol = RR(ring(P0, "pcol", 2, [128, 2], psum=True))
        prow = RR(ring(P0, "prow", 2, [128, 512], psum=True))
        col_tiles = {0: 0, 1: 4, 2: 8, 3: 12, 6: 24, 7: 28, 8: 32, 9: 36}
        row_tiles = {4: (0, 0), 5: (0, 512), 10: (1, 0), 11: (1, 512)}
        for ct in range(12):
            W = wring.next()
            s.dma("sp", I("dma_start", out=W[:], in_=wada_d[:, ct * 512:(ct + 1) * 512].rearrange("(k p) n -> p k n", p=128)), writes=[W])
            if ct in col_tiles:
                for jj in range(4):
                    pc = pcol.next()
                    for k in range(8):
                        s.op("pe", I("matmul", pc[:, :], lhsT=W[:, k, jj * 128:(jj + 1) * 128], rhs=sc[:, k, :], start=(k == 0), stop=False), inc=False,
                             reads=[W, sc], writes=[pc])
                    c0 = ct * 512 + jj * 128
                    s.op("pe", I("matmul", pc[:, :], lhsT=bada[0:1, c0:c0 + 128], rhs=kc[0:1, 3, 0:2], start=False, stop=True), inc=True,
                         reads=[bada, kc], writes=[pc])
                    bi = col_tiles[ct] + jj
                    s.op("act", I("activation", out=adacol[:, bi, :], in_=pc[:, :], func=AF.Identity), reads=[pc], writes=[adacol])
            else:
                pr = prow.next()
                for k in range(8):
                    s.op("pe", I("matmul", pr[:, :], lhsT=scb[:, k, :], rhs=W[:, k, :], start=(k == 0), stop=False), inc=False,
                         reads=[W, scb], writes=[pr])
                s.op("pe", I("matmul", pr[:, :], lhsT=kc[0:1, 3, :], rhs=bada[0:1, ct * 512:(ct + 1) * 512], start=False, stop=True), inc=True,
                     reads=[bada, kc], writes=[pr])
                gi, off = row_tiles[ct]
                s.op("act", I("activation", out=growraw[:, gi, off:off + 512], in_=pr[:, :], func=AF.Identity), reads=[pr], writes=[growraw])
        s.op("dve", I("scalar_tensor_tensor", out=colA1[:], in0=adacol[:, 8:16, 0], scalar=1.0, in1=gpm_c[:], op0=ALU.add, op1=ALU.mult), reads=[adacol, gpm_c], writes=[colA1])
        s.op("dve", I("scalar_tensor_tensor", out=colcA1[:], in0=adacol[:, 8:16, 1], scalar=1.0, in1=gpm_c[:], op0=ALU.add, op1=ALU.mult), reads=[adacol, gpm_c], writes=[colcA1])
        s.op("dve", I("scalar_tensor_tensor", out=colA2[:], in0=adacol[:, 32:40, 0], scalar=1.0, in1=gpl_c[:], op0=ALU.add, op1=ALU.mult), reads=[adacol, gpl_c], writes=[colA2])
        s.op("dve", I("tensor_copy", out=colS1[:], in_=adacol[:, 0:8, 0]), reads=[adacol], writes=[colS1])
        s.op("dve", I("tensor_copy", out=colcS1[:], in_=adacol[:, 0:8, 1]), reads=[adacol], writes=[colcS1])
        s.op("dve", I("tensor_copy", out=colS2[:], in_=adacol[:, 24:32, 0]), reads=[adacol], writes=[colS2])
        s.op("dve", I("tensor_tensor", out=G1[:], in0=growraw[:, 0, :], in1=gq_row[:], op=ALU.mult), reads=[growraw, gq_row], writes=[G1])
        s.op("dve", I("tensor_tensor", out=G2[:], in0=growraw[:, 1, :], in1=gl_row[:], op=ALU.mult), reads=[growraw, gl_row], writes=[G2])
        if debug and "ada" in dbg_d:
            dump("ada", colA1[:], lambda d: d[:, 0:8], [colA1]); dump("ada", colS1[:], lambda d: d[:, 8:16], [colS1])
            dump("ada", colcA1[:], lambda d: d[:, 16:24], [colcA1]); dump("ada", colA2[:], lambda d: d[:, 24:32], [colA2])
            dump("ada", colS2[:], lambda d: d[:, 32:40], [colS2])
            dump("ada", G1[:, 0:64], lambda d: d[:, 40:104], [G1]); dump("ada", G2[:, 0:64], lambda d: d[:, 104:168], [G2])
        s.barrier()
        s.emit()
    if debug and debug.get("stop") == "p0":
        return

    def prenorm_load(W, src_rows_ap, src_buf):
        xt = W["xt"].next()
        s.dma("sp", I("dma_start", out=xt[:], in_=src_rows_ap), reads=[src_buf], writes=[xt])
        return xt

    def prenorm_front(W, src_rows_ap, src_buf, xt=None):
        if xt is None:
            xt = prenorm_load(W, src_rows_ap, src_buf)
        sq = W["sq"].next(); ss = W["ss"].next(); xn = W["xn"].next()
        s.op("act", I("activation", out=sq[:], in_=xt[:], func=AF.Square, accum_out=ss[:, 0:1]), reads=[xt], writes=[sq, ss])
        s.op("dve", I("tensor_scalar", out=ss[:, 1:2], in0=ss[:, 0:1], scalar1=1.0 / D, scalar2=EPS, op0=ALU.mult, op1=ALU.add), reads=[ss], writes=[ss])
        s.op("pool", I("tensor_tensor", out=ss[:, 2:3], in0=ss[:, 1:2], in1=halfneg[:], op=ALU.pow), reads=[ss, halfneg], writes=[ss])
        s.op("dve", I("tensor_scalar", out=xn[:], in0=xt[:], scalar1=ss[:, 2:3], scalar2=None, op0=ALU.mult), reads=[xt, ss], writes=[xn])
        return xt, xn

    def prenorm_back(W, xn, colA, colS, dst_fn, dst_buf, all_act=False, all_dve=False):
        pt = W["pt"].next()
        for k in range(8):
            s.op("pe", I("transpose", out=pt[:, k * 128:(k + 1) * 128], in_=xn[:, k * 128:(k + 1) * 128], identity=ident_b), inc=(k == 7), reads=[xn, kcb], writes=[pt])
        for k in range(8):
            if (k % 2 == 0 or all_act) and not all_dve:
                s.op("act", I("activation", out=dst_fn(k), in_=pt[:, k * 128:(k + 1) * 128], func=AF.Identity, scale=colA[:, k:k + 1], bias=colS[:, k:k + 1]),
                     reads=[pt, colA, colS], writes=[dst_buf])
            else:
                s.op("dve", I("tensor_scalar", out=dst_fn(k), in0=pt[:, k * 128:(k + 1) * 128], scalar1=colA[:, k:k + 1], scalar2=colS[:, k:k + 1], op0=ALU.mult, op1=ALU.add),
                     reads=[pt, colA, colS], writes=[dst_buf])

    def prenorm_T(W, src_rows_ap, src_buf, colA, colS, dst_fn, dst_buf):
        xt, xn = prenorm_front(W, src_rows_ap, src_buf)
        prenorm_back(W, xn, colA, colS, dst_fn, dst_buf)
        return xt

    def prenorm_work(stack, tag):
        return {
            "xt": RR(ring(stack, tag + "xt", 4, [128, D])),
            "sq": RR(ring(stack, tag + "sq", 1, [128, D], BF16)),
            "ss": RR(ring(stack, tag + "ss", 4, [128, 4])),
            "xn": RR(ring(stack, tag + "xn", 4, [128, D], BF16)),
            "pt": RR(ring(stack, tag + "pt", 2, [128, D], BF16, psum=True)),
        }

    def wload_cast(dst, dst_ap, src_ap):
        s.dma_sw(I("dma_start", out=dst_ap, in_=src_ap), writes=[dst])

    with contextlib.ExitStack() as PS:
        xsT = sb(PS, "xsT", [128, 8, L + 2], BF16)
        BT = sb(PS, "BT", [128, 2, L + 2], BF16)
        CT = sb(PS, "CT", [128, 2, L + 2], BF16)
        dts = sb(PS, "dts", [128, NCH, 32])
        cxsT = sb(PS, "cxsT", [128, 8, CTXL + 2], BF16)
        cBT = sb(PS, "cBT", [128, 2, CTXL + 2], BF16)
        cCT = sb(PS, "cCT", [128, 2, CTXL + 2], BF16)
        cdts = sb(PS, "cdts", [128, 2, 32])

        with contextlib.ExitStack() as P2:
            wx = sb(P2, "wx", [128, 8, 1536], BF16)
            wdt = sb(P2, "wdt", [128, 8, 32], BF16)
            wload_cast(wx, wx[:], win_d[:, 1024:2560].rearrange("(k p) n -> p k n", p=128))
            wload_cast(wdt, wdt[:], win_d[:, 2560:2592].rearrange("(k p) n -> p k n", p=128))
            PW = prenorm_work(P2, "p2")
            hx_ring = RR(ring(P2, "hxT", 2, [128, 8, 512], BF16))
            pmm = RR(ring(P2, "pmm", 3, [128, 512], psum=True))
            pdt = RR(ring(P2, "pdt", 1, [128, 32], psum=True))
            Pr = RR(ring(P2, "Ppre", 3, [128, 516], BF16))
            pcv = RR(ring(P2, "pcv", 2, [128, 512], psum=True))
            carry = sb(P2, "carry", [128, 12, 4], BF16)
            dgw = sb(P2, "dgw", [128, 60, 128], BF16)
            s.op("dve", I("tensor_tensor", out=dgw[:], in0=kcb[:, 0:1, :].to_broadcast([128, 60, 128]),
                          in1=convw[:].rearrange("p j k -> p (j k)").unsqueeze(2).to_broadcast([128, 60, 128]), op=ALU.mult), reads=[kcb, convw], writes=[dgw])
            dtt = RR(ring(P2, "dtt", 2, [128, 32]))

            def conv_chunk(j, P, ncol, dst_ap, dst_buf):
                pc = pcv.next()
                for k in range(5):
                    s.op("pe", I("matmul", pc[:, 0:ncol], lhsT=dgw[:, j * 5 + k, :], rhs=P[:, k:k + ncol], start=(k == 0), stop=(k == 4)), inc=(k == 4), reads=[dgw, P], writes=[pc])
                s.op("act", I("activation", out=dst_ap, in_=pc[:, 0:ncol], func=AF.Silu, bias=convb[:, j:j + 1]), reads=[pc, convb], writes=[dst_buf])

            import os
            SUBCUT = int(os.environ.get("P2SUB", "9"))

            def inproj_seq(src_d, src_buf, ntok, colA, colS, xs_dst, B_dst, C_dst, dt_dst, nj):
                s.op("pool", I("memset", carry[:], 0.0), writes=[carry])
                tile_n = min(512, ntok)
                ntile = ntok // tile_n
                nsub = tile_n // 128

                def dest(j, c0, n):
                    if j < 8:
                        return xs_dst[:, j, c0:c0 + n], xs_dst
                    if j < 10:
                        return B_dst[:, j - 8, c0:c0 + n], B_dst
                    return C_dst[:, j - 10, c0:c0 + n], C_dst

                def tile_loads(i):
                    return [prenorm_load(PW, src_d[i * tile_n + sub * 128:i * tile_n + (sub + 1) * 128, :], src_buf) for sub in range(nsub)]

                nxt = tile_loads(0)
                for i in range(ntile):
                    hx = hx_ring.next()
                    xts = nxt
                    fr = [prenorm_front(PW, None, None, xt=xts[sub]) for sub in range(nsub)]
                    if i + 1 < ntile:
                        nxt = tile_loads(i + 1)
                    for sub in range(nsub):
                        prenorm_back(PW, fr[sub][1], colA, colS, lambda k, sub=sub, hx=hx: hx[:, k, sub * 128:(sub + 1) * 128], hx, all_dve=True)
                    for sub in range(nsub if SUBCUT >= 2 else 0):
                        cidx = i * nsub + sub
                        pd = pdt.next(); t1 = dtt.next()
                        for k in range(8):
                            s.op("pe", I("matmul", pd[:, :], lhsT=hx[:, k, sub * 128:(sub + 1) * 128], rhs=wdt[:, k, :], start=(k == 0), stop=(k == 7)), inc=(k == 7),
                                 reads=[hx, wdt], writes=[pd])
                        s.op("dve", I("tensor_tensor", out=t1[:], in0=pd[:, :], in1=dtb_row[:], op=ALU.add), reads=[pd, dtb_row], writes=[t1])
                        s.op("act", I("activation", out=t1[:], in_=t1[:], func=AF.Exp), reads=[t1], writes=[t1])
                        s.op("act", I("activation", out=dt_dst[:, cidx, :], in_=t1[:], func=AF.Ln, bias=1.0), reads=[t1], writes=[dt_dst])
                    pend = None
                    for j in range(nj if SUBCUT >= 3 else 0):
                        pm = pmm.next(); P = Pr.next()
                        for k in range(8):
                            s.op("pe", I("matmul", pm[:, 0:tile_n], lhsT=wx[:, k, j * 128:(j + 1) * 128], rhs=hx[:, k, 0:tile_n], start=(k == 0), stop=(k == 7)), inc=(k == 7),
                                 reads=[hx, wx], writes=[pm])
                        s.op("pool", I("tensor_copy", out=P[:, 0:4], in_=carry[:, j, :]), reads=[carry], writes=[P])
                        s.op("act", I("activation", out=P[:, 4:4 + tile_n], in_=pm[:, 0:tile_n], func=AF.Identity), reads=[pm], writes=[P])
                        s.op("pool", I("tensor_copy", out=carry[:, j, :], in_=P[:, tile_n:tile_n + 4]), reads=[P], writes=[carry])
                        if pend is not None:
                            conv_chunk(*pend)
                        dap, dbuf = dest(j, i * tile_n, tile_n)
                        pend = (j, P, tile_n, dap, dbuf)
                    if pend is not None:
                        conv_chunk(*pend)
                for j in range(nj if SUBCUT >= 5 else 0):
                    P = Pr.next()
                    s.op("pool", I("tensor_copy", out=P[:, 0:4], in_=carry[:, j, :]), reads=[carry], writes=[P])
                    s.op("pool", I("memset", P[:, 4:8], 0.0), writes=[P])
                    dap, dbuf = dest(j, ntok, 2)
                    conv_chunk(j, P, 2, dap, dbuf)

            ctx_buf = Buf(ctx_d, "ctx")
            import os
            cut = int(os.environ.get("P2CUT", "9"))
            if cut >= 1:
                inproj_seq(ctx_d, ctx_buf, CTXL, colcA1, colcS1, cxsT, cBT, cCT, cdts, 12)
            if cut >= 2:
                inproj_seq(x_d, dram_x, L, colA1, colS1, xsT, BT, CT, dts, 12)
            if debug and "xsT" in dbg_d:
                tmpf = sb(P2, "dbgtmp", [128, 512])
                for (nm, srcb, n3) in (("xsT", xsT, 8), ("BT", BT, 2), ("CT", CT, 2)):
                    for j in range(n3):
                        for q in range(1):
                            s.op("dve", I("tensor_copy", out=tmpf[:], in_=srcb[:, j, 2:514]), reads=[srcb], writes=[tmpf])
                            dump(nm, tmpf[:], lambda d, j=j: d[:, j, :], [tmpf])
                dump("dts", dts[:], lambda d: d[:, :, :], [dts])
                dump("cdts", cdts[:], lambda d: d[:, :, :], [cdts])
            s.barrier()
            s.emit()
        if debug and debug.get("stop") == "p2":
            return

        with contextlib.ExitStack() as P3:
            Sst = [sb(P3, "S_f", [128, D]), sb(P3, "S_b", [128, D])]
            Sbf = [sb(P3, "Sbf_f", [128, D], BF16), sb(P3, "Sbf_b", [128, D], BF16)]
            tri_b16 = {0: kcb[:, 1, :], 1: kcb[:, 2, :]}
            tri_f32 = {0: trif, 1: trib}
            Umat = sb(P3, "Umat", [128, 2, 128])
            Dg = sb(P3, "Dg", [128, 16, 128], BF16)
            s.op("dve", I("tensor_tensor", out=Dg[:], in0=kcb[:, 0:1, :].to_broadcast([128, 16, 128]), in1=dsk_row[:].unsqueeze(2).to_broadcast([128, 16, 128]), op=ALU.mult),
                 reads=[kcb, dsk_row], writes=[Dg])
            s.op("dve", I("tensor_scalar", out=Umat[:], in0=kc[:, 1:3, :], scalar1=-1.0, scalar2=1.0, op0=ALU.mult, op1=ALU.add), reads=[kc], writes=[Umat])
            R2 = lambda name, shape, dt=F32, n=2: RR(ring(P3, name, n, shape, dt))
            dtA_r = R2("dtA", [128, 16])
            sm_r = R2("smalls", [128, 5, 16])
            rhs_r = R2("rhs32", [128, 16, 128], F32, 1)
            L_r = R2("Lmat", [128, 16, 128], BF16, 1); M_r = R2("Mmat", [128, 16, 128], BF16)
            cbm_r = R2("cbm", [128, 2, 128], BF16)
            xs_r = R2("xs_sb", [128, 16, 64], BF16); xdt_r = R2("xdt", [128, 16, 64], BF16); xdw_r = R2("xdw", [128, 16, 64], BF16)
            Btok = sb(P3, "Btok", [128, NCH, 256], BF16)
            cBtok = sb(P3, "cBtok", [128, 2, 256], BF16)
            yt_r = R2("ytmp", [128, D], F32, 1); yo_r = R2("yout", [128, D]); yb_r = R2("ybld", [128, D], F32, 1)
            p_small = ps(P3, "p_small", [128, 512])
            p_T = ps(P3, "p_T", [128, 1024], BF16)
            p_seg = RR(ring(P3, "p_seg", 2, [128, 512], psum=True))
            p_yo = [ps(P3, "p_yo0", [128, 512]), ps(P3, "p_yo1", [128, 512])]
            p_st = [ps(P3, "p_st0", [128, 512]), ps(P3, "p_st1", [128, 512])]

            def make_btok(Bsrc, dst, nchunk):
                for c in range(nchunk):
                    c0 = 2 + c * 128
                    for g in range(2):
                        s.op("pe", I("transpose", out=p_T[:, g * 128:(g + 1) * 128], in_=Bsrc[:, g, c0:c0 + 128], identity=ident_b), reads=[Bsrc, kcb], writes=[p_T])
                    s.op("act", I("activation", out=dst[:, c, :], in_=p_T[:, 0:256], func=AF.Identity), reads=[p_T], writes=[dst])

            make_btok(cBT, cBtok, 2)
            make_btok(BT, Btok, NCH)

            def stageA(d, xsrc, Bsrc, Csrc, dtsrc, Btsrc, c, need_y):
                c0 = 2 + c * 128
                ctxd = {}
                dtA = dtA_r.next(); sm = sm_r.next()
                dtd = dtsrc[:, c, d * 16:(d + 1) * 16]
                s.op("dve", I("tensor_tensor", out=dtA[:], in0=dtd, in1=a_row[:, d * 16:(d + 1) * 16], op=ALU.mult), reads=[dtsrc, a_row], writes=[dtA])
                s.op("pe", I("matmul", p_small[:, 0:16], lhsT=tri_f32[d], rhs=dtA[:], start=True, stop=True), inc=True, reads=[kc, dtA], writes=[p_small])
                s.op("pe", I("matmul", p_small[:, 16:32], lhsT=ones, rhs=dtA[:], start=True, stop=True), inc=True, reads=[kc, dtA], writes=[p_small])
                s.op("act", I("activation", out=sm[:, 0, :], in_=p_small[:, 0:16], func=AF.Identity), reads=[p_small], writes=[sm])
                s.op("act", I("activation", out=sm[:, 2, :], in_=p_small[:, 0:16], func=AF.Exp), reads=[p_small], writes=[sm])
                s.op("act", I("activation", out=sm[:, 4, :], in_=p_small[:, 16:32], func=AF.Exp), reads=[p_small], writes=[sm])
                s.op("dve", I("tensor_tensor", out=sm[:, 1, :], in0=p_small[:, 16:32], in1=sm[:, 0, :], op=ALU.subtract), reads=[p_small, sm], writes=[sm])
                s.op("act", I("activation", out=sm[:, 3, :], in_=sm[:, 1, :], func=AF.Exp), reads=[sm], writes=[sm])
                s.op("dve", I("tensor_tensor", out=sm[:, 3, :], in0=sm[:, 3, :], in1=dtd, op=ALU.mult), reads=[sm, dtsrc], writes=[sm])
                for k in range(8):
                    s.op("pe", I("transpose", out=p_T[:, k * 128:(k + 1) * 128], in_=xsrc[:, k, c0:c0 + 128], identity=ident_b), inc=(k == 7), reads=[xsrc, kcb], writes=[p_T])
                xs_sb = xs_r.next(); xdw = xdw_r.next()
                s.op("act", I("activation", out=xs_sb[:].rearrange("p h q -> p (h q)"), in_=p_T[:, :], func=AF.Identity), reads=[p_T], writes=[xs_sb])
                s.op("pool", I("tensor_tensor", out=xdw[:], in0=xs_sb[:], in1=sm[:, 3, :].unsqueeze(2).to_broadcast([128, 16, 64]), op=ALU.mult), reads=[xs_sb, sm], writes=[xdw])
                ctxd.update(sm=sm, xs_sb=xs_sb, xdw=xdw, Btsrc=Btsrc, c=c, c0=c0)
                if need_y:
                    xdt = xdt_r.next(); cbm = cbm_r.next(); rhs = rhs_r.next(); Lm = L_r.next(); Mm = M_r.next()
                    if not hasattr(rhs, "halves"):
                        rhs.halves = [Buf(rhs.ap, rhs.name + "_lo"), Buf(rhs.ap, rhs.name + "_hi")]
                    s.op("dve", I("tensor_tensor", out=xdt[:], in0=xs_sb[:], in1=dtd.unsqueeze(2).to_broadcast([128, 16, 64]), op=ALU.mult), reads=[xs_sb, dtsrc], writes=[xdt])
                    s.op("dve", I("tensor_tensor", out=rhs[:, 0:8, :], in0=tri_f32[d].unsqueeze(1).to_broadcast([128, 8, 128]), in1=dtA[:, 0:8].unsqueeze(2).to_broadcast([128, 8, 128]), op=ALU.mult),
                         reads=[kc, dtA], writes=[rhs.halves[0]])
                    for h in range(8, 16):
                        s.op("act", I("activation", out=rhs[:, h, :], in_=tri_f32[d], func=AF.Identity, scale=dtA[:, h:h + 1]), reads=[kc, dtA], writes=[rhs.halves[1]])
                    for g in range(2):
                        s.op("pe", I("matmul", p_small[:, 256 + g * 128:256 + (g + 1) * 128], lhsT=Bsrc[:, g, c0:c0 + 128], rhs=Csrc[:, g, c0:c0 + 128], start=True, stop=True), inc=True,
                             reads=[Bsrc, Csrc], writes=[p_small])
                    s.op("dve", I("tensor_tensor", out=cbm[:], in0=p_small[:, 256:512].rearrange("p (g t) -> p g t", g=2),
                                  in1=tri_f32[d].unsqueeze(1).to_broadcast([128, 2, 128]), op=ALU.mult), reads=[p_small, kc], writes=[cbm])
                    ctxd.update(rhs=rhs, Lm=Lm, cbm=cbm)
                    ctxd.update(xdt=xdt, Mm=Mm)
                return ctxd

            def stageA2(d, A):
                rhs = A["rhs"]; Lm = A["Lm"]; Mm = A["Mm"]; cbm = A["cbm"]
                for q in range(4):
                    pq = p_seg.next()
                    s.op("pe", I("matmul", pq[:, :], lhsT=Umat[:, d, :], rhs=rhs[:, q * 4:(q + 1) * 4, :].rearrange("p h t -> p (h t)"), start=True, stop=True), inc=True,
                         reads=[Umat, rhs.halves[q // 2]], writes=[pq])
                    s.op("act", I("activation", out=Lm[:, q * 4:(q + 1) * 4, :].rearrange("p h t -> p (h t)"), in_=pq[:, :], func=AF.Exp), reads=[pq], writes=[Lm])
                    gq = q // 2
                    s.op("dve", I("tensor_tensor", out=Mm[:, q * 4:(q + 1) * 4, :], in0=Lm[:, q * 4:(q + 1) * 4, :],
                                  in1=cbm[:, gq:gq + 1, :].to_broadcast([128, 4, 128]), op=ALU.mult), reads=[Lm, cbm], writes=[Mm])

            def stageB(d, Csrc, c, A, need_y, final):
                c0 = A["c0"]; sm = A["sm"]; Btsrc = A["Btsrc"]
                S = Sst[d]; Sb = Sbf[d]
                if need_y:
                    for g in range(2):
                        s.op("pe", I("matmul", p_yo[g][:, :], lhsT=Csrc[:, g, c0:c0 + 128], rhs=Sb[:, g * 512:(g + 1) * 512], start=True, stop=True), inc=True, reads=[Csrc, Sb], writes=[p_yo[g]])
                for g in range(2):
                    s.op("pe", I("matmul", p_st[g][:, :], lhsT=Btsrc[:, c, g * 128:(g + 1) * 128], rhs=A["xdw"][:, g * 8:(g + 1) * 8, :].rearrange("p h q -> p (h q)"), start=True, stop=True), inc=True,
                         reads=[Btsrc, A["xdw"]], writes=[p_st[g]])
                for g in range(2):
                    s.op("pool", I("tensor_tensor", out=S[:, g * 512:(g + 1) * 512].rearrange("p (h q) -> p h q", h=8), in0=S[:, g * 512:(g + 1) * 512].rearrange("p (h q) -> p h q", h=8),
                                   in1=sm[:, 4, g * 8:(g + 1) * 8].unsqueeze(2).to_broadcast([128, 8, 64]), op=ALU.mult), reads=[S, sm], writes=[S])
                yt = yt_r.next() if need_y else None
                if need_y:
                    for g in range(2):
                        s.op("dve", I("tensor_tensor", out=yt[:, g * 512:(g + 1) * 512].rearrange("p (h q) -> p h q", h=8), in0=p_yo[g][:, :].rearrange("p (h q) -> p h q", h=8),
                                      in1=sm[:, 2, g * 8:(g + 1) * 8].unsqueeze(2).to_broadcast([128, 8, 64]), op=ALU.mult), reads=[p_yo[g], sm], writes=[yt])
                for g in range(2):
                    s.op("dve", I("tensor_tensor", out=S[:, g * 512:(g + 1) * 512], in0=p_st[g][:, :], in1=S[:, g * 512:(g + 1) * 512], op=ALU.add), reads=[p_st[g], S], writes=[S])
                s.op("act", I("activation", out=Sb[:], in_=S[:], func=AF.Identity), reads=[S], writes=[Sb])
                if need_y:
                    yo = yo_r.next()
                    for g in range(2):
                        for h in range(8):
                            hh = g * 8 + h
                            s.op("pe", I("matmul", p_yo[g][:, h * 64:(h + 1) * 64], lhsT=A["Mm"][:, hh, :], rhs=A["xdt"][:, hh, :], start=True, stop=(d != 0)), inc=(d != 0 and h == 7),
                                 reads=[A["Mm"], A["xdt"]], writes=[p_yo[g]])
                            if d == 0:
                                s.op("pe", I("matmul", p_yo[g][:, h * 64:(h + 1) * 64], lhsT=Dg[:, hh, :], rhs=A["xs_sb"][:, hh, :], start=False, stop=True), inc=(h == 7),
                                     reads=[Dg, A["xs_sb"]], writes=[p_yo[g]])
                        s.op("dve", I("tensor_tensor", out=yo[:, g * 512:(g + 1) * 512], in0=p_yo[g][:, :], in1=yt[:, g * 512:(g + 1) * 512], op=ALU.add), reads=[p_yo[g], yt], writes=[yo])
                    final(c, yo, A)

            def ssd_pass(d, xsrc, Bsrc, Csrc, dtsrc, Btsrc, nchunk, need_y, final):
                order = list(range(nchunk)) if d == 0 else list(range(nchunk - 1, -1, -1))
                prev = None
                for step in range(nchunk + 1):
                    cur = None
                    if step < nchunk:
                        cur = (order[step], stageA(d, xsrc, Bsrc, Csrc, dtsrc, Btsrc, order[step], need_y))
                    if prev is not None:
                        stageB(d, Csrc, prev[0], prev[1], need_y, final)
                    if cur is not None and need_y:
                        stageA2(d, cur[1])
                    prev = cur

            for d in range(2):
                s.op("pool", I("memset", Sst[d][:], 0.0), writes=[Sst[d]])
                s.op("pool", I("memset", Sbf[d][:], 0.0), writes=[Sbf[d]])
                ssd_pass(d, cxsT, cBT, cCT, cdts, cBtok, 2, False, None)
            if debug and "S0" in dbg_d:
                dump("S0", Sst[0][:], lambda dd: dd[:, 0, :], [Sst[0]]); dump("S0", Sst[1][:], lambda dd: dd[:, 1, :], [Sst[1]])

            def fin_b(c, yo, A):
                s.dma("sp", I("dma_start", out=ysc_d[c * 128:(c + 1) * 128, :], in_=yo[:]), reads=[yo], writes=[ysc[c]])

            ssd_pass(1, xsT, BT, CT, dts, Btok, NCH, True, fin_b)

            def fin_f(c, yo, A):
                yb = yb_r.next()
                s.dma("sp", I("dma_start", out=yb[:], in_=ysc_d[c * 128:(c + 1) * 128, :]), reads=[ysc[c]], writes=[yb])
                s.op("pool", I("tensor_tensor", out=yb[:], in0=yb[:], in1=yo[:], op=ALU.add), reads=[yb, yo], writes=[yb])
                s.dma("sp", I("dma_start", out=ysc_d[c * 128:(c + 1) * 128, :], in_=yb[:]), reads=[yb], writes=[ysc[c]])

            ssd_pass(0, xsT, BT, CT, dts, Btok, NCH, True, fin_f)
            s.barrier()
            s.emit()

    if debug and debug.get("stop") == "ssd":
        for c in range(NCH):
            s.dma("sp", I("dma_start", out=out_d[c * 128:(c + 1) * 128, :], in_=ysc_d[c * 128:(c + 1) * 128, :]), reads=[ysc[c]], writes=[outb[c]])
        s.barrier()
        s.emit()
        return

    with contextlib.ExitStack() as P4:
        uT = sb(P4, "pmT_all", [128, 8, L], BF16)
        uTh = [Buf(uT.ap, "pmT%d" % j) for j in range(8)]
        pscol = sb(P4, "pscol", [128, 8])
        s.dma("sp", I("dma_start", out=pscol[:], in_=pscale_d.rearrange("o (k p) -> p (o k)", p=128), allow_slow_non_contiguous=True), writes=[pscol])
        with contextlib.ExitStack() as P4ab:
            utok = sb(P4ab, "utok", [128, NCH, 1024], BF16)
            utokh = [Buf(utok.ap, "utok%d" % c) for c in range(NCH)]
            with contextlib.ExitStack() as P4a:
                wu = sb(P4a, "wu", [128, 8, 1024], BF16)
                wload_cast(wu, wu[:], win_d[:, 2592:3616].rearrange("(k p) n -> p k n", p=128))
                PW = prenorm_work(P4a, "p4a")
                hx_ring = RR(ring(P4a, "hx4", 2, [128, 8, 512], BF16))
                pu = RR(ring(P4a, "pu", 4, [128, 512], psum=True))

                def tile_loads4(i):
                    return [prenorm_load(PW, x_d[i * 512 + sub * 128:i * 512 + (sub + 1) * 128, :], dram_x) for sub in range(4)]

                nxt = tile_loads4(0)
                for i in range(8):
                    hx = hx_ring.next()
                    xts = nxt
                    fr = [prenorm_front(PW, None, None, xt=xts[sub]) for sub in range(4)]
                    if i + 1 < 8:
                        nxt = tile_loads4(i + 1)
                    for sub in range(4):
                        prenorm_back(PW, fr[sub][1], colA1, colS1, lambda k, sub=sub, hx=hx: hx[:, k, sub * 128:(sub + 1) * 128], hx, all_dve=True)
                    for sub in range(4):
                        c = i * 4 + sub
                        for hf in range(2):
                            pq = pu.next()
                            for k in range(8):
                                s.op("pe", I("matmul", pq[:, :], lhsT=hx[:, k, sub * 128:(sub + 1) * 128], rhs=wu[:, k, hf * 512:(hf + 1) * 512], start=(k == 0), stop=(k == 7)), inc=(k == 7), reads=[wu, hx], writes=[pq])
                            if hf == 0:
                                s.op("act", I("activation", out=utok[:, c, 0:512], in_=pq[:, :], func=AF.Identity), reads=[pq], writes=[utokh[c]])
                            else:
                                s.op("dve", I("tensor_copy", out=utok[:, c, 512:1024], in_=pq[:, :]), reads=[pq], writes=[utokh[c]])
                s.barrier()
                s.emit()
            with contextlib.ExitStack() as P4b:
                kinv = sb(P4b, "kinv_sb", [128, 4, 64])
                s.dma("sp", I("dma_start", out=kinv[:].rearrange("p w j -> p (w j)"), in_=kinv_d[0, :].partition_broadcast(128)), writes=[kinv])
                kpb = sb(P4b, "kpb", [128, NPM, 128], BF16)
                wload_cast(kpb, kpb[:], kpool_d[:, :, :])
                pw = sb(P4b, "pw", [128, 4, 2, 256], BF16)
                wload_cast(pw, pw[:], poolw_d.rearrange("g (ci p) n -> p g ci n", p=128))
                imap = sb(P4b, "imap", [128, 64, 64])
                dT = [sb(P4b, "dT0", [128, L], BF16), sb(P4b, "dT1", [128, L], BF16)]
                tmp_r = RR(ring(P4b, "ptmp", 2, [128, 512]))
                p_S = RR(ring(P4b, "p_S", 2, [128, 512], psum=True))
                p_U = RR(ring(P4b, "p_U", 2, [128, 512], BF16, psum=True))
                ppm = RR(ring(P4b, "ppm", 4, [128, 512], psum=True))
                for g in range(4):
                    s.op("dve", I("tensor_tensor", out=imap[:], in0=kinv[:, g, :].unsqueeze(2).to_broadcast([128, 64, 64]), in1=kinv[:, g, :].unsqueeze(1).to_broadcast([128, 64, 64]), op=ALU.mult),
                         reads=[kinv], writes=[imap])
                    deltas = sorted(d_ for (gi, d_) in _PIDX if gi == g)
                    for jj in range(2):
                        j = 2 * g + jj
                        for q in range(8):
                            pS = p_S.next(); pU = p_U.next(); tmp = tmp_r.next()
                            for cc in range(4):
                                cd = 4 * q + cc
                                valid = [d_ for d_ in deltas if 0 <= cd + d_ < NCH]
                                for n_, d_ in enumerate(valid):
                                    s.op("pe", I("matmul", pS[:, cc * 128:(cc + 1) * 128], lhsT=utok[:, cd + d_, j * 128:(j + 1) * 128], rhs=kpb[:, _PIDX[(g, d_)], :],
                                                 start=(n_ == 0), stop=(n_ == len(valid) - 1)), inc=(n_ == len(valid) - 1), reads=[utokh[cd + d_], kpb], writes=[pS])
                                s.op("pe", I("transpose", out=pU[:, cc * 128:(cc + 1) * 128], in_=utok[:, cd, j * 128:(j + 1) * 128], identity=ident_b), reads=[utokh[cd], kcb], writes=[pU])
                            s.op("dve", I("tensor_tensor", out=tmp[:], in0=pS[:, :], in1=imap[:, 8 * q:8 * q + 8, :].rearrange("p r q -> p (r q)"), op=ALU.mult), reads=[pS, imap], writes=[tmp])
                            s.op("dve", I("tensor_tensor", out=dT[jj][:, q * 512:(q + 1) * 512], in0=tmp[:], in1=pU[:, :], op=ALU.subtract), reads=[tmp, pU], writes=[dT[jj]])
                    for i in range(8):
                        pcs = [ppm.next(), ppm.next()]
                        for co in range(2):
                            for ci in range(2):
                                s.op("pe", I("matmul", pcs[co][:, :], lhsT=pw[:, g, ci, co * 128:(co + 1) * 128], rhs=dT[ci][:, i * 512:(i + 1) * 512], start=(ci == 0), stop=(ci == 1)), inc=(ci == 1),
                                     reads=[pw, dT[ci]], writes=[pcs[co]])
                        for co in range(2):
                            s.op("act", I("activation", out=uT[:, 2 * g + co, i * 512:(i + 1) * 512], in_=pcs[co][:, :], func=AF.Identity, scale=pscol[:, 2 * g + co:2 * g + co + 1]),
                                 reads=[pcs[co], pscol], writes=[uTh[2 * g + co]])
                s.barrier()
                s.emit()
        with contextlib.ExitStack() as P4c:
            wz = sb(P4c, "wz", [128, 8, 1024], BF16)
            wo = sb(P4c, "wo", [128, 16, 1024], BF16)
            sncol = sb(P4c, "sncol", [128, 8])
            s.dma("sp", I("dma_start", out=sncol[:], in_=snw_d.rearrange("o (k p) -> p (o k)", p=128), allow_slow_non_contiguous=True), writes=[sncol])
            wload_cast(wz, wz[:], win_d[:, 0:1024].rearrange("(k p) n -> p k n", p=128))
            wload_cast(wo, wo[:], wout_d.rearrange("(k p) n -> p k n", p=128))
            for k in range(8):
                s.op("dve" if k % 2 else "pool", I("tensor_scalar", out=wo[:, k, :], in0=wo[:, k, :], scalar1=sncol[:, k:k + 1], scalar2=None, op0=ALU.mult), reads=[wo, sncol], writes=[wo])
            PW = {
                "xt": RR(ring(P4c, "p4cxt", 7, [128, D])),
                "sq": RR(ring(P4c, "p4csq", 1, [128, D], BF16)),
                "ss": RR(ring(P4c, "p4css", 3, [128, 4])),
                "xn": RR(ring(P4c, "p4cxn", 2, [128, D], BF16)),
                "pt": RR(ring(P4c, "p4cpt", 1, [128, D], BF16, psum=True)),
            }
            hxc_r = RR(ring(P4c, "hxc", 3, [128, 8, 128], BF16))
            sz_r = RR(ring(P4c, "sz", 1, [128, D])); yt_r = RR(ring(P4c, "yt4", 4, [128, D]))
            ygn_r = RR(ring(P4c, "ygn", 2, [128, D], BF16)); ygT_r = RR(ring(P4c, "ygT", 2, [128, 8, 128], BF16))
            st_r = RR(ring(P4c, "st4", 3, [128, 16])); x1_r = RR(ring(P4c, "x1", 3, [128, D]))
            sq4 = sb(P4c, "sq4", [128, 512], BF16)
            p_z = RR(ring(P4c, "p_z", 2, [128, 512], psum=True))
            p_T4 = ps(P4c, "p_T4", [128, D], BF16)
            p_mix = RR(ring(P4c, "p_mix", 4, [128, 512], psum=True))
            ST = {}

            def FL(c):
                xt = prenorm_load(PW, x_d[c * 128:(c + 1) * 128, :], dram_x)
                yt = yt_r.next()
                s.dma("sp", I("dma_start", out=yt[:], in_=ysc_d[c * 128:(c + 1) * 128, :]), reads=[ysc[c]], writes=[yt])
                ST[c] = dict(xt=xt, yt=yt)

            def F(c):
                xt, xn = prenorm_front(PW, None, None, xt=ST[c]["xt"])
                ST[c].update(xn=xn, st=st_r.next())

            def PT(c):
                hxc = hxc_r.next(); ST[c]["hxc"] = hxc
                prenorm_back(PW, ST[c]["xn"], colA1, colS1, lambda k, hxc=hxc: hxc[:, k, :], hxc, all_act=True)

            def YT(c):
                ygn = ST[c]["ygn"]; ygT = ygT_r.next(); ST[c]["ygT"] = ygT
                for k in range(8):
                    s.op("pe", I("transpose", out=p_T4[:, k * 128:(k + 1) * 128], in_=ygn[:, k * 128:(k + 1) * 128], identity=ident_b), inc=(k == 7), reads=[ygn, kcb], writes=[p_T4])
                s.op("act", I("activation", out=ygT[:].rearrange("p k t -> p (k t)"), in_=p_T4[:, :], func=AF.Identity), reads=[p_T4], writes=[ygT])

            def Z(c):
                hxc = ST[c]["hxc"]; yt = ST[c]["yt"]; st = ST[c]["st"]
                sz = sz_r.next(); ygn = ygn_r.next(); ST[c]["ygn"] = ygn
                for hf in range(2):
                    pz = p_z.next()
                    for k in range(8):
                        s.op("pe", I("matmul", pz[:, :], lhsT=hxc[:, k, :], rhs=wz[:, k, hf * 512:(hf + 1) * 512], start=(k == 0), stop=(k == 7)), inc=(k == 7), reads=[hxc, wz], writes=[pz])
                    s.op("act", I("activation", out=sz[:, hf * 512:(hf + 1) * 512], in_=pz[:, :], func=AF.Silu), reads=[pz], writes=[sz])
                s.op("dve", I("tensor_tensor", out=yt[:], in0=yt[:], in1=sz[:], op=ALU.mult), reads=[yt, sz], writes=[yt])
                for gg in range(2):
                    s.op("act", I("activation", out=sq4[:], in_=yt[:, gg * 512:(gg + 1) * 512], func=AF.Square, accum_out=st[:, gg:gg + 1]), reads=[yt], writes=[sq4, st])
                s.op("dve", I("tensor_scalar", out=st[:, 2:4], in0=st[:, 0:2], scalar1=1.0 / 512, scalar2=EPS, op0=ALU.mult, op1=ALU.add), reads=[st], writes=[st])
                s.op("pool", I("tensor_tensor", out=st[:, 4:6], in0=st[:, 2:4], in1=halfneg[:, 0:1].to_broadcast([128, 2]), op=ALU.pow), reads=[st, halfneg], writes=[st])
                for gg in range(2):
                    s.op("dve", I("tensor_scalar", out=ygn[:, gg * 512:(gg + 1) * 512], in0=yt[:, gg * 512:(gg + 1) * 512], scalar1=st[:, 4 + gg:5 + gg], scalar2=None, op0=ALU.mult),
                         reads=[yt, st], writes=[ygn])

            def OP(c):
                ygT = ST[c]["ygT"]; st = ST[c]["st"]
                pms = [p_mix.next(), p_mix.next()]; ST[c]["pms"] = pms
                for hf in range(2):
                    for k in range(16):
                        lhs = ygT[:, k, :] if k < 8 else uT[:, k - 8, c * 128:(c + 1) * 128]
                        rd = [ygT, wo] if k < 8 else [uTh[k - 8], wo]
                        s.op("pe", I("matmul", pms[hf][:, :], lhsT=lhs, rhs=wo[:, k, hf * 512:(hf + 1) * 512], start=(k == 0), stop=(k == 15)), inc=(k == 15), reads=rd, writes=[pms[hf]])
                    s.op("act", I("activation", out=sq4[:], in_=pms[hf][:, :], func=AF.Square, accum_out=st[:, 6 + hf:7 + hf]), reads=[pms[hf]], writes=[sq4, st])
                s.op("dve", I("tensor_tensor", out=st[:, 8:9], in0=st[:, 6:7], in1=st[:, 7:8], op=ALU.add), reads=[st], writes=[st])
                s.op("dve", I("tensor_scalar", out=st[:, 9:10], in0=st[:, 8:9], scalar1=1.0 / D, scalar2=EPS, op0=ALU.mult, op1=ALU.add), reads=[st], writes=[st])
                s.op("pool", I("tensor_tensor", out=st[:, 10:11], in0=st[:, 9:10], in1=halfneg[:], op=ALU.pow), reads=[st, halfneg], writes=[st])

            def FIN(c):
                pms = ST[c]["pms"]; st = ST[c]["st"]; xt = ST[c]["xt"]; x1 = x1_r.next()
                for hf in range(2):
                    s.op("dve", I("scalar_tensor_tensor", out=x1[:, hf * 512:(hf + 1) * 512], in0=pms[hf][:, :], scalar=st[:, 10:11], in1=G1[:, hf * 512:(hf + 1) * 512], op0=ALU.mult, op1=ALU.mult),
                         reads=[pms[hf], st, G1], writes=[x1])
                s.op("dve", I("tensor_tensor", out=x1[:], in0=x1[:], in1=xt[:], op=ALU.add), reads=[x1, xt], writes=[x1])
                PEND.append((c, x1))
                del ST[c]

            PEND = []

            def flush_stores():
                while PEND:
                    c, x1 = PEND.pop(0)
                    s.dma("sp", I("dma_start", out=out_d[c * 128:(c + 1) * 128, :], in_=x1[:]), reads=[x1], writes=[outb[c]])

            FL(0); FL(1); FL(2); F(0); F(1); PT(0)
            for t in range(NCH + 2):
                if t + 3 < NCH:
                    FL(t + 3)
                flush_stores()
                if t + 2 < NCH:
                    F(t + 2)
                if t + 1 < NCH:
                    PT(t + 1)
                if 0 <= t - 1 < NCH:
                    YT(t - 1)
                if t < NCH:
                    Z(t)
                if 0 <= t - 1 < NCH:
                    OP(t - 1)
                if 0 <= t - 2 < NCH:
                    FIN(t - 2)
            flush_stores()
            s.barrier()
            s.emit()
    if debug and debug.get("stop") == "p4":
        return

    with contextlib.ExitStack() as P5:
        w1 = sb(P5, "w1", [128, 8, DFF], BF16)
        w2 = sb(P5, "w2", [128, 32, D], BF16)
        for hh in range(2):
            wload_cast(w1, w1[:, :, hh * 2048:(hh + 1) * 2048], w1_d[:, hh * 2048:(hh + 1) * 2048].rearrange("(k p) n -> p k n", p=128))
        wload_cast(w2, w2[:], w2_d.rearrange("(f p) n -> p f n", p=128))
        PW = {
            "xt": RR(ring(P5, "p5xt", 4, [128, D])),
            "sq": RR(ring(P5, "p5sq", 1, [128, D], BF16)),
            "ss": RR(ring(P5, "p5ss", 4, [128, 4])),
            "xn": RR(ring(P5, "p5xn", 2, [128, D], BF16)),
            "pt": RR(ring(P5, "p5pt", 2, [128, D], BF16, psum=True)),
        }
        hm_r = RR(ring(P5, "hm", 2, [128, 8, 256], BF16))
        hT = sb(P5, "hT", [128, 32, 256], BF16)
        r_r = RR(ring(P5, "relu", 3, [128, 256], BF16))
        o_r = RR(ring(P5, "o5", 4, [128, D]))
        st_r = RR(ring(P5, "st5", 2, [128, 8]))
        sq5 = sb(P5, "sq5", [128, 512], BF16)
        p_h = RR(ring(P5, "p_h", 3, [128, 256], psum=True))
        p_o = RR(ring(P5, "p_o", 3, [128, 512], psum=True))
        NT = L // 256
        MS = {}

        def front5(i):
            fr = []
            for sub in range(2):
                c = i * 2 + sub
                fr.append(prenorm_front(PW, out_d[c * 128:(c + 1) * 128, :], outb[c]))
            MS[i] = dict(fr=fr)

        def back5(i):
            hm = hm_r.next(); MS[i]["hm"] = hm
            for sub in range(2):
                prenorm_back(PW, MS[i]["fr"][sub][1], colA2, colS2, lambda k, sub=sub, hm=hm: hm[:, k, sub * 128:(sub + 1) * 128], hm, all_dve=True)

        def mlp1(i):
            hm = MS[i]["hm"]
            for f in range(32):
                if f == 8 and i + 1 < NT:
                    front5(i + 1)
                ph = p_h.next(); rr = r_r.next()
                for k in range(8):
                    s.op("pe", I("matmul", ph[:, :], lhsT=w1[:, k, f * 128:(f + 1) * 128], rhs=hm[:, k, :], start=(k == 0), stop=(k == 7)), inc=(k == 7), reads=[w1, hm], writes=[ph])
                s.op("act", I("activation", out=rr[:], in_=ph[:, :], func=AF.Relu), reads=[ph], writes=[rr])
                s.op("pool", I("tensor_tensor", out=hT[:, f, :], in0=rr[:], in1=rr[:], op=ALU.mult), reads=[rr], writes=[hT])

        def mlp2(i):
            for sub in range(2):
                c = i * 2 + sub
                o = o_r.next(); st = st_r.next(); xt = MS[i]["fr"][sub][0]
                pos = [p_o.next(), p_o.next()]
                for hf in range(2):
                    for f in range(32):
                        s.op("pe", I("matmul", pos[hf][:, :], lhsT=hT[:, f, sub * 128:(sub + 1) * 128], rhs=w2[:, f, hf * 512:(hf + 1) * 512], start=(f == 0), stop=(f == 31)), inc=(f == 31), reads=[hT, w2], writes=[pos[hf]])
                    s.op("act", I("activation", out=sq5[:], in_=pos[hf][:, :], func=AF.Square, accum_out=st[:, hf:hf + 1]), reads=[pos[hf]], writes=[sq5, st])
                s.op("dve", I("tensor_tensor", out=st[:, 2:3], in0=st[:, 0:1], in1=st[:, 1:2], op=ALU.add), reads=[st], writes=[st])
                s.op("dve", I("tensor_scalar", out=st[:, 3:4], in0=st[:, 2:3], scalar1=1.0 / D, scalar2=EPS, op0=ALU.mult, op1=ALU.add), reads=[st], writes=[st])
                s.op("pool", I("tensor_tensor", out=st[:, 4:5], in0=st[:, 3:4], in1=halfneg[:], op=ALU.pow), reads=[st, halfneg], writes=[st])
                for hf in range(2):
                    s.op("dve", I("scalar_tensor_tensor", out=o[:, hf * 512:(hf + 1) * 512], in0=pos[hf][:, :], scalar=st[:, 4:5], in1=G2[:, hf * 512:(hf + 1) * 512], op0=ALU.mult, op1=ALU.mult),
                         reads=[pos[hf], st, G2], writes=[o])
                s.op("pool", I("tensor_tensor", out=o[:], in0=o[:], in1=xt[:], op=ALU.add), reads=[o, xt], writes=[o])
                PEND5.append((c, o))
            del MS[i]

        PEND5 = []

        def flush5():
            while PEND5:
                c, o = PEND5.pop(0)
                s.dma("sp", I("dma_start", out=out_d[c * 128:(c + 1) * 128, :], in_=o[:]), reads=[o], writes=[outb[c]])

        front5(0); back5(0)
        for i in range(NT):
            flush5()
            mlp1(i)
            if i + 1 < NT:
                back5(i + 1)
            mlp2(i)
        flush5()
        s.barrier()
        s.emit()


def pool_mats():
    mats = []; index = {}
    a = np.repeat(np.arange(2), 64); j = np.tile(np.arange(64), 2)
    for gi, w in enumerate((2, 4, 8, 16)):
        h = w // 2
        for delta in range(-5, 6):
            rd = 2 * delta + a[:, None] - a[None, :]
            cd = j[:, None] - j[None, :]
            m = ((rd >= -h) & (rd <= h - 1) & (cd >= -h) & (cd <= h - 1)).astype(np.float32)
            if m.any():
                index[(gi, delta)] = len(mats); mats.append(m)
    return np.stack(mats, axis=1), index


_PM, _PIDX = pool_mats()
NPM = _PM.shape[1]


def make_consts():
    k = np.arange(128)
    kc = np.zeros((128, 5, 128), np.float32)
    kc[:, 0, :] = np.eye(128, dtype=np.float32)
    kc[:, 1, :] = (k[:, None] <= k[None, :]).astype(np.float32)
    kc[:, 2, :] = (k[:, None] >= k[None, :]).astype(np.float32)
    kc[:, 3, :] = 1.0
    inv = np.zeros((4, 64), np.float32)
    t = np.arange(64)
    for gi, w in enumerate((2, 4, 8, 16)):
        lo = np.clip(t - w // 2, 0, 64); hi = np.clip(t + w // 2, 0, 64)
        inv[gi] = 1.0 / (hi - lo)
    return kc, inv.reshape(1, 256)


def core_inputs(inputs, b):
    f = lambda a: np.ascontiguousarray(np.asarray(a, dtype=np.float32))
    kc, kinv = make_consts()
    m = {
        "x": f(inputs["x"][b]), "ctx": f(inputs["ctx"][b]), "c": f(inputs["c"][b:b + 1]),
        "c_ctx": f(np.asarray(inputs["c_ctx"]).reshape(1, D)),
        "w_ada": f(inputs["w_ada"][0]), "b_ada": f(inputs["b_ada"][0:1]),
        "pre_mix_g": f(inputs["pre_mix_g"][0:1]), "post_mix_g": f(inputs["post_mix_g"][0:1]),
        "pre_mlp_g": f(inputs["pre_mlp_g"][0:1]), "post_mlp_g": f(inputs["post_mlp_g"][0:1]),
        "w_in": f(inputs["w_in"][0]), "conv_w": f(inputs["conv_w"][0]), "conv_b": f(inputs["conv_b"][0:1]),
        "dt_bias": f(np.asarray(inputs["dt_bias"][0]).reshape(1, 32)), "a_log": f(np.asarray(inputs["a_log"][0]).reshape(1, 32)),
        "d_skip": f(inputs["d_skip"][0:1]), "ssm_norm_w": f(inputs["ssm_norm_w"][0:1]),
        "pool_w": f(inputs["pool_w"][0]), "pool_scale": f(inputs["pool_scale"][0:1]),
        "w_out": f(inputs["w_out"][0]), "w_mlp1": f(inputs["w_mlp1"][0]), "w_mlp2": f(inputs["w_mlp2"][0]),
        "kconst": kc, "kinv": kinv, "kpool": np.ascontiguousarray(_PM),
    }
    return m


def kernel(**inputs):
    nc = build_program()
    nb = inputs["x"].shape[0]
    in_maps = [core_inputs(inputs, b) for b in range(nb)]
    res = run_bass_kernel_spmd(nc, in_maps, core_ids=list(range(nb)))
    out = np.stack([np.asarray(r["out"]) for r in res.results], axis=0)
    return out.astype(np.float32)
```

```python
import contextlib
import numpy as np
import concourse.bass as bass
import concourse.mybir as mybir
from concourse.bass_utils import run_bass_kernel_spmd

F32 = mybir.dt.float32
BF16 = mybir.dt.bfloat16
AF = mybir.ActivationFunctionType
ALU = mybir.AluOpType

L = 4096
D = 1024
KD = 8
NCH = 32
CTXL = 256
DFF = 4096
EPS = 1e-6
ENGS = ("pe", "act", "dve", "pool", "sp")
N_DMA_SEMS = 40
N_SW_SEMS = 14


class Buf:
    def __init__(self, ap=None, name=""):
        self.ap = ap
        self.name = name
        self.w = {}
        self.r = {}
        self.strict = "sq" in name

    def __getitem__(self, k):
        return self.ap[k]


def I(name, *args, **kw):
    return lambda e: getattr(e, name)(*args, **kw)


class Sched:
    def __init__(self, nc, stack):
        self.nc = nc
        self.q = {e: [] for e in ENGS}
        self.cnt = {e: 0 for e in ENGS}
        self.known = {e: {} for e in ENGS}
        self.dma_val = [0] * N_DMA_SEMS
        self.dma_rr = 0
        self.csem = {e: stack.enter_context(nc.semaphore("c_" + e)) for e in ENGS}
        self.dsem = [stack.enter_context(nc.semaphore("d_%d" % i)) for i in range(N_DMA_SEMS)]
        self.swsem = [stack.enter_context(nc.semaphore("w_%d" % i)) for i in range(N_SW_SEMS)]
        self.sw_used = 0
        self.n_ins = 0

    def _need(self, eng, key, val, waits):
        if key == eng and eng == "pe":
            return
        if self.known[eng].get(key, 0) >= val:
            return
        self.known[eng][key] = val
        waits[key] = max(waits.get(key, 0), val)

    def _deps(self, eng, reads, writes):
        waits = {}
        for t in reads:
            for k, v in t.w.items():
                self._need(eng, k, v, waits)
        for t in writes:
            for k, v in t.w.items():
                if k != eng or eng == "pool" or t.strict:
                    self._need(eng, k, v, waits)
            for k, v in t.r.items():
                if k != eng or eng == "pool" or t.strict:
                    self._need(eng, k, v, waits)
        return waits

    def _commit(self, key, val, reads, writes):
        for t in reads:
            t.r[key] = max(t.r.get(key, 0), val)
        for t in writes:
            t.w[key] = max(t.w.get(key, 0), val)
            t.r = {}

    def op(self, eng, fn, reads=(), writes=(), inc=True):
        waits = self._deps(eng, reads, writes)
        if inc:
            self.cnt[eng] += 1
            self.q[eng].append((waits, fn, ("c", eng)))
            self._commit(eng, self.cnt[eng], reads, writes)
        else:
            assert eng == "pe"
            self.q[eng].append((waits, fn, ("n", eng)))
            self._commit(eng, self.cnt[eng] + 1, reads, writes)

    def dma(self, eng, fn, reads=(), writes=()):
        waits = self._deps(eng, reads, writes)
        i = self.dma_rr
        self.dma_rr = (self.dma_rr + 1) % N_DMA_SEMS
        if self.dma_val[i] > 0:
            self._need(eng, i, self.dma_val[i], waits)
        self.dma_val[i] += 16
        self.q[eng].append((waits, fn, ("d", i)))
        self._commit(i, self.dma_val[i], reads, writes)

    def dma_sw(self, fn, reads=(), writes=()):
        eng = "pool"
        waits = self._deps(eng, reads, writes)
        key = "sw%d" % self.sw_used
        self.sw_used += 1
        assert self.sw_used <= N_SW_SEMS
        self.q[eng].append((waits, fn, ("w", key)))
        self._commit(key, 16, reads, writes)

    def _sem(self, key):
        if isinstance(key, int):
            return self.dsem[key]
        if key.startswith("sw"):
            return self.swsem[int(key[2:])]
        return self.csem[key]

    def barrier(self):
        for eng in ENGS:
            waits = {}
            for e2 in ENGS:
                if e2 != eng and self.cnt[e2] > 0:
                    self._need(eng, e2, self.cnt[e2], waits)
            for i in range(N_DMA_SEMS):
                if self.dma_val[i] > 0:
                    self._need(eng, i, self.dma_val[i], waits)
            for i in range(self.sw_used):
                self._need(eng, "sw%d" % i, 16, waits)
            self.q[eng].append((waits, None, None))

    def emit(self):
        nc = self.nc
        with nc.Block() as block:
            def run(name, e):
                for waits, fn, inc in self.q[name]:
                    for key, val in waits.items():
                        e.wait_ge(self._sem(key), val)
                    if fn is None:
                        continue
                    ins = fn(e)
                    self.n_ins += 1
                    if inc[0] == "n":
                        pass
                    elif inc[0] == "c":
                        ins.then_inc(self.csem[inc[1]], 1)
                    elif inc[0] == "w":
                        ins.then_inc(self._sem(inc[1]), 16)
                    else:
                        ins.then_inc(self.dsem[inc[1]], 16)

            @block.tensor
            def _(e):
                run("pe", e)

            @block.scalar
            def _(e):
                run("act", e)

            @block.vector
            def _(e):
                run("dve", e)

            @block.gpsimd
            def _(e):
                run("pool", e)

            @block.sync
            def _(e):
                run("sp", e)
        self.q = {e: [] for e in ENGS}


def build_program(debug=None):
    nc = bass.Bass("TRN2", target_bir_lowering=False)
    ES = contextlib.ExitStack()
    with ES:
        _build(nc, ES, debug)
    return nc


def _build(nc, ES, debug):
    def din(name, shape):
        return nc.dram_tensor(name, list(shape), F32, kind="ExternalInput").ap()

    x_d = din("x", [L, D]); ctx_d = din("ctx", [CTXL, D])
    c_d = din("c", [1, D]); cctx_d = din("c_ctx", [1, D])
    wada_d = din("w_ada", [D, 6 * D]); bada_d = din("b_ada", [1, 6 * D])
    gpm_d = din("pre_mix_g", [1, D]); gqm_d = din("post_mix_g", [1, D])
    gpl_d = din("pre_mlp_g", [1, D]); gql_d = din("post_mlp_g", [1, D])
    win_d = din("w_in", [D, 3616]); convw_d = din("conv_w", [5, 1536]); convb_d = din("conv_b", [1, 1536])
    dtb_d = din("dt_bias", [1, 32]); alog_d = din("a_log", [1, 32]); dsk_d = din("d_skip", [1, 16])
    snw_d = din("ssm_norm_w", [1, D]); poolw_d = din("pool_w", [4, 256, 256]); pscale_d = din("pool_scale", [1, D])
    wout_d = din("w_out", [2 * D, D]); w1_d = din("w_mlp1", [D, DFF]); w2_d = din("w_mlp2", [DFF, D])
    kconst_d = din("kconst", [128, 5, 128])
    kinv_d = din("kinv", [1, 4 * 64])
    kpool_d = din("kpool", [128, NPM, 128])
    out_d = nc.dram_tensor("out", [L, D], F32, kind="ExternalOutput").ap()
    ysc_d = nc.dram_tensor("y_scratch", [L, D], F32, kind="Internal").ap()
    dbg_d = {}
    if debug:
        for name, shape in debug.get("shapes", {}).items():
            dbg_d[name] = nc.dram_tensor("dbg_" + name, list(shape), F32, kind="ExternalOutput").ap()

    s = Sched(nc, ES)

    def sb(stack, name, shape, dt=F32):
        return Buf(stack.enter_context(nc.sbuf_tensor(name, list(shape), dt)), name)

    def ps(stack, name, shape, dt=F32):
        return Buf(stack.enter_context(nc.psum_tensor(name, list(shape), dt)), name)

    def ring(stack, name, n, shape, dt=F32, psum=False):
        mk = ps if psum else sb
        return [mk(stack, "%s%d" % (name, i), shape, dt) for i in range(n)]

    class RR:
        def __init__(self, bufs):
            self.bufs = bufs; self.i = 0

        def next(self):
            b = self.bufs[self.i % len(self.bufs)]; self.i += 1
            return b

    def dump(name, src_ap, dst_slice, reads):
        if debug and name in dbg_d:
            s.dma("sp", I("dma_start", out=dst_slice(dbg_d[name]), in_=src_ap), reads=reads, writes=[Buf(None, "dbg")])

    G = ES
    kc = sb(G, "kc", [128, 5, 128])
    kcb = sb(G, "kcb", [128, 5, 128], BF16)
    ident_b = kcb[:, 0, :]
    trif, trib, ones = kc[:, 1, :], kc[:, 2, :], kc[:, 3, :]
    colA1 = sb(G, "colA1", [128, 8]); colS1 = sb(G, "colS1", [128, 8])
    colcA1 = sb(G, "colcA1", [128, 8]); colcS1 = sb(G, "colcS1", [128, 8])
    colA2 = sb(G, "colA2", [128, 8]); colS2 = sb(G, "colS2", [128, 8])
    G1 = sb(G, "G1", [128, D]); G2 = sb(G, "G2", [128, D])
    convw = sb(G, "convw", [128, 12, 5]); convb = sb(G, "convb", [128, 12])
    dtb_row = sb(G, "dtb_row", [128, 32]); a_row = sb(G, "a_row", [128, 32]); dsk_row = sb(G, "dsk_row", [128, 16])
    halfneg = sb(G, "halfneg", [128, 1])
    dram_x = Buf(x_d, "x"); dram_out = Buf(out_d, "out")
    ysc = [Buf(ysc_d, "ysc%d" % i) for i in range(NCH)]
    outb = [Buf(out_d, "out%d" % i) for i in range(NCH)]

    with contextlib.ExitStack() as P0:
        s.dma("sp", I("dma_start", out=kc[:], in_=kconst_d[:, :, :]), writes=[kc])
        s.op("dve", I("tensor_copy", out=kcb[:], in_=kc[:]), reads=[kc], writes=[kcb])
        s.op("dve", I("memset", halfneg[:], -0.5), writes=[halfneg])

        def col_load(dst, src_row, n):
            s.dma("sp", I("dma_start", out=dst[:], in_=src_row.rearrange("o (k p) -> p (o k)", p=128),
                                              allow_slow_non_contiguous=True), writes=[dst])

        gpm_c = sb(P0, "gpm_c", [128, 8]); gpl_c = sb(P0, "gpl_c", [128, 8])
        col_load(gpm_c, gpm_d, 8); col_load(gpl_c, gpl_d, 8)
        col_load(convb, convb_d, 12)
        for k in range(5):
            s.dma("sp", I("dma_start", out=convw[:, :, k], in_=convw_d[k:k + 1, :].rearrange("o (j p) -> p (o j)", p=128),
                                                   allow_slow_non_contiguous=True), writes=[convw])
        s.dma("sp", I("dma_start", out=dtb_row[:], in_=dtb_d.partition_broadcast(128) if len(dtb_d.shape) == 1 else dtb_d[0, :].partition_broadcast(128)), writes=[dtb_row])
        alog_row = sb(P0, "alog_row", [128, 32])
        s.dma("sp", I("dma_start", out=alog_row[:], in_=alog_d[0, :].partition_broadcast(128)), writes=[alog_row])
        s.dma("sp", I("dma_start", out=dsk_row[:], in_=dsk_d[0, :].partition_broadcast(128)), writes=[dsk_row])
        s.op("act", I("activation", out=a_row[:], in_=alog_row[:], func=AF.Exp), reads=[alog_row], writes=[a_row])
        s.op("dve", I("tensor_scalar", out=a_row[:], in0=a_row[:], scalar1=-1.0, scalar2=None, op0=ALU.mult), reads=[a_row], writes=[a_row])
        gq_row = sb(P0, "gq_row", [128, D]); gl_row = sb(P0, "gl_row", [128, D])
        s.dma("sp", I("dma_start", out=gq_row[:], in_=gqm_d[0, :].partition_broadcast(128)), writes=[gq_row])
        s.dma("sp", I("dma_start", out=gl_row[:], in_=gql_d[0, :].partition_broadcast(128)), writes=[gl_row])

        craw = sb(P0, "craw", [128, 8, 2]); sc = sb(P0, "sc", [128, 8, 2]); scb = sb(P0, "scb", [128, 8, 128])
        s.dma("sp", I("dma_start", out=craw[:, :, 0], in_=c_d.rearrange("o (k p) -> p (o k)", p=128), allow_slow_non_contiguous=True), writes=[craw])
        s.dma("sp", I("dma_start", out=craw[:, :, 1], in_=cctx_d.rearrange("o (k p) -> p (o k)", p=128), allow_slow_non_contiguous=True), writes=[craw])
        s.op("act", I("activation", out=sc[:], in_=craw[:], func=AF.Silu), reads=[craw], writes=[sc])
        s.op("dve", I("tensor_copy", out=scb[:], in_=sc[:, :, 0:1].to_broadcast([128, 8, 128])), reads=[sc], writes=[scb])
        bada = sb(P0, "bada", [1, 6 * D])
        s.dma("sp", I("dma_start", out=bada[:], in_=bada_d[:, :]), writes=[bada])
        adacol = sb(P0, "adacol", [128, 48, 2])
        growraw = sb(P0, "growraw", [128, 2, D])
        wring = RR(ring(P0, "wada", 4, [128, 8, 512]))
        pcol = RR(ring(P0, "pcol", 2, [128, 2], psum=True))
        prow = RR(ring(P0, "prow", 2, [128, 512], psum=True))
        col_tiles = {0: 0, 1: 4, 2: 8, 3: 12, 6: 24, 7: 28, 8: 32, 9: 36}
        row_tiles = {4: (0, 0), 5: (0, 512), 10: (1, 0), 11: (1, 512)}
        for ct in range(12):
            W = wring.next()
            s.dma("sp", I("dma_start", out=W[:], in_=wada_d[:, ct * 512:(ct + 1) * 512].rearrange("(k p) n -> p k n", p=128)), writes=[W])
            if ct in col_tiles:
                for jj in range(4):
                    pc = pcol.next()
                    for k in range(8):
                        s.op("pe", I("matmul", pc[:, :], lhsT=W[:, k, jj * 128:(jj + 1) * 128], rhs=sc[:, k, :], start=(k == 0), stop=False), inc=False,
                             reads=[W, sc], writes=[pc])
                    c0 = ct * 512 + jj * 128
                    s.op("pe", I("matmul", pc[:, :], lhsT=bada[0:1, c0:c0 + 128], rhs=kc[0:1, 3, 0:2], start=False, stop=True), inc=True,
                         reads=[bada, kc], writes=[pc])
                    bi = col_tiles[ct] + jj
                    s.op("act", I("activation", out=adacol[:, bi, :], in_=pc[:, :], func=AF.Identity), reads=[pc], writes=[adacol])
            else:
                pr = prow.next()
                for k in range(8):
                    s.op("pe", I("matmul", pr[:, :], lhsT=scb[:, k, :], rhs=W[:, k, :], start=(k == 0), stop=False), inc=False,
                         reads=[W, scb], writes=[pr])
                s.op("pe", I("matmul", pr[:, :], lhsT=kc[0:1, 3, :], rhs=bada[0:1, ct * 512:(ct + 1) * 512], start=False, stop=True), inc=True,
                     reads=[bada, kc], writes=[pr])
                gi, off = row_tiles[ct]
                s.op("act", I("activation", out=growraw[:, gi, off:off + 512], in_=pr[:, :], func=AF.Identity), reads=[pr], writes=[growraw])
        s.op("dve", I("scalar_tensor_tensor", out=colA1[:], in0=adacol[:, 8:16, 0], scalar=1.0, in1=gpm_c[:], op0=ALU.add, op1=ALU.mult), reads=[adacol, gpm_c], writes=[colA1])
        s.op("dve", I("scalar_tensor_tensor", out=colcA1[:], in0=adacol[:, 8:16, 1], scalar=1.0, in1=gpm_c[:], op0=ALU.add, op1=ALU.mult), reads=[adacol, gpm_c], writes=[colcA1])
        s.op("dve", I("scalar_tensor_tensor", out=colA2[:], in0=adacol[:, 32:40, 0], scalar=1.0, in1=gpl_c[:], op0=ALU.add, op1=ALU.mult), reads=[adacol, gpl_c], writes=[colA2])
        s.op("dve", I("tensor_copy", out=colS1[:], in_=adacol[:, 0:8, 0]), reads=[adacol], writes=[colS1])
        s.op("dve", I("tensor_copy", out=colcS1[:], in_=adacol[:, 0:8, 1]), reads=[adacol], writes=[colcS1])
        s.op("dve", I("tensor_copy", out=colS2[:], in_=adacol[:, 24:32, 0]), reads=[adacol], writes=[colS2])
        s.op("dve", I("tensor_tensor", out=G1[:], in0=growraw[:, 0, :], in1=gq_row[:], op=ALU.mult), reads=[growraw, gq_row], writes=[G1])
        s.op("dve", I("tensor_tensor", out=G2[:], in0=growraw[:, 1, :], in1=gl_row[:], op=ALU.mult), reads=[growraw, gl_row], writes=[G2])
        if debug and "ada" in dbg_d:
            dump("ada", colA1[:], lambda d: d[:, 0:8], [colA1]); dump("ada", colS1[:], lambda d: d[:, 8:16], [colS1])
            dump("ada", colcA1[:], lambda d: d[:, 16:24], [colcA1]); dump("ada", colA2[:], lambda d: d[:, 24:32], [colA2])
            dump("ada", colS2[:], lambda d: d[:, 32:40], [colS2])
            dump("ada", G1[:, 0:64], lambda d: d[:, 40:104], [G1]); dump("ada", G2[:, 0:64], lambda d: d[:, 104:168], [G2])
        s.barrier()
        s.emit()
    if debug and debug.get("stop") == "p0":
        return

    def prenorm_load(W, src_rows_ap, src_buf):
        xt = W["xt"].next()
        s.dma("sp", I("dma_start", out=xt[:], in_=src_rows_ap), reads=[src_buf], writes=[xt])
        return xt

    def prenorm_front(W, src_rows_ap, src_buf, xt=None):
        if xt is None:
            xt = prenorm_load(W, src_rows_ap, src_buf)
        sq = W["sq"].next(); ss = W["ss"].next(); xn = W["xn"].next()
        s.op("act", I("activation", out=sq[:], in_=xt[:], func=AF.Square, accum_out=ss[:, 0:1]), reads=[xt], writes=[sq, ss])
        s.op("dve", I("tensor_scalar", out=ss[:, 1:2], in0=ss[:, 0:1], scalar1=1.0 / D, scalar2=EPS, op0=ALU.mult, op1=ALU.add), reads=[ss], writes=[ss])
        s.op("pool", I("tensor_tensor", out=ss[:, 2:3], in0=ss[:, 1:2], in1=halfneg[:], op=ALU.pow), reads=[ss, halfneg], writes=[ss])
        s.op("dve", I("tensor_scalar", out=xn[:], in0=xt[:], scalar1=ss[:, 2:3], scalar2=None, op0=ALU.mult), reads=[xt, ss], writes=[xn])
        return xt, xn

    def prenorm_back(W, xn, colA, colS, dst_fn, dst_buf, all_act=False, all_dve=False):
        pt = W["pt"].next()
        for k in range(8):
            s.op("pe", I("transpose", out=pt[:, k * 128:(k + 1) * 128], in_=xn[:, k * 128:(k + 1) * 128], identity=ident_b), inc=(k == 7), reads=[xn, kcb], writes=[pt])
        for k in range(8):
            if (k % 2 == 0 or all_act) and not all_dve:
                s.op("act", I("activation", out=dst_fn(k), in_=pt[:, k * 128:(k + 1) * 128], func=AF.Identity, scale=colA[:, k:k + 1], bias=colS[:, k:k + 1]),
                     reads=[pt, colA, colS], writes=[dst_buf])
            else:
                s.op("dve", I("tensor_scalar", out=dst_fn(k), in0=pt[:, k * 128:(k + 1) * 128], scalar1=colA[:, k:k + 1], scalar2=colS[:, k:k + 1], op0=ALU.mult, op1=ALU.add),
                     reads=[pt, colA, colS], writes=[dst_buf])

    def prenorm_T(W, src_rows_ap, src_buf, colA, colS, dst_fn, dst_buf):
        xt, xn = prenorm_front(W, src_rows_ap, src_buf)
        prenorm_back(W, xn, colA, colS, dst_fn, dst_buf)
        return xt

    def prenorm_work(stack, tag):
        return {
            "xt": RR(ring(stack, tag + "xt", 4, [128, D])),
            "sq": RR(ring(stack, tag + "sq", 1, [128, D], BF16)),
            "ss": RR(ring(stack, tag + "ss", 4, [128, 4])),
            "xn": RR(ring(stack, tag + "xn", 4, [128, D], BF16)),
            "pt": RR(ring(stack, tag + "pt", 2, [128, D], BF16, psum=True)),
        }

    def wload_cast(dst, dst_ap, src_ap):
        s.dma_sw(I("dma_start", out=dst_ap, in_=src_ap), writes=[dst])

    with contextlib.ExitStack() as PS:
        xsT = sb(PS, "xsT", [128, 8, L + 2], BF16)
        BT = sb(PS, "BT", [128, 2, L + 2], BF16)
        CT = sb(PS, "CT", [128, 2, L + 2], BF16)
        dts = sb(PS, "dts", [128, NCH, 32])
        cxsT = sb(PS, "cxsT", [128, 8, CTXL + 2], BF16)
        cBT = sb(PS, "cBT", [128, 2, CTXL + 2], BF16)
        cCT = sb(PS, "cCT", [128, 2, CTXL + 2], BF16)
        cdts = sb(PS, "cdts", [128, 2, 32])

        with contextlib.ExitStack() as P2:
            wx = sb(P2, "wx", [128, 8, 1536], BF16)
            wdt = sb(P2, "wdt", [128, 8, 32], BF16)
            wload_cast(wx, wx[:], win_d[:, 1024:2560].rearrange("(k p) n -> p k n", p=128))
            wload_cast(wdt, wdt[:], win_d[:, 2560:2592].rearrange("(k p) n -> p k n", p=128))
            PW = prenorm_work(P2, "p2")
            hx_ring = RR(ring(P2, "hxT", 2, [128, 8, 512], BF16))
            pmm = RR(ring(P2, "pmm", 3, [128, 512], psum=True))
            pdt = RR(ring(P2, "pdt", 1, [128, 32], psum=True))
            Pr = RR(ring(P2, "Ppre", 3, [128, 516], BF16))
            pcv = RR(ring(P2, "pcv", 2, [128, 512], psum=True))
            carry = sb(P2, "carry", [128, 12, 4], BF16)
            dgw = sb(P2, "dgw", [128, 60, 128], BF16)
            s.op("dve", I("tensor_tensor", out=dgw[:], in0=kcb[:, 0:1, :].to_broadcast([128, 60, 128]),
                          in1=convw[:].rearrange("p j k -> p (j k)").unsqueeze(2).to_broadcast([128, 60, 128]), op=ALU.mult), reads=[kcb, convw], writes=[dgw])
            dtt = RR(ring(P2, "dtt", 2, [128, 32]))

            for P_ in Pr.bufs:
                P_.head = Buf(P_.ap, P_.name + "_head")

            def conv_chunk(j, P, ncol, dst_ap, dst_buf):
                pc = pcv.next()
                for k in range(5):
                    s.op("pe", I("matmul", pc[:, 0:ncol], lhsT=dgw[:, j * 5 + k, :], rhs=P[:, k:k + ncol], start=(k == 0), stop=(k == 4)), inc=(k == 4), reads=[dgw, P, P.head], writes=[pc])
                s.op("act", I("activation", out=dst_ap, in_=pc[:, 0:ncol], func=AF.Silu, bias=convb[:, j:j + 1]), reads=[pc, convb], writes=[dst_buf])

            import os
            SUBCUT = int(os.environ.get("P2SUB", "9"))

            def inproj_seq(src_d, src_buf, ntok, colA, colS, xs_dst, B_dst, C_dst, dt_dst, nj):
                s.op("pool", I("memset", carry[:], 0.0), writes=[carry])
                tile_n = min(512, ntok)
                ntile = ntok // tile_n
                nsub = tile_n // 128

                def dest(j, c0, n):
                    if j < 8:
                        return xs_dst[:, j, c0:c0 + n], xs_dst
                    if j < 10:
                        return B_dst[:, j - 8, c0:c0 + n], B_dst
                    return C_dst[:, j - 10, c0:c0 + n], C_dst

                def tile_loads(i):
                    return [prenorm_load(PW, src_d[i * tile_n + sub * 128:i * tile_n + (sub + 1) * 128, :], src_buf) for sub in range(nsub)]

                nxt = tile_loads(0)
                for i in range(ntile):
                    hx = hx_ring.next()
                    xts = nxt
                    fr = [prenorm_front(PW, None, None, xt=xts[sub]) for sub in range(nsub)]
                    if i + 1 < ntile:
                        nxt = tile_loads(i + 1)
                    for sub in range(nsub):
                        prenorm_back(PW, fr[sub][1], colA, colS, lambda k, sub=sub, hx=hx: hx[:, k, sub * 128:(sub + 1) * 128], hx, all_dve=True)
                    for sub in range(nsub if SUBCUT >= 2 else 0):
                        cidx = i * nsub + sub
                        pd = pdt.next(); t1 = dtt.next()
                        for k in range(8):
                            s.op("pe", I("matmul", pd[:, :], lhsT=hx[:, k, sub * 128:(sub + 1) * 128], rhs=wdt[:, k, :], start=(k == 0), stop=(k == 7)), inc=(k == 7),
                                 reads=[hx, wdt], writes=[pd])
                        s.op("dve", I("tensor_tensor", out=t1[:], in0=pd[:, :], in1=dtb_row[:], op=ALU.add), reads=[pd, dtb_row], writes=[t1])
                        s.op("act", I("activation", out=t1[:], in_=t1[:], func=AF.Exp), reads=[t1], writes=[t1])
                        s.op("act", I("activation", out=dt_dst[:, cidx, :], in_=t1[:], func=AF.Ln, bias=1.0), reads=[t1], writes=[dt_dst])
                    pend = None
                    for j in range(nj if SUBCUT >= 3 else 0):
                        pm = pmm.next(); P = Pr.next()
                        for k in range(8):
                            s.op("pe", I("matmul", pm[:, 0:tile_n], lhsT=wx[:, k, j * 128:(j + 1) * 128], rhs=hx[:, k, 0:tile_n], start=(k == 0), stop=(k == 7)), inc=(k == 7),
                                 reads=[hx, wx], writes=[pm])
                        s.op("pool", I("tensor_copy", out=P[:, 0:4], in_=carry[:, j, :]), reads=[carry], writes=[P.head])
                        s.op("act", I("activation", out=P[:, 4:4 + tile_n], in_=pm[:, 0:tile_n], func=AF.Identity), reads=[pm], writes=[P])
                        s.op("pool", I("tensor_copy", out=carry[:, j, :], in_=P[:, tile_n:tile_n + 4]), reads=[P], writes=[carry])
                        if pend is not None:
                            conv_chunk(*pend)
                        dap, dbuf = dest(j, i * tile_n, tile_n)
                        pend = (j, P, tile_n, dap, dbuf)
                    if pend is not None:
                        conv_chunk(*pend)
                for j in range(nj if SUBCUT >= 5 else 0):
                    P = Pr.next()
                    s.op("pool", I("tensor_copy", out=P[:, 0:4], in_=carry[:, j, :]), reads=[carry], writes=[P.head])
                    s.op("pool", I("memset", P[:, 4:8], 0.0), writes=[P])
                    dap, dbuf = dest(j, ntok, 2)
                    conv_chunk(j, P, 2, dap, dbuf)

            ctx_buf = Buf(ctx_d, "ctx")
            import os
            cut = int(os.environ.get("P2CUT", "9"))
            if cut >= 1:
                inproj_seq(ctx_d, ctx_buf, CTXL, colcA1, colcS1, cxsT, cBT, cCT, cdts, 12)
            if cut >= 2:
                inproj_seq(x_d, dram_x, L, colA1, colS1, xsT, BT, CT, dts, 12)
            if debug and "xsT" in dbg_d:
                tmpf = sb(P2, "dbgtmp", [128, 512])
                for (nm, srcb, n3) in (("xsT", xsT, 8), ("BT", BT, 2), ("CT", CT, 2)):
                    for j in range(n3):
                        for q in range(1):
                            s.op("dve", I("tensor_copy", out=tmpf[:], in_=srcb[:, j, 2:514]), reads=[srcb], writes=[tmpf])
                            dump(nm, tmpf[:], lambda d, j=j: d[:, j, :], [tmpf])
                dump("dts", dts[:], lambda d: d[:, :, :], [dts])
                dump("cdts", cdts[:], lambda d: d[:, :, :], [cdts])
            s.barrier()
            s.emit()
        if debug and debug.get("stop") == "p2":
            return

        with contextlib.ExitStack() as P3:
            Sst = [sb(P3, "S_f", [128, D]), sb(P3, "S_b", [128, D])]
            Sbf = [sb(P3, "Sbf_f", [128, D], BF16), sb(P3, "Sbf_b", [128, D], BF16)]
            tri_b16 = {0: kcb[:, 1, :], 1: kcb[:, 2, :]}
            tri_f32 = {0: trif, 1: trib}
            Umat = sb(P3, "Umat", [128, 2, 128])
            Dg = sb(P3, "Dg", [128, 16, 128], BF16)
            s.op("dve", I("tensor_tensor", out=Dg[:], in0=kcb[:, 0:1, :].to_broadcast([128, 16, 128]), in1=dsk_row[:].unsqueeze(2).to_broadcast([128, 16, 128]), op=ALU.mult),
                 reads=[kcb, dsk_row], writes=[Dg])
            s.op("dve", I("tensor_scalar", out=Umat[:], in0=kc[:, 1:3, :], scalar1=-1.0, scalar2=1.0, op0=ALU.mult, op1=ALU.add), reads=[kc], writes=[Umat])
            R2 = lambda name, shape, dt=F32, n=2: RR(ring(P3, name, n, shape, dt))
            dtA_r = R2("dtA", [128, 16])
            sm_r = R2("smalls", [128, 5, 16])
            rhs_r = R2("rhs32", [128, 16, 128], F32, 1)
            L_r = R2("Lmat", [128, 16, 128], BF16, 1); M_r = R2("Mmat", [128, 16, 128], BF16)
            cbm_r = R2("cbm", [128, 2, 128], BF16)
            xs_r = R2("xs_sb", [128, 16, 64], BF16); xdt_r = R2("xdt", [128, 16, 64], BF16); xdw_r = R2("xdw", [128, 16, 64], BF16)
            Btok = sb(P3, "Btok", [128, NCH, 256], BF16)
            cBtok = sb(P3, "cBtok", [128, 2, 256], BF16)
            yt_r = R2("ytmp", [128, D], F32, 1); yo_r = R2("yout", [128, D]); yb_r = R2("ybld", [128, D], F32, 1)
            p_small = ps(P3, "p_small", [128, 512])
            p_T = ps(P3, "p_T", [128, 1024], BF16)
            p_seg = RR(ring(P3, "p_seg", 2, [128, 512], psum=True))
            p_yo = [ps(P3, "p_yo0", [128, 512]), ps(P3, "p_yo1", [128, 512])]
            p_st = [ps(P3, "p_st0", [128, 512]), ps(P3, "p_st1", [128, 512])]

            def make_btok(Bsrc, dst, nchunk):
                for c in range(nchunk):
                    c0 = 2 + c * 128
                    for g in range(2):
                        s.op("pe", I("transpose", out=p_T[:, g * 128:(g + 1) * 128], in_=Bsrc[:, g, c0:c0 + 128], identity=ident_b), reads=[Bsrc, kcb], writes=[p_T])
                    s.op("act", I("activation", out=dst[:, c, :], in_=p_T[:, 0:256], func=AF.Identity), reads=[p_T], writes=[dst])

            make_btok(cBT, cBtok, 2)
            make_btok(BT, Btok, NCH)

            def stageA(d, xsrc, Bsrc, Csrc, dtsrc, Btsrc, c, need_y):
                c0 = 2 + c * 128
                ctxd = {}
                dtA = dtA_r.next(); sm = sm_r.next()
                dtd = dtsrc[:, c, d * 16:(d + 1) * 16]
                s.op("dve", I("tensor_tensor", out=dtA[:], in0=dtd, in1=a_row[:, d * 16:(d + 1) * 16], op=ALU.mult), reads=[dtsrc, a_row], writes=[dtA])
                s.op("pe", I("matmul", p_small[:, 0:16], lhsT=tri_f32[d], rhs=dtA[:], start=True, stop=True), inc=True, reads=[kc, dtA], writes=[p_small])
                s.op("pe", I("matmul", p_small[:, 16:32], lhsT=ones, rhs=dtA[:], start=True, stop=True), inc=True, reads=[kc, dtA], writes=[p_small])
                s.op("act", I("activation", out=sm[:, 0, :], in_=p_small[:, 0:16], func=AF.Identity), reads=[p_small], writes=[sm])
                s.op("act", I("activation", out=sm[:, 2, :], in_=p_small[:, 0:16], func=AF.Exp), reads=[p_small], writes=[sm])
                s.op("act", I("activation", out=sm[:, 4, :], in_=p_small[:, 16:32], func=AF.Exp), reads=[p_small], writes=[sm])
                s.op("dve", I("tensor_tensor", out=sm[:, 1, :], in0=p_small[:, 16:32], in1=sm[:, 0, :], op=ALU.subtract), reads=[p_small, sm], writes=[sm])
                s.op("act", I("activation", out=sm[:, 3, :], in_=sm[:, 1, :], func=AF.Exp), reads=[sm], writes=[sm])
                s.op("dve", I("tensor_tensor", out=sm[:, 3, :], in0=sm[:, 3, :], in1=dtd, op=ALU.mult), reads=[sm, dtsrc], writes=[sm])
                for k in range(8):
                    s.op("pe", I("transpose", out=p_T[:, k * 128:(k + 1) * 128], in_=xsrc[:, k, c0:c0 + 128], identity=ident_b), inc=(k == 7), reads=[xsrc, kcb], writes=[p_T])
                xs_sb = xs_r.next(); xdw = xdw_r.next()
                s.op("act", I("activation", out=xs_sb[:].rearrange("p h q -> p (h q)"), in_=p_T[:, :], func=AF.Identity), reads=[p_T], writes=[xs_sb])
                s.op("pool", I("tensor_tensor", out=xdw[:], in0=xs_sb[:], in1=sm[:, 3, :].unsqueeze(2).to_broadcast([128, 16, 64]), op=ALU.mult), reads=[xs_sb, sm], writes=[xdw])
                ctxd.update(sm=sm, xs_sb=xs_sb, xdw=xdw, Btsrc=Btsrc, c=c, c0=c0)
                if need_y:
                    xdt = xdt_r.next(); cbm = cbm_r.next(); rhs = rhs_r.next(); Lm = L_r.next(); Mm = M_r.next()
                    if not hasattr(rhs, "halves"):
                        rhs.halves = [Buf(rhs.ap, rhs.name + "_lo"), Buf(rhs.ap, rhs.name + "_hi")]
                    s.op("dve", I("tensor_tensor", out=xdt[:], in0=xs_sb[:], in1=dtd.unsqueeze(2).to_broadcast([128, 16, 64]), op=ALU.mult), reads=[xs_sb, dtsrc], writes=[xdt])
                    s.op("dve", I("tensor_tensor", out=rhs[:, 0:8, :], in0=tri_f32[d].unsqueeze(1).to_broadcast([128, 8, 128]), in1=dtA[:, 0:8].unsqueeze(2).to_broadcast([128, 8, 128]), op=ALU.mult),
                         reads=[kc, dtA], writes=[rhs.halves[0]])
                    for h in range(8, 16):
                        s.op("act", I("activation", out=rhs[:, h, :], in_=tri_f32[d], func=AF.Identity, scale=dtA[:, h:h + 1]), reads=[kc, dtA], writes=[rhs.halves[1]])
                    for g in range(2):
                        s.op("pe", I("matmul", p_small[:, 256 + g * 128:256 + (g + 1) * 128], lhsT=Bsrc[:, g, c0:c0 + 128], rhs=Csrc[:, g, c0:c0 + 128], start=True, stop=True), inc=True,
                             reads=[Bsrc, Csrc], writes=[p_small])
                    s.op("dve", I("tensor_tensor", out=cbm[:], in0=p_small[:, 256:512].rearrange("p (g t) -> p g t", g=2),
                                  in1=tri_f32[d].unsqueeze(1).to_broadcast([128, 2, 128]), op=ALU.mult), reads=[p_small, kc], writes=[cbm])
                    ctxd.update(rhs=rhs, Lm=Lm, cbm=cbm)
                    ctxd.update(xdt=xdt, Mm=Mm)
                return ctxd

            def stageA2(d, A):
                rhs = A["rhs"]; Lm = A["Lm"]; Mm = A["Mm"]; cbm = A["cbm"]
                for q in range(4):
                    pq = p_seg.next()
                    s.op("pe", I("matmul", pq[:, :], lhsT=Umat[:, d, :], rhs=rhs[:, q * 4:(q + 1) * 4, :].rearrange("p h t -> p (h t)"), start=True, stop=True), inc=True,
                         reads=[Umat, rhs.halves[q // 2]], writes=[pq])
                    s.op("act", I("activation", out=Lm[:, q * 4:(q + 1) * 4, :].rearrange("p h t -> p (h t)"), in_=pq[:, :], func=AF.Exp), reads=[pq], writes=[Lm])
                    gq = q // 2
                    s.op("dve", I("tensor_tensor", out=Mm[:, q * 4:(q + 1) * 4, :], in0=Lm[:, q * 4:(q + 1) * 4, :],
                                  in1=cbm[:, gq:gq + 1, :].to_broadcast([128, 4, 128]), op=ALU.mult), reads=[Lm, cbm], writes=[Mm])

            def stageB(d, Csrc, c, A, need_y, final):
                c0 = A["c0"]; sm = A["sm"]; Btsrc = A["Btsrc"]
                S = Sst[d]; Sb = Sbf[d]
                if need_y:
                    for g in range(2):
                        s.op("pe", I("matmul", p_yo[g][:, :], lhsT=Csrc[:, g, c0:c0 + 128], rhs=Sb[:, g * 512:(g + 1) * 512], start=True, stop=True), inc=True, reads=[Csrc, Sb], writes=[p_yo[g]])
                for g in range(2):
                    s.op("pe", I("matmul", p_st[g][:, :], lhsT=Btsrc[:, c, g * 128:(g + 1) * 128], rhs=A["xdw"][:, g * 8:(g + 1) * 8, :].rearrange("p h q -> p (h q)"), start=True, stop=True), inc=True,
                         reads=[Btsrc, A["xdw"]], writes=[p_st[g]])
                for g in range(2):
                    s.op("pool", I("tensor_tensor", out=S[:, g * 512:(g + 1) * 512].rearrange("p (h q) -> p h q", h=8), in0=S[:, g * 512:(g + 1) * 512].rearrange("p (h q) -> p h q", h=8),
                                   in1=sm[:, 4, g * 8:(g + 1) * 8].unsqueeze(2).to_broadcast([128, 8, 64]), op=ALU.mult), reads=[S, sm], writes=[S])
                yt = yt_r.next() if need_y else None
                if need_y:
                    for g in range(2):
                        s.op("dve", I("tensor_tensor", out=yt[:, g * 512:(g + 1) * 512].rearrange("p (h q) -> p h q", h=8), in0=p_yo[g][:, :].rearrange("p (h q) -> p h q", h=8),
                                      in1=sm[:, 2, g * 8:(g + 1) * 8].unsqueeze(2).to_broadcast([128, 8, 64]), op=ALU.mult), reads=[p_yo[g], sm], writes=[yt])
                for g in range(2):
                    s.op("dve", I("tensor_tensor", out=S[:, g * 512:(g + 1) * 512], in0=p_st[g][:, :], in1=S[:, g * 512:(g + 1) * 512], op=ALU.add), reads=[p_st[g], S], writes=[S])
                s.op("act", I("activation", out=Sb[:], in_=S[:], func=AF.Identity), reads=[S], writes=[Sb])
                if need_y:
                    yo = yo_r.next()
                    for g in range(2):
                        for h in range(8):
                            hh = g * 8 + h
                            s.op("pe", I("matmul", p_yo[g][:, h * 64:(h + 1) * 64], lhsT=A["Mm"][:, hh, :], rhs=A["xdt"][:, hh, :], start=True, stop=(d != 0)), inc=(d != 0 and h == 7),
                                 reads=[A["Mm"], A["xdt"]], writes=[p_yo[g]])
                            if d == 0:
                                s.op("pe", I("matmul", p_yo[g][:, h * 64:(h + 1) * 64], lhsT=Dg[:, hh, :], rhs=A["xs_sb"][:, hh, :], start=False, stop=True), inc=(h == 7),
                                     reads=[Dg, A["xs_sb"]], writes=[p_yo[g]])
                        s.op("dve", I("tensor_tensor", out=yo[:, g * 512:(g + 1) * 512], in0=p_yo[g][:, :], in1=yt[:, g * 512:(g + 1) * 512], op=ALU.add), reads=[p_yo[g], yt], writes=[yo])
                    final(c, yo, A)

            def ssd_pass(d, xsrc, Bsrc, Csrc, dtsrc, Btsrc, nchunk, need_y, final):
                order = list(range(nchunk)) if d == 0 else list(range(nchunk - 1, -1, -1))
                prev = None
                for step in range(nchunk + 1):
                    cur = None
                    if step < nchunk:
                        cur = (order[step], stageA(d, xsrc, Bsrc, Csrc, dtsrc, Btsrc, order[step], need_y))
                    if prev is not None:
                        stageB(d, Csrc, prev[0], prev[1], need_y, final)
                    if cur is not None and need_y:
                        stageA2(d, cur[1])
                    prev = cur

            for d in range(2):
                s.op("pool", I("memset", Sst[d][:], 0.0), writes=[Sst[d]])
                s.op("pool", I("memset", Sbf[d][:], 0.0), writes=[Sbf[d]])
                ssd_pass(d, cxsT, cBT, cCT, cdts, cBtok, 2, False, None)
            if debug and "S0" in dbg_d:
                dump("S0", Sst[0][:], lambda dd: dd[:, 0, :], [Sst[0]]); dump("S0", Sst[1][:], lambda dd: dd[:, 1, :], [Sst[1]])

            def fin_b(c, yo, A):
                s.dma("sp", I("dma_start", out=ysc_d[c * 128:(c + 1) * 128, :], in_=yo[:]), reads=[yo], writes=[ysc[c]])

            ssd_pass(1, xsT, BT, CT, dts, Btok, NCH, True, fin_b)

            def fin_f(c, yo, A):
                yb = yb_r.next()
                s.dma("sp", I("dma_start", out=yb[:], in_=ysc_d[c * 128:(c + 1) * 128, :]), reads=[ysc[c]], writes=[yb])
                s.op("pool", I("tensor_tensor", out=yb[:], in0=yb[:], in1=yo[:], op=ALU.add), reads=[yb, yo], writes=[yb])
                s.dma("sp", I("dma_start", out=ysc_d[c * 128:(c + 1) * 128, :], in_=yb[:]), reads=[yb], writes=[ysc[c]])

            ssd_pass(0, xsT, BT, CT, dts, Btok, NCH, True, fin_f)
            s.barrier()
            s.emit()

    if debug and debug.get("stop") == "ssd":
        for c in range(NCH):
            s.dma("sp", I("dma_start", out=out_d[c * 128:(c + 1) * 128, :], in_=ysc_d[c * 128:(c + 1) * 128, :]), reads=[ysc[c]], writes=[outb[c]])
        s.barrier()
        s.emit()
        return

    with contextlib.ExitStack() as P4:
        uT = sb(P4, "pmT_all", [128, 8, L], BF16)
        uTh = [Buf(uT.ap, "pmT%d" % j) for j in range(8)]
        pscol = sb(P4, "pscol", [128, 8])
        s.dma("sp", I("dma_start", out=pscol[:], in_=pscale_d.rearrange("o (k p) -> p (o k)", p=128), allow_slow_non_contiguous=True), writes=[pscol])
        with contextlib.ExitStack() as P4ab:
            utok = sb(P4ab, "utok", [128, NCH, 1024], BF16)
            utokh = [Buf(utok.ap, "utok%d" % c) for c in range(NCH)]
            with contextlib.ExitStack() as P4a:
                wu = sb(P4a, "wu", [128, 8, 1024], BF16)
                wload_cast(wu, wu[:], win_d[:, 2592:3616].rearrange("(k p) n -> p k n", p=128))
                PW = prenorm_work(P4a, "p4a")
                hx_ring = RR(ring(P4a, "hx4", 2, [128, 8, 512], BF16))
                pu = RR(ring(P4a, "pu", 4, [128, 512], psum=True))

                def tile_loads4(i):
                    return [prenorm_load(PW, x_d[i * 512 + sub * 128:i * 512 + (sub + 1) * 128, :], dram_x) for sub in range(4)]

                nxt = tile_loads4(0)
                for i in range(8):
                    hx = hx_ring.next()
                    xts = nxt
                    fr = [prenorm_front(PW, None, None, xt=xts[sub]) for sub in range(4)]
                    if i + 1 < 8:
                        nxt = tile_loads4(i + 1)
                    for sub in range(4):
                        prenorm_back(PW, fr[sub][1], colA1, colS1, lambda k, sub=sub, hx=hx: hx[:, k, sub * 128:(sub + 1) * 128], hx, all_dve=True)
                    for sub in range(4):
                        c = i * 4 + sub
                        for hf in range(2):
                            pq = pu.next()
                            for k in range(8):
                                s.op("pe", I("matmul", pq[:, :], lhsT=hx[:, k, sub * 128:(sub + 1) * 128], rhs=wu[:, k, hf * 512:(hf + 1) * 512], start=(k == 0), stop=(k == 7)), inc=(k == 7), reads=[wu, hx], writes=[pq])
                            if hf == 0:
                                s.op("act", I("activation", out=utok[:, c, 0:512], in_=pq[:, :], func=AF.Identity), reads=[pq], writes=[utokh[c]])
                            else:
                                s.op("dve", I("tensor_copy", out=utok[:, c, 512:1024], in_=pq[:, :]), reads=[pq], writes=[utokh[c]])
                s.barrier()
                s.emit()
            with contextlib.ExitStack() as P4b:
                kinv = sb(P4b, "kinv_sb", [128, 4, 64])
                s.dma("sp", I("dma_start", out=kinv[:].rearrange("p w j -> p (w j)"), in_=kinv_d[0, :].partition_broadcast(128)), writes=[kinv])
                kpb = sb(P4b, "kpb", [128, NPM, 128], BF16)
                wload_cast(kpb, kpb[:], kpool_d[:, :, :])
                pw = sb(P4b, "pw", [128, 4, 2, 256], BF16)
                wload_cast(pw, pw[:], poolw_d.rearrange("g (ci p) n -> p g ci n", p=128))
                imap = sb(P4b, "imap", [128, 64, 64])
                dT = [sb(P4b, "dT0", [128, L], BF16), sb(P4b, "dT1", [128, L], BF16)]
                tmp_r = RR(ring(P4b, "ptmp", 2, [128, 512]))
                p_S = RR(ring(P4b, "p_S", 2, [128, 512], psum=True))
                p_U = RR(ring(P4b, "p_U", 2, [128, 512], BF16, psum=True))
                ppm = RR(ring(P4b, "ppm", 4, [128, 512], psum=True))
                for g in range(4):
                    s.op("dve", I("tensor_tensor", out=imap[:], in0=kinv[:, g, :].unsqueeze(2).to_broadcast([128, 64, 64]), in1=kinv[:, g, :].unsqueeze(1).to_broadcast([128, 64, 64]), op=ALU.mult),
                         reads=[kinv], writes=[imap])
                    deltas = sorted(d_ for (gi, d_) in _PIDX if gi == g)
                    for jj in range(2):
                        j = 2 * g + jj
                        for q in range(8):
                            pS = p_S.next(); pU = p_U.next(); tmp = tmp_r.next()
                            for cc in range(4):
                                cd = 4 * q + cc
                                valid = [d_ for d_ in deltas if 0 <= cd + d_ < NCH]
                                for n_, d_ in enumerate(valid):
                                    s.op("pe", I("matmul", pS[:, cc * 128:(cc + 1) * 128], lhsT=utok[:, cd + d_, j * 128:(j + 1) * 128], rhs=kpb[:, _PIDX[(g, d_)], :],
                                                 start=(n_ == 0), stop=(n_ == len(valid) - 1)), inc=(n_ == len(valid) - 1), reads=[utokh[cd + d_], kpb], writes=[pS])
                                s.op("pe", I("transpose", out=pU[:, cc * 128:(cc + 1) * 128], in_=utok[:, cd, j * 128:(j + 1) * 128], identity=ident_b), reads=[utokh[cd], kcb], writes=[pU])
                            s.op("dve", I("tensor_tensor", out=tmp[:], in0=pS[:, :], in1=imap[:, 8 * q:8 * q + 8, :].rearrange("p r q -> p (r q)"), op=ALU.mult), reads=[pS, imap], writes=[tmp])
                            s.op("dve", I("tensor_tensor", out=dT[jj][:, q * 512:(q + 1) * 512], in0=tmp[:], in1=pU[:, :], op=ALU.subtract), reads=[tmp, pU], writes=[dT[jj]])
                    for i in range(8):
                        pcs = [ppm.next(), ppm.next()]
                        for co in range(2):
                            for ci in range(2):
                                s.op("pe", I("matmul", pcs[co][:, :], lhsT=pw[:, g, ci, co * 128:(co + 1) * 128], rhs=dT[ci][:, i * 512:(i + 1) * 512], start=(ci == 0), stop=(ci == 1)), inc=(ci == 1),
                                     reads=[pw, dT[ci]], writes=[pcs[co]])
                        for co in range(2):
                            s.op("act", I("activation", out=uT[:, 2 * g + co, i * 512:(i + 1) * 512], in_=pcs[co][:, :], func=AF.Identity, scale=pscol[:, 2 * g + co:2 * g + co + 1]),
                                 reads=[pcs[co], pscol], writes=[uTh[2 * g + co]])
                s.barrier()
                s.emit()
        with contextlib.ExitStack() as P4c:
            wz = sb(P4c, "wz", [128, 8, 1024], BF16)
            wo = sb(P4c, "wo", [128, 16, 1024], BF16)
            sncol = sb(P4c, "sncol", [128, 8])
            s.dma("sp", I("dma_start", out=sncol[:], in_=snw_d.rearrange("o (k p) -> p (o k)", p=128), allow_slow_non_contiguous=True), writes=[sncol])
            wload_cast(wz, wz[:], win_d[:, 0:1024].rearrange("(k p) n -> p k n", p=128))
            wload_cast(wo, wo[:], wout_d.rearrange("(k p) n -> p k n", p=128))
            for k in range(8):
                s.op("dve" if k % 2 else "pool", I("tensor_scalar", out=wo[:, k, :], in0=wo[:, k, :], scalar1=sncol[:, k:k + 1], scalar2=None, op0=ALU.mult), reads=[wo, sncol], writes=[wo])
            PW = {
                "xt": RR(ring(P4c, "p4cxt", 7, [128, D])),
                "sq": RR(ring(P4c, "p4csq", 1, [128, D], BF16)),
                "ss": RR(ring(P4c, "p4css", 3, [128, 4])),
                "xn": RR(ring(P4c, "p4cxn", 2, [128, D], BF16)),
                "pt": RR(ring(P4c, "p4cpt", 1, [128, D], BF16, psum=True)),
            }
            hxc_r = RR(ring(P4c, "hxc", 3, [128, 8, 128], BF16))
            sz_r = RR(ring(P4c, "sz", 1, [128, D])); yt_r = RR(ring(P4c, "yt4", 4, [128, D]))
            ygn_r = RR(ring(P4c, "ygn", 2, [128, D], BF16)); ygT_r = RR(ring(P4c, "ygT", 2, [128, 8, 128], BF16))
            st_r = RR(ring(P4c, "st4", 3, [128, 16])); x1_r = RR(ring(P4c, "x1", 3, [128, D]))
            sq4 = sb(P4c, "sq4", [128, 512], BF16)
            p_z = RR(ring(P4c, "p_z", 2, [128, 512], psum=True))
            p_T4 = ps(P4c, "p_T4", [128, D], BF16)
            p_mix = RR(ring(P4c, "p_mix", 4, [128, 512], psum=True))
            ST = {}

            def FL(c):
                xt = prenorm_load(PW, x_d[c * 128:(c + 1) * 128, :], dram_x)
                yt = yt_r.next()
                s.dma("sp", I("dma_start", out=yt[:], in_=ysc_d[c * 128:(c + 1) * 128, :]), reads=[ysc[c]], writes=[yt])
                ST[c] = dict(xt=xt, yt=yt)

            def F(c):
                xt, xn = prenorm_front(PW, None, None, xt=ST[c]["xt"])
                ST[c].update(xn=xn, st=st_r.next())

            def PT(c):
                hxc = hxc_r.next(); ST[c]["hxc"] = hxc
                prenorm_back(PW, ST[c]["xn"], colA1, colS1, lambda k, hxc=hxc: hxc[:, k, :], hxc, all_act=True)

            def YT(c):
                ygn = ST[c]["ygn"]; ygT = ygT_r.next(); ST[c]["ygT"] = ygT
                for k in range(8):
                    s.op("pe", I("transpose", out=p_T4[:, k * 128:(k + 1) * 128], in_=ygn[:, k * 128:(k + 1) * 128], identity=ident_b), inc=(k == 7), reads=[ygn, kcb], writes=[p_T4])
                s.op("act", I("activation", out=ygT[:].rearrange("p k t -> p (k t)"), in_=p_T4[:, :], func=AF.Identity), reads=[p_T4], writes=[ygT])

            def Z(c):
                hxc = ST[c]["hxc"]; yt = ST[c]["yt"]; st = ST[c]["st"]
                sz = sz_r.next(); ygn = ygn_r.next(); ST[c]["ygn"] = ygn
                for hf in range(2):
                    pz = p_z.next()
                    for k in range(8):
                        s.op("pe", I("matmul", pz[:, :], lhsT=hxc[:, k, :], rhs=wz[:, k, hf * 512:(hf + 1) * 512], start=(k == 0), stop=(k == 7)), inc=(k == 7), reads=[hxc, wz], writes=[pz])
                    s.op("act", I("activation", out=sz[:, hf * 512:(hf + 1) * 512], in_=pz[:, :], func=AF.Silu), reads=[pz], writes=[sz])
                s.op("dve", I("tensor_tensor", out=yt[:], in0=yt[:], in1=sz[:], op=ALU.mult), reads=[yt, sz], writes=[yt])
                for gg in range(2):
                    s.op("act", I("activation", out=sq4[:], in_=yt[:, gg * 512:(gg + 1) * 512], func=AF.Square, accum_out=st[:, gg:gg + 1]), reads=[yt], writes=[sq4, st])
                s.op("dve", I("tensor_scalar", out=st[:, 2:4], in0=st[:, 0:2], scalar1=1.0 / 512, scalar2=EPS, op0=ALU.mult, op1=ALU.add), reads=[st], writes=[st])
                s.op("pool", I("tensor_tensor", out=st[:, 4:6], in0=st[:, 2:4], in1=halfneg[:, 0:1].to_broadcast([128, 2]), op=ALU.pow), reads=[st, halfneg], writes=[st])
                for gg in range(2):
                    s.op("dve", I("tensor_scalar", out=ygn[:, gg * 512:(gg + 1) * 512], in0=yt[:, gg * 512:(gg + 1) * 512], scalar1=st[:, 4 + gg:5 + gg], scalar2=None, op0=ALU.mult),
                         reads=[yt, st], writes=[ygn])

            def OP(c):
                ygT = ST[c]["ygT"]; st = ST[c]["st"]
                pms = [p_mix.next(), p_mix.next()]; ST[c]["pms"] = pms
                for hf in range(2):
                    for k in range(16):
                        lhs = ygT[:, k, :] if k < 8 else uT[:, k - 8, c * 128:(c + 1) * 128]
                        rd = [ygT, wo] if k < 8 else [uTh[k - 8], wo]
                        s.op("pe", I("matmul", pms[hf][:, :], lhsT=lhs, rhs=wo[:, k, hf * 512:(hf + 1) * 512], start=(k == 0), stop=(k == 15)), inc=(k == 15), reads=rd, writes=[pms[hf]])
                    s.op("act", I("activation", out=sq4[:], in_=pms[hf][:, :], func=AF.Square, accum_out=st[:, 6 + hf:7 + hf]), reads=[pms[hf]], writes=[sq4, st])
                s.op("dve", I("tensor_tensor", out=st[:, 8:9], in0=st[:, 6:7], in1=st[:, 7:8], op=ALU.add), reads=[st], writes=[st])
                s.op("dve", I("tensor_scalar", out=st[:, 9:10], in0=st[:, 8:9], scalar1=1.0 / D, scalar2=EPS, op0=ALU.mult, op1=ALU.add), reads=[st], writes=[st])
                s.op("pool", I("tensor_tensor", out=st[:, 10:11], in0=st[:, 9:10], in1=halfneg[:], op=ALU.pow), reads=[st, halfneg], writes=[st])

            def FIN(c):
                pms = ST[c]["pms"]; st = ST[c]["st"]; xt = ST[c]["xt"]; x1 = x1_r.next()
                for hf in range(2):
                    s.op("dve", I("scalar_tensor_tensor", out=x1[:, hf * 512:(hf + 1) * 512], in0=pms[hf][:, :], scalar=st[:, 10:11], in1=G1[:, hf * 512:(hf + 1) * 512], op0=ALU.mult, op1=ALU.mult),
                         reads=[pms[hf], st, G1], writes=[x1])
                s.op("dve", I("tensor_tensor", out=x1[:], in0=x1[:], in1=xt[:], op=ALU.add), reads=[x1, xt], writes=[x1])
                PEND.append((c, x1))
                del ST[c]

            PEND = []

            def flush_stores():
                while PEND:
                    c, x1 = PEND.pop(0)
                    s.dma("sp", I("dma_start", out=out_d[c * 128:(c + 1) * 128, :], in_=x1[:]), reads=[x1], writes=[outb[c]])

            FL(0); FL(1); FL(2); F(0); F(1); PT(0)
            for t in range(NCH + 2):
                if t + 3 < NCH:
                    FL(t + 3)
                flush_stores()
                if t + 2 < NCH:
                    F(t + 2)
                if t + 1 < NCH:
                    PT(t + 1)
                if 0 <= t - 1 < NCH:
                    YT(t - 1)
                if t < NCH:
                    Z(t)
                if 0 <= t - 1 < NCH:
                    OP(t - 1)
                if 0 <= t - 2 < NCH:
                    FIN(t - 2)
            flush_stores()
            s.barrier()
            s.emit()
    if debug and debug.get("stop") == "p4":
        return

    with contextlib.ExitStack() as P5:
        w1 = sb(P5, "w1", [128, 8, DFF], BF16)
        w2 = sb(P5, "w2", [128, 32, D], BF16)
        for hh in range(2):
            wload_cast(w1, w1[:, :, hh * 2048:(hh + 1) * 2048], w1_d[:, hh * 2048:(hh + 1) * 2048].rearrange("(k p) n -> p k n", p=128))
        wload_cast(w2, w2[:], w2_d.rearrange("(f p) n -> p f n", p=128))
        PW = {
            "xt": RR(ring(P5, "p5xt", 4, [128, D])),
            "sq": RR(ring(P5, "p5sq", 1, [128, D], BF16)),
            "ss": RR(ring(P5, "p5ss", 4, [128, 4])),
            "xn": RR(ring(P5, "p5xn", 2, [128, D], BF16)),
            "pt": RR(ring(P5, "p5pt", 2, [128, D], BF16, psum=True)),
        }
        hm_r = RR(ring(P5, "hm", 2, [128, 8, 256], BF16))
        hT = sb(P5, "hT", [128, 32, 256], BF16)
        r_r = RR(ring(P5, "relu", 3, [128, 256], BF16))
        o_r = RR(ring(P5, "o5", 4, [128, D]))
        st_r = RR(ring(P5, "st5", 2, [128, 8]))
        sq5 = sb(P5, "sq5", [128, 512], BF16)
        p_h = RR(ring(P5, "p_h", 3, [128, 256], psum=True))
        p_o = RR(ring(P5, "p_o", 3, [128, 512], psum=True))
        NT = L // 256
        MS = {}

        def front5(i):
            fr = []
            for sub in range(2):
                c = i * 2 + sub
                fr.append(prenorm_front(PW, out_d[c * 128:(c + 1) * 128, :], outb[c]))
            MS[i] = dict(fr=fr)

        def back5(i):
            hm = hm_r.next(); MS[i]["hm"] = hm
            for sub in range(2):
                prenorm_back(PW, MS[i]["fr"][sub][1], colA2, colS2, lambda k, sub=sub, hm=hm: hm[:, k, sub * 128:(sub + 1) * 128], hm, all_dve=True)

        def mlp1(i):
            hm = MS[i]["hm"]
            for f in range(32):
                if f == 8 and i + 1 < NT:
                    front5(i + 1)
                ph = p_h.next(); rr = r_r.next()
                for k in range(8):
                    s.op("pe", I("matmul", ph[:, :], lhsT=w1[:, k, f * 128:(f + 1) * 128], rhs=hm[:, k, :], start=(k == 0), stop=(k == 7)), inc=(k == 7), reads=[w1, hm], writes=[ph])
                s.op("act", I("activation", out=rr[:], in_=ph[:, :], func=AF.Relu), reads=[ph], writes=[rr])
                s.op("pool", I("tensor_tensor", out=hT[:, f, :], in0=rr[:], in1=rr[:], op=ALU.mult), reads=[rr], writes=[hT])

        def mlp2(i):
            for sub in range(2):
                c = i * 2 + sub
                o = o_r.next(); st = st_r.next(); xt = MS[i]["fr"][sub][0]
                pos = [p_o.next(), p_o.next()]
                for hf in range(2):
                    for f in range(32):
                        s.op("pe", I("matmul", pos[hf][:, :], lhsT=hT[:, f, sub * 128:(sub + 1) * 128], rhs=w2[:, f, hf * 512:(hf + 1) * 512], start=(f == 0), stop=(f == 31)), inc=(f == 31), reads=[hT, w2], writes=[pos[hf]])
                    s.op("act", I("activation", out=sq5[:], in_=pos[hf][:, :], func=AF.Square, accum_out=st[:, hf:hf + 1]), reads=[pos[hf]], writes=[sq5, st])
                s.op("dve", I("tensor_tensor", out=st[:, 2:3], in0=st[:, 0:1], in1=st[:, 1:2], op=ALU.add), reads=[st], writes=[st])
                s.op("dve", I("tensor_scalar", out=st[:, 3:4], in0=st[:, 2:3], scalar1=1.0 / D, scalar2=EPS, op0=ALU.mult, op1=ALU.add), reads=[st], writes=[st])
                s.op("pool", I("tensor_tensor", out=st[:, 4:5], in0=st[:, 3:4], in1=halfneg[:], op=ALU.pow), reads=[st, halfneg], writes=[st])
                for hf in range(2):
                    s.op("dve", I("scalar_tensor_tensor", out=o[:, hf * 512:(hf + 1) * 512], in0=pos[hf][:, :], scalar=st[:, 4:5], in1=G2[:, hf * 512:(hf + 1) * 512], op0=ALU.mult, op1=ALU.mult),
                         reads=[pos[hf], st, G2], writes=[o])
                s.op("pool", I("tensor_tensor", out=o[:], in0=o[:], in1=xt[:], op=ALU.add), reads=[o, xt], writes=[o])
                PEND5.append((c, o))
            del MS[i]

        PEND5 = []

        def flush5():
            while PEND5:
                c, o = PEND5.pop(0)
                s.dma("sp", I("dma_start", out=out_d[c * 128:(c + 1) * 128, :], in_=o[:]), reads=[o], writes=[outb[c]])

        front5(0); back5(0)
        for i in range(NT):
            flush5()
            mlp1(i)
            if i + 1 < NT:
                back5(i + 1)
            mlp2(i)
        flush5()
        s.barrier()
        s.emit()


def pool_mats():
    mats = []; index = {}
    a = np.repeat(np.arange(2), 64); j = np.tile(np.arange(64), 2)
    for gi, w in enumerate((2, 4, 8, 16)):
        h = w // 2
        for delta in range(-5, 6):
            rd = 2 * delta + a[:, None] - a[None, :]
            cd = j[:, None] - j[None, :]
            m = ((rd >= -h) & (rd <= h - 1) & (cd >= -h) & (cd <= h - 1)).astype(np.float32)
            if m.any():
                index[(gi, delta)] = len(mats); mats.append(m)
    return np.stack(mats, axis=1), index


_PM, _PIDX = pool_mats()
NPM = _PM.shape[1]


def make_consts():
    k = np.arange(128)
    kc = np.zeros((128, 5, 128), np.float32)
    kc[:, 0, :] = np.eye(128, dtype=np.float32)
    kc[:, 1, :] = (k[:, None] <= k[None, :]).astype(np.float32)
    kc[:, 2, :] = (k[:, None] >= k[None, :]).astype(np.float32)
    kc[:, 3, :] = 1.0
    inv = np.zeros((4, 64), np.float32)
    t = np.arange(64)
    for gi, w in enumerate((2, 4, 8, 16)):
        lo = np.clip(t - w // 2, 0, 64); hi = np.clip(t + w // 2, 0, 64)
        inv[gi] = 1.0 / (hi - lo)
    return kc, inv.reshape(1, 256)


def core_inputs(inputs, b):
    f = lambda a: np.ascontiguousarray(np.asarray(a, dtype=np.float32))
    kc, kinv = make_consts()
    m = {
        "x": f(inputs["x"][b]), "ctx": f(inputs["ctx"][b]), "c": f(inputs["c"][b:b + 1]),
        "c_ctx": f(np.asarray(inputs["c_ctx"]).reshape(1, D)),
        "w_ada": f(inputs["w_ada"][0]), "b_ada": f(inputs["b_ada"][0:1]),
        "pre_mix_g": f(inputs["pre_mix_g"][0:1]), "post_mix_g": f(inputs["post_mix_g"][0:1]),
        "pre_mlp_g": f(inputs["pre_mlp_g"][0:1]), "post_mlp_g": f(inputs["post_mlp_g"][0:1]),
        "w_in": f(inputs["w_in"][0]), "conv_w": f(inputs["conv_w"][0]), "conv_b": f(inputs["conv_b"][0:1]),
        "dt_bias": f(np.asarray(inputs["dt_bias"][0]).reshape(1, 32)), "a_log": f(np.asarray(inputs["a_log"][0]).reshape(1, 32)),
        "d_skip": f(inputs["d_skip"][0:1]), "ssm_norm_w": f(inputs["ssm_norm_w"][0:1]),
        "pool_w": f(inputs["pool_w"][0]), "pool_scale": f(inputs["pool_scale"][0:1]),
        "w_out": f(inputs["w_out"][0]), "w_mlp1": f(inputs["w_mlp1"][0]), "w_mlp2": f(inputs["w_mlp2"][0]),
        "kconst": kc, "kinv": kinv, "kpool": np.ascontiguousarray(_PM),
    }
    return m


def kernel(**inputs):
    nc = build_program()
    nb = inputs["x"].shape[0]
    in_maps = [core_inputs(inputs, b) for b in range(nb)]
    res = run_bass_kernel_spmd(nc, in_maps, core_ids=list(range(nb)))
    out = np.stack([np.asarray(r["out"]) for r in res.results], axis=0)
    return out.astype(np.float32)
```
